# Optimizing a Trainium2 kernel written in Bass

```python
import math
import jax, jax.numpy as jnp
from jax import lax
import numpy as np

D_MODEL = 1024
BATCH = 32
SEQ = 2048
DEPTH = 4

CTX_LEN = 256
GRID_W = 64
EPS = 1e-6
ROPE_BASE = 10000.0
Q_BLOCK = 128

MLA_HEADS = 8
MLA_NOPE = 64
MLA_ROPE = 32
MLA_V = 64
MLA_Q_RANK = 256
MLA_KV_RANK = 128
MLA_OUT = MLA_HEADS * MLA_V
MLA_SCALE = (MLA_NOPE + MLA_ROPE) ** -0.5

HY_WIDTH = 256
HY_SHORT = 3
HY_EMB = 33
HY_BANDS = (HY_EMB - 1) // 2
HY_HIDDEN = 64
HY_TARGET = 1e-2
HY_FAST = 0.3
HY_SLOW = 1.5
HY_SHIFT = 0.05

DF_HEADS = 4
DF_DIM = 32
DF_V = 2 * DF_DIM
DF_OUT = DF_HEADS * DF_V
DF_SCALE = DF_DIM ** -0.5

N_BRANCH = 3
D_FF = 4 * D_MODEL

COLS = (MLA_Q_RANK, MLA_KV_RANK, MLA_ROPE, 3 * HY_WIDTH,
        2 * DF_HEADS * DF_DIM, 2 * DF_HEADS * DF_DIM, DF_OUT, N_BRANCH * D_MODEL)
D_IN = sum(COLS)

kernel_name = "hybrid_mla_hyena_diffattn_dit"

f32 = jnp.float32


def rmsnorm(x, g):
    xf = x.astype(f32)
    y = xf * lax.rsqrt(jnp.mean(xf * xf, axis=-1, keepdims=True) + EPS)
    return (y * g.astype(f32)).astype(x.dtype)


def modulate(h, shift, scale):
    return h * (1 + scale) + shift


def split_cols(p):
    parts, off = [], 0
    for n in COLS:
        parts.append(p[..., off:off + n])
        off += n
    return parts


def to_heads(t, n_heads):
    B, L, _ = t.shape
    return t.reshape(B, L, n_heads, -1).transpose(0, 2, 1, 3)


def from_heads(t):
    B, H, L, d = t.shape
    return t.transpose(0, 2, 1, 3).reshape(B, L, H * d)


def axial_cos_sin(n_tokens, d_rot):
    rows = n_tokens // GRID_W
    row = jnp.repeat(jnp.arange(rows), GRID_W).astype(f32)
    col = jnp.tile(jnp.arange(GRID_W), rows).astype(f32)
    nf = d_rot // 4
    inv = ROPE_BASE ** (-jnp.arange(nf, dtype=f32) / nf)
    ang = jnp.concatenate([row[:, None] * inv, col[:, None] * inv], axis=-1)
    return jnp.cos(ang), jnp.sin(ang)


def apply_rope(x, cos, sin):
    half = x.shape[-1] // 2
    x1 = x[..., :half].astype(f32)
    x2 = x[..., half:].astype(f32)
    return jnp.concatenate([x1 * cos - x2 * sin, x2 * cos + x1 * sin], axis=-1).astype(x.dtype)


def attend(q, k, v, scale):
    B, H, Lq, dk = q.shape
    nb = Lq // Q_BLOCK
    qb = jnp.moveaxis(q.reshape(B, H, nb, Q_BLOCK, dk), 2, 0)

    def block(qi):
        s = jnp.einsum('bhqd,bhkd->bhqk', qi, k).astype(f32) * scale
        p = jax.nn.softmax(s, axis=-1).astype(v.dtype)
        return jnp.einsum('bhqk,bhkv->bhqv', p, v)

    o = lax.map(block, qb)
    return jnp.moveaxis(o, 0, 2).reshape(B, H, Lq, v.shape[-1])


def mla_qkv(pq, pkv, pkr, lp, rope):
    q = to_heads(rmsnorm(pq, lp['mla_q_norm_g']) @ lp['mla_w_uq'], MLA_HEADS)
    kv = to_heads(rmsnorm(pkv, lp['mla_kv_norm_g']) @ lp['mla_w_ukv'], MLA_HEADS)
    q_nope, q_rope = q[..., :MLA_NOPE], q[..., MLA_NOPE:]
    k_nope, v = kv[..., :MLA_NOPE], kv[..., MLA_NOPE:]
    k_rope = pkr[:, None]
    if rope is not None:
        q_rope = apply_rope(q_rope, *rope)
        k_rope = apply_rope(k_rope, *rope)
    k_rope = jnp.broadcast_to(k_rope, k_nope.shape[:-1] + (MLA_ROPE,))
    q = jnp.concatenate([q_nope, q_rope], axis=-1)
    k = jnp.concatenate([k_nope, k_rope], axis=-1)
    return q, k, v


def short_conv(u, w, b):
    C = u.shape[-1]
    y = lax.conv_general_dilated(u, w[:, None, :].astype(u.dtype), window_strides=(1,),
                                 padding=[(HY_SHORT // 2, HY_SHORT // 2)],
                                 dimension_numbers=('NWC', 'WIO', 'NWC'),
                                 feature_group_count=C)
    return y + b


def hyena_filters(L, lp):
    t = jnp.linspace(0.0, 1.0, L, dtype=f32)[:, None]
    w = (2.0 * math.pi / L) * jnp.arange(L, dtype=f32)[:, None]
    bands = jnp.linspace(1e-4, HY_BANDS - 1, HY_BANDS, dtype=f32)[None]
    z = jnp.concatenate([t, jnp.cos(bands * w), -jnp.sin(bands * w)], axis=-1)
    freq = lp['hy_freq'].astype(f32)
    a = jnp.sin(freq * (z @ lp['hy_w1'].astype(f32) + lp['hy_b1'].astype(f32)))
    a = jnp.sin(freq * (a @ lp['hy_w2'].astype(f32) + lp['hy_b2'].astype(f32)))
    h = a @ lp['hy_w3'].astype(f32) + lp['hy_b3'].astype(f32)
    deltas = jnp.linspace(math.log(HY_TARGET) / HY_SLOW, math.log(HY_TARGET) / HY_FAST,
                          HY_WIDTH, dtype=f32)
    window = jnp.exp(-t * jnp.abs(deltas)) + HY_SHIFT
    h = h.reshape(L, 2, HY_WIDTH) * window[:, None, :]
    return h[:, 0], h[:, 1]


def bidir_longconv(u, h_f, h_b, skip):
    L = u.shape[1]
    f = jnp.concatenate([h_f, jnp.zeros((1, h_f.shape[1]), f32), h_b[:0:-1]], axis=0)
    uf = u.astype(f32)
    y = jnp.fft.irfft(jnp.fft.rfft(uf, n=2 * L, axis=1) * jnp.fft.rfft(f, axis=0)[None],
                      n=2 * L, axis=1)[:, :L]
    return (y + uf * skip.astype(f32)).astype(u.dtype)


def hyena_branch(p, lp):
    L = p.shape[1]
    uc = short_conv(p, lp['hy_conv_w'], lp['hy_conv_b'])
    x0, x1, v = jnp.split(uc, 3, axis=-1)
    h_f, h_b = hyena_filters(L, lp)
    return x0 * bidir_longconv(v * x1, h_f, h_b, lp['hy_skip'])


def diff_qkv(pq, pk, pv, rope):
    q = to_heads(pq, 2 * DF_HEADS)
    k = to_heads(pk, 2 * DF_HEADS)
    if rope is not None:
        q = apply_rope(q, *rope)
        k = apply_rope(k, *rope)
    B, _, L, _ = q.shape
    q = q.reshape(B, DF_HEADS, 2, L, DF_DIM)
    k = k.reshape(B, DF_HEADS, 2, L, DF_DIM)
    return q, k, to_heads(pv, DF_HEADS)


def diff_lambda(lp, lam_init):
    l1 = jnp.exp(jnp.sum(lp['df_lq1'].astype(f32) * lp['df_lk1'].astype(f32)))
    l2 = jnp.exp(jnp.sum(lp['df_lq2'].astype(f32) * lp['df_lk2'].astype(f32)))
    return l1 - l2 + lam_init


def diff_combine(o1, o2, lam, lam_init, g):
    o = o1 - lam.astype(o1.dtype) * o2
    return from_heads(rmsnorm(o, g) * (1.0 - lam_init))


def merge_branches(ya, yb, yc, pg, lp):
    g = jax.nn.sigmoid(pg.astype(f32)).astype(ya.dtype)
    g_a, g_b, g_c = jnp.split(g, N_BRANCH, axis=-1)
    m = g_a * (ya @ lp['w_br_a']) + g_b * (yb @ lp['w_br_b']) + g_c * (yc @ lp['w_br_c'])
    return m @ lp['w_out']


def sqrelu_mlp(h, w1, w2):
    return jnp.square(jax.nn.relu(h @ w1)) @ w2


def token_mixers(h, hc, lp, lam_init, rope_mla, rope_df, with_ctx):
    pq, pkv, pkr, phy, pdq, pdk, pdv, pg = split_cols(h @ lp['w_in'])
    cq, ckv, ckr, chy, cdq, cdk, cdv, cg = split_cols(hc @ lp['w_in'])

    qa, ka, va = mla_qkv(pq, pkv, pkr, lp, rope_mla)
    qa_c, ka_c, va_c = mla_qkv(cq, ckv, ckr, lp, None)
    ya = from_heads(attend(qa, jnp.concatenate([ka_c, ka], axis=2),
                           jnp.concatenate([va_c, va], axis=2), MLA_SCALE))

    yb = hyena_branch(phy, lp)

    lam = diff_lambda(lp, lam_init)
    qd, kd, vd = diff_qkv(pdq, pdk, pdv, rope_df)
    qd_c, kd_c, vd_c = diff_qkv(cdq, cdk, cdv, None)
    v_all = jnp.concatenate([vd_c, vd], axis=2)
    o1 = attend(qd[:, :, 0], jnp.concatenate([kd_c[:, :, 0], kd[:, :, 0]], axis=2), v_all, DF_SCALE)
    o2 = attend(qd[:, :, 1], jnp.concatenate([kd_c[:, :, 1], kd[:, :, 1]], axis=2), v_all, DF_SCALE)
    yc = diff_combine(o1, o2, lam, lam_init, lp['df_subln_g'])

    y = merge_branches(ya, yb, yc, pg, lp)
    if not with_ctx:
        return y, None

    ya_c = from_heads(attend(qa_c, ka_c, va_c, MLA_SCALE))
    yb_c = hyena_branch(chy, lp)
    o1c = attend(qd_c[:, :, 0], kd_c[:, :, 0], vd_c, DF_SCALE)
    o2c = attend(qd_c[:, :, 1], kd_c[:, :, 1], vd_c, DF_SCALE)
    yc_c = diff_combine(o1c, o2c, lam, lam_init, lp['df_subln_g'])
    y_c = merge_branches(ya_c, yb_c, yc_c, cg, lp)
    return y, y_c


def setup_inputs(seed: int = 0) -> dict:
    key = jax.random.key(seed)
    it = iter(jax.random.split(key, 40))

    def nrm(shape, scale):
        return jax.random.normal(next(it), shape, f32) * scale

    def gain(shape):
        return 1.0 + nrm(shape, 0.05)

    L = DEPTH
    return {
        'x': nrm((BATCH, SEQ, D_MODEL), 1.0),
        'c': nrm((BATCH, D_MODEL), 1.0),
        'ctx': nrm((BATCH, CTX_LEN, D_MODEL), 1.0),
        'c_ctx': nrm((D_MODEL,), 1.0),
        'norm_mix_g': gain((L, D_MODEL)),
        'norm_ffn_g': gain((L, D_MODEL)),
        'w_mod': nrm((L, D_MODEL, 6 * D_MODEL), 0.02),
        'b_mod': nrm((L, 6 * D_MODEL), 0.01),
        'w_in': nrm((L, D_MODEL, D_IN), D_MODEL ** -0.5),
        'mla_q_norm_g': gain((L, MLA_Q_RANK)),
        'mla_w_uq': nrm((L, MLA_Q_RANK, MLA_HEADS * (MLA_NOPE + MLA_ROPE)), MLA_Q_RANK ** -0.5),
        'mla_kv_norm_g': gain((L, MLA_KV_RANK)),
        'mla_w_ukv': nrm((L, MLA_KV_RANK, MLA_HEADS * (MLA_NOPE + MLA_V)), MLA_KV_RANK ** -0.5),
        'hy_conv_w': nrm((L, HY_SHORT, 3 * HY_WIDTH), 0.5),
        'hy_conv_b': nrm((L, 3 * HY_WIDTH), 0.01),
        'hy_w1': nrm((L, HY_EMB, HY_HIDDEN), HY_EMB ** -0.5),
        'hy_b1': nrm((L, HY_HIDDEN), 0.1),
        'hy_freq': gain((L, HY_HIDDEN)),
        'hy_w2': nrm((L, HY_HIDDEN, HY_HIDDEN), HY_HIDDEN ** -0.5),
        'hy_b2': nrm((L, HY_HIDDEN), 0.1),
        'hy_w3': nrm((L, HY_HIDDEN, 2 * HY_WIDTH), 0.005),
        'hy_b3': nrm((L, 2 * HY_WIDTH), 0.001),
        'hy_skip': nrm((L, HY_WIDTH), 0.5),
        'df_lq1': nrm((L, DF_DIM), 0.1),
        'df_lk1': nrm((L, DF_DIM), 0.1),
        'df_lq2': nrm((L, DF_DIM), 0.1),
        'df_lk2': nrm((L, DF_DIM), 0.1),
        'df_subln_g': gain((L, DF_V)),
        'w_br_a': nrm((L, MLA_OUT, D_MODEL), MLA_OUT ** -0.5),
        'w_br_b': nrm((L, HY_WIDTH, D_MODEL), HY_WIDTH ** -0.5),
        'w_br_c': nrm((L, DF_OUT, D_MODEL), DF_OUT ** -0.5),
        'w_out': nrm((L, D_MODEL, D_MODEL), D_MODEL ** -0.5),
        'w_fc1': nrm((L, D_MODEL, D_FF), D_MODEL ** -0.5),
        'w_fc2': nrm((L, D_FF, D_MODEL), D_FF ** -0.5),
        'final_norm_g': gain((D_MODEL,)),
    }


def reference(x, c, ctx, c_ctx, norm_mix_g, norm_ffn_g, w_mod, b_mod, w_in,
              mla_q_norm_g, mla_w_uq, mla_kv_norm_g, mla_w_ukv,
              hy_conv_w, hy_conv_b, hy_w1, hy_b1, hy_freq, hy_w2, hy_b2, hy_w3, hy_b3, hy_skip,
              df_lq1, df_lk1, df_lq2, df_lk2, df_subln_g,
              w_br_a, w_br_b, w_br_c, w_out, w_fc1, w_fc2, final_norm_g):
    n_lat = x.shape[1]
    rope_mla = axial_cos_sin(n_lat, MLA_ROPE)
    rope_df = axial_cos_sin(n_lat, DF_DIM)
    s_c = jax.nn.silu(c)
    s_cc = jax.nn.silu(c_ctx)
    xc = ctx
    for l in range(DEPTH):
        with_ctx = l < DEPTH - 1
        lam_init = 0.8 - 0.6 * math.exp(-0.3 * l)
        lp = {
            'w_in': w_in[l],
            'mla_q_norm_g': mla_q_norm_g[l], 'mla_w_uq': mla_w_uq[l],
            'mla_kv_norm_g': mla_kv_norm_g[l], 'mla_w_ukv': mla_w_ukv[l],
            'hy_conv_w': hy_conv_w[l], 'hy_conv_b': hy_conv_b[l],
            'hy_w1': hy_w1[l], 'hy_b1': hy_b1[l], 'hy_freq': hy_freq[l],
            'hy_w2': hy_w2[l], 'hy_b2': hy_b2[l], 'hy_w3': hy_w3[l], 'hy_b3': hy_b3[l],
            'hy_skip': hy_skip[l],
            'df_lq1': df_lq1[l], 'df_lk1': df_lk1[l], 'df_lq2': df_lq2[l], 'df_lk2': df_lk2[l],
            'df_subln_g': df_subln_g[l],
            'w_br_a': w_br_a[l], 'w_br_b': w_br_b[l], 'w_br_c': w_br_c[l], 'w_out': w_out[l],
        }
        sh_m, sc_m, g_m, sh_f, sc_f, g_f = [t[:, None] for t in
                                            jnp.split(s_c @ w_mod[l] + b_mod[l], 6, axis=-1)]
        csh_m, csc_m, cg_m, csh_f, csc_f, cg_f = jnp.split(s_cc @ w_mod[l] + b_mod[l], 6, axis=-1)

        h = modulate(rmsnorm(x, norm_mix_g[l]), sh_m, sc_m)
        hc = modulate(rmsnorm(xc, norm_mix_g[l]), csh_m, csc_m)
        y, y_c = token_mixers(h, hc, lp, lam_init, rope_mla, rope_df, with_ctx)

        x = x + g_m * y
        x = x + g_f * sqrelu_mlp(modulate(rmsnorm(x, norm_ffn_g[l]), sh_f, sc_f), w_fc1[l], w_fc2[l])
        if with_ctx:
            xc = xc + cg_m * y_c
            xc = xc + cg_f * sqrelu_mlp(modulate(rmsnorm(xc, norm_ffn_g[l]), csh_f, csc_f),
                                        w_fc1[l], w_fc2[l])
    return rmsnorm(x, final_norm_g)
```

```python
import math
import contextlib
import numpy as np
import ml_dtypes
import concourse.bass as bass
import concourse.mybir as mybir
from concourse.bass_utils import run_bass_kernel_spmd

F32 = mybir.dt.float32
BF16 = mybir.dt.bfloat16
ALU = mybir.AluOpType
AF = mybir.ActivationFunctionType
AX = mybir.AxisListType

ENGS = ("pe", "dve", "act", "pool", "sp")
NDMASEM = 8


def _prod(xs):
    r = 1
    for v in xs:
        r *= int(v)
    return r


_RS_CACHE = {}


def region_of(ap):
    t = ap.tensor
    nm = t.name
    rs = _RS_CACHE.get(nm)
    if rs is None:
        rs = _prod(list(t.shape)[1:])
        _RS_CACHE[nm] = rs
    off = int(ap.offset)
    r0, c0 = divmod(off, rs)
    r1, c1 = r0, c0
    ne = 1
    for step, cnt in ap.ap:
        step = int(step)
        cnt = int(cnt)
        if cnt <= 1 or step == 0:
            continue
        ne *= cnt
        a, b = divmod(step, rs)
        r1 += a * (cnt - 1)
        c1 += b * (cnt - 1)
    if c1 >= rs:
        r1 += c1 // rs
        c0, c1 = 0, rs - 1
    dense = ne >= (r1 + 1 - r0) * (c1 + 1 - c0)
    return (nm, r0, r1 + 1, c0, c1 + 1, dense)


def _ovl(a, b):
    return a[1] < b[2] and b[1] < a[2] and a[3] < b[4] and b[3] < a[4]


def _cov(a, b):
    return a[5] and a[1] <= b[1] and a[2] >= b[2] and a[3] <= b[3] and a[4] >= b[4]


class Op:
    __slots__ = ("eng", "fn", "idx", "cdeps", "ddeps", "dma", "inc", "semval", "ownwait")

    def __init__(self, eng, fn, idx):
        self.eng = eng
        self.fn = fn
        self.idx = idx
        self.cdeps = {}
        self.ddeps = {}
        self.dma = None
        self.inc = False
        self.semval = 0
        self.ownwait = None


class Sched:
    def __init__(self, nc):
        self.nc = nc
        self.ops = {e: [] for e in ENGS}
        self.order = []
        self.bufs = {}
        self.dma_use = {e: [0] * NDMASEM for e in ENGS}
        self.dma_rr = {e: 0 for e in ENGS}

    def _adddep(self, op, prod, kind):
        if prod[0] == "c":
            _, e, i = prod
            if e == op.eng:
                if kind != "raw":
                    return
                if e == "pe":
                    return
            if op.cdeps.get(e, -1) < i:
                op.cdeps[e] = i
        else:
            _, q, si, val = prod
            k = (q, si)
            if op.ddeps.get(k, 0) < val:
                op.ddeps[k] = val

    def add(self, eng, fn, reads=(), writes=(), dma=False):
        op = Op(eng, fn, len(self.ops[eng]))
        if dma:
            si = self.dma_rr[eng]
            self.dma_rr[eng] = (si + 1) % NDMASEM
            prev = self.dma_use[eng][si]
            self.dma_use[eng][si] = prev + 16
            op.dma = (eng, si, prev + 16)
            if prev:
                op.ownwait = (eng, si, prev)
            me = ("d", eng, si, prev + 16)
        else:
            me = ("c", eng, op.idx)
        rregs = [region_of(a) for a in reads]
        wregs = [region_of(a) for a in writes]
        for rg in rregs:
            b = self.bufs.setdefault(rg[0], {"w": [], "r": []})
            for (wr, prod) in b["w"]:
                if _ovl(wr, rg):
                    self._adddep(op, prod, "raw")
        for rg in wregs:
            b = self.bufs.setdefault(rg[0], {"w": [], "r": []})
            for (wr, prod) in b["w"]:
                if _ovl(wr, rg):
                    self._adddep(op, prod, "waw")
            for (rr, cons) in b["r"]:
                if _ovl(rr, rg):
                    self._adddep(op, cons, "war")
        for rg in rregs:
            b = self.bufs[rg[0]]
            if not dma:
                b["r"] = [(rr, c) for (rr, c) in b["r"]
                          if not (c[0] == "c" and c[1] == eng and _cov(rg, rr))]
            b["r"].append((rg, me))
        for rg in wregs:
            b = self.bufs[rg[0]]
            b["w"] = [(wr, p) for (wr, p) in b["w"] if not _cov(rg, wr)]
            b["r"] = [(rr, c) for (rr, c) in b["r"] if not _cov(rg, rr)]
            b["w"].append((rg, me))
        self.ops[eng].append(op)
        self.order.append(op)
        return op

    def dma(self, q, out, in_, **kw):
        return self.add(q, lambda e: e.dma_start(out=out, in_=in_, **kw),
                        reads=[in_], writes=[out], dma=True)

    def emit(self):
        nc = self.nc
        seen_c = {e: {f: -1 for f in ENGS} for e in ENGS}
        seen_d = {e: {} for e in ENGS}
        waits = {}
        for op in self.order:
            w = []
            e = op.eng
            for f, i in op.cdeps.items():
                if seen_c[e][f] < i:
                    seen_c[e][f] = i
                    w.append(("c", f, i))
                    self.ops[f][i].inc = True
            dd = dict(op.ddeps)
            if op.ownwait is not None:
                q, si, val = op.ownwait
                if dd.get((q, si), 0) < val:
                    dd[(q, si)] = val
            for (q, si), val in dd.items():
                if seen_d[e].get((q, si), 0) < val:
                    seen_d[e][(q, si)] = val
                    w.append(("d", q, si, val))
            waits[id(op)] = w
        for e in ENGS:
            n = 0
            for op in self.ops[e]:
                if op.inc:
                    n += 1
                op.semval = n
        self.stats = {e: (len(self.ops[e]), sum(1 for o in self.ops[e] if o.inc)) for e in ENGS}
        with contextlib.ExitStack() as st:
            csem = {e: st.enter_context(nc.semaphore("cs_" + e)) for e in ENGS}
            dsem = {e: [st.enter_context(nc.semaphore("ds_%s%d" % (e, i))) for i in range(NDMASEM)]
                    for e in ENGS if any(o.dma for o in self.ops[e])}
            block = st.enter_context(nc.Block())

            def run(e):
                def body(eng):
                    for op in self.ops[e]:
                        for w in waits[id(op)]:
                            if w[0] == "c":
                                eng.wait_ge(csem[w[1]], self.ops[w[1]][w[2]].semval)
                            else:
                                eng.wait_ge(dsem[w[1]][w[2]], w[3])
                        ins = op.fn(eng)
                        if op.dma is not None:
                            ins.then_inc(dsem[op.dma[0]][op.dma[1]], 16)
                        elif op.inc:
                            ins.then_inc(csem[e], 1)
                    if e in dsem:
                        for si in range(NDMASEM):
                            v = self.dma_use[e][si]
                            if v and seen_d[e].get((e, si), 0) < v:
                                eng.wait_ge(dsem[e][si], v)
                return body

            block.tensor(run("pe"))
            block.vector(run("dve"))
            block.scalar(run("act"))
            block.gpsimd(run("pool"))
            block.sync(run("sp"))


D = 1024
SEQ = 2048
CTX = 256
DEPTH = 4
NB = 4
TT = NB * (SEQ + CTX)
NBLK = TT // 512
D_IN = 5024
EPS = 1e-6
MLA_SCALE = 96 ** -0.5
DF_SCALE = 32 ** -0.5
NA = 2496
C_KRP, C_DQP, C_DKP = 1952, 1984, 2240
WEIGHT_NAMES = ["norm_mix_g", "norm_ffn_g", "w_mod", "b_mod", "w_in", "mla_q_norm_g", "mla_w_uq",
                "mla_kv_norm_g", "mla_w_ukv", "hy_conv_w", "hy_conv_b", "hy_w1", "hy_b1", "hy_freq",
                "hy_w2", "hy_b2", "hy_w3", "hy_b3", "hy_skip", "df_lq1", "df_lk1", "df_lq2", "df_lk2",
                "df_subln_g", "w_br_a", "w_br_b", "w_br_c", "w_out", "w_fc1", "w_fc2", "final_norm_g",
                "c_ctx"]
WEIGHT_SHAPES = {
    "norm_mix_g": [4, 1024], "norm_ffn_g": [4, 1024], "w_mod": [4, 1024, 6144], "b_mod": [4, 6144],
    "w_in": [4, 1024, 5024], "mla_q_norm_g": [4, 256], "mla_w_uq": [4, 256, 768],
    "mla_kv_norm_g": [4, 128], "mla_w_ukv": [4, 128, 1024], "hy_conv_w": [4, 3, 768],
    "hy_conv_b": [4, 768], "hy_w1": [4, 33, 64], "hy_b1": [4, 64], "hy_freq": [4, 64],
    "hy_w2": [4, 64, 64], "hy_b2": [4, 64], "hy_w3": [4, 64, 512], "hy_b3": [4, 512],
    "hy_skip": [4, 256], "df_lq1": [4, 32], "df_lk1": [4, 32], "df_lq2": [4, 32], "df_lk2": [4, 32],
    "df_subln_g": [4, 64], "w_br_a": [4, 512, 1024], "w_br_b": [4, 256, 1024], "w_br_c": [4, 256, 1024],
    "w_out": [4, 1024, 1024], "w_fc1": [4, 1024, 4096], "w_fc2": [4, 4096, 1024], "final_norm_g": [1024],
    "c_ctx": [1024],
}


def lam_init_of(l):
    return 0.8 - 0.6 * math.exp(-0.3 * l)


def host_constants():
    f32 = np.float32
    cs = {}
    L = SEQ
    t = np.arange(L)
    row = (t // 64).astype(f32)
    col = (t % 64).astype(f32)
    inv = (10000.0 ** (-np.arange(8, dtype=f32) / 8)).astype(f32)
    ang = np.concatenate([row[:, None] * inv, col[:, None] * inv], axis=-1).astype(f32)
    cosT = np.cos(ang).astype(f32).T
    sinT = np.sin(ang).astype(f32).T
    rc = np.zeros((128, L), f32)
    rsn = np.zeros((128, L), f32)
    for p in range(128):
        q = p % 32
        i = q % 16
        rc[p] = cosT[i]
        rsn[p] = -sinT[i] if q < 16 else sinT[i]
    cs["rope_c"] = rc
    cs["rope_s"] = rsn
    for tag, L in (("l", SEQ), ("c", CTX)):
        tt_ = np.linspace(0.0, 1.0, L, dtype=f32)[:, None]
        w = ((2.0 * math.pi / L) * np.arange(L, dtype=f32))[:, None].astype(f32)
        bands = np.linspace(1e-4, 15.0, 16, dtype=f32)[None]
        z = np.concatenate([tt_, np.cos(bands * w), -np.sin(bands * w)], axis=-1).astype(f32)
        cs["zT_" + tag] = np.ascontiguousarray(z.T)
        deltas = np.linspace(math.log(1e-2) / 1.5, math.log(1e-2) / 0.3, 256, dtype=f32)
        win = (np.exp(-tt_ * np.abs(deltas)[None]) + 0.05).astype(f32)
        ntt = L // 128
        cs["win_" + tag] = np.ascontiguousarray(win.reshape(ntt, 128, 256).transpose(1, 0, 2))
        N = 2 * L
        ti = np.arange(L, dtype=np.float64)[:, None]
        fi = np.arange(L, dtype=np.float64)[None, :]
        th = math.pi * (2 * fi + 1) / N
        Cm = np.cos(th * ti)
        Sm = np.sin(th * ti)
        nfc = L // 128
        M = np.stack([Cm, Sm], 0)
        FW = M.reshape(2, ntt, 128, nfc, 128).transpose(3, 2, 0, 1, 4)
        cs["FW_" + tag] = np.ascontiguousarray(FW).astype(ml_dtypes.bfloat16).reshape(nfc, 128, 2 * ntt * 128)
        tbw = 512 if L >= 512 else L
        ntb = L // tbw
        GI = M.reshape(2, ntb, tbw, nfc, 128).transpose(1, 4, 3, 0, 2)
        cs["GI_" + tag] = np.ascontiguousarray(GI).astype(ml_dtypes.bfloat16).reshape(ntb, 128, nfc * 2 * tbw)
    return cs


CONST_SPECS = {
    "rope_c": ([128, 2048], F32), "rope_s": ([128, 2048], F32),
    "zT_l": ([33, 2048], F32), "zT_c": ([33, 256], F32),
    "win_l": ([128, 16, 256], F32), "win_c": ([128, 2, 256], F32),
    "FW_l": ([16, 128, 4096], BF16), "FW_c": ([2, 128, 512], BF16),
    "GI_l": ([4, 128, 16384], BF16), "GI_c": ([1, 128, 1024], BF16),
}


class Arena:
    def __init__(self, handle, n):
        self.h = handle
        self.n = n
        self.top = 0

    def alloc(self, n, shape=None):
        n = (n + 15) // 16 * 16
        assert self.top + n <= self.n, ("arena overflow", self.h.name, self.top, n, self.n)
        v = self.h[:, self.top:self.top + n]
        self.top += n
        return v

    def a3(self, a, b):
        v = self.alloc(a * b)
        return v.rearrange("p (a b) -> p a b", a=a)


def build_program(n_layers=DEPTH, dbg=None):
    nc = bass.Bass("TRN2", target_bir_lowering=False)
    S = Sched(nc)
    dbg = dbg or []
    dr = {}
    dr["x"] = nc.dram_tensor("x", [NB, SEQ, D], F32, kind="ExternalInput")
    dr["c"] = nc.dram_tensor("c", [NB, D], F32, kind="ExternalInput")
    dr["ctx"] = nc.dram_tensor("ctx", [NB, CTX, D], F32, kind="ExternalInput")
    for nm in WEIGHT_NAMES:
        dr[nm] = nc.dram_tensor(nm, WEIGHT_SHAPES[nm], F32, kind="ExternalInput")
    for nm, (shp, dt) in CONST_SPECS.items():
        dr[nm] = nc.dram_tensor(nm, shp, dt, kind="ExternalInput")
    out = nc.dram_tensor("out", [NB, SEQ, D], F32, kind="ExternalOutput")

    def scratch(nm, shape, dt):
        kind = "ExternalOutput" if nm in dbg else "Internal"
        dr[nm] = nc.dram_tensor(nm, shape, dt, kind=kind)
        return dr[nm]

    xT = scratch("xT", [D, TT], F32)
    hT = scratch("hT", [D, TT], BF16)
    h2T = scratch("h2T", [D, TT], BF16)
    QnT = scratch("QnT", [512, TT], BF16)
    QrT = scratch("QrT", [256, TT], BF16)
    KnT = scratch("KnT", [512, TT], BF16)
    KrT = scratch("KrT", [32, TT], BF16)
    Vm = scratch("Vm", [TT, 512], BF16)
    phyT = scratch("phyT", [768, TT], BF16)
    DqT = scratch("DqT", [256, TT], BF16)
    DkT = scratch("DkT", [256, TT], BF16)
    Dv = scratch("Dv", [TT, 256], BF16)
    hx0T = scratch("hx0T", [256, TT], BF16)
    huT = scratch("huT", [256, TT], BF16)
    yaT = scratch("yaT", [512, TT], BF16)
    ybT = scratch("ybT", [256, TT], BF16)
    ycT = scratch("ycT", [256, TT], BF16)

    def fm(t):
        return t.rearrange("(c p) t -> p c t", p=128)

    st = contextlib.ExitStack()
    with st:
        AB_N = 58 * 1024
        AF_N = 18 * 1024
        abh = st.enter_context(nc.sbuf_tensor("arena_bf", [128, AB_N], BF16))
        afh = st.enter_context(nc.sbuf_tensor("arena_f", [128, AF_N], F32))
        AB = Arena(abh, AB_N)
        AFa = Arena(afh, AF_N)
        banks = [st.enter_context(nc.psum_tensor("bank%d" % i, [128, 512], F32)) for i in range(7)]
        bankT = st.enter_context(nc.psum_tensor("bankT", [128, 1024], BF16))
        rr = {"i": 0}

        def bank(lo=0, hi=7):
            n = hi - lo
            rr["i"] = (rr["i"] + 1) % n
            return banks[lo + rr["i"]][:, :]

        def mm(o, lhsT, rhs, start=True, stop=True):
            S.add("pe", lambda e: e.matmul(o, lhsT=lhsT, rhs=rhs, start=start, stop=stop),
                  reads=[lhsT, rhs], writes=[o])

        def tr(o, in_, ident_ap):
            S.add("pe", lambda e: e.transpose(o, in_, ident_ap), reads=[in_, ident_ap], writes=[o])

        def act(o, in_, func, scale=1.0, bias=0.0):
            rd = [in_]
            if not isinstance(scale, (int, float)):
                rd.append(scale)
            if not isinstance(bias, (int, float)):
                rd.append(bias)
            S.add("act", lambda e: e.activation(out=o, in_=in_, func=func, bias=bias, scale=scale),
                  reads=rd, writes=[o])

        def cp(eng, o, in_):
            if eng == "act":
                S.add("act", lambda e: e.copy(out=o, in_=in_), reads=[in_], writes=[o])
            else:
                S.add(eng, lambda e: e.tensor_copy(out=o, in_=in_), reads=[in_], writes=[o])

        def tt(eng, o, a, b, op):
            S.add(eng, lambda e: e.tensor_tensor(out=o, in0=a, in1=b, op=op), reads=[a, b], writes=[o])

        def ts(eng, o, a, s1, s2, op0, op1=None):
            rd = [a]
            if not isinstance(s1, (int, float)):
                rd.append(s1)
            if s2 is not None and not isinstance(s2, (int, float)):
                rd.append(s2)
            if op1 is None:
                S.add(eng, lambda e: e.tensor_scalar(out=o, in0=a, scalar1=s1, scalar2=None, op0=op0),
                      reads=rd, writes=[o])
            else:
                S.add(eng, lambda e: e.tensor_scalar(out=o, in0=a, scalar1=s1, scalar2=s2, op0=op0, op1=op1),
                      reads=rd, writes=[o])

        def stt(eng, o, a, s, b, op0, op1):
            rd = [a, b]
            if not isinstance(s, (int, float)):
                rd.append(s)
            S.add(eng, lambda e: e.scalar_tensor_tensor(out=o, in0=a, scalar=s, in1=b, op0=op0, op1=op1),
                  reads=rd, writes=[o])

        def recip(o, in_):
            S.add("dve", lambda e: e.reciprocal(out=o, in_=in_), reads=[in_], writes=[o])

        def memset(eng, o, v):
            S.add(eng, lambda e: e.memset(o, v), writes=[o])

        evq = {"i": 0}

        def evac(o, in_):
            evq["i"] ^= 1
            cp("dve" if evq["i"] else "act", o, in_)

        def rstd_from(ps_ap, o, n):
            act(o, ps_ap, AF.Sqrt, scale=1.0 / n, bias=EPS)
            recip(o, o)

        ident = AFa.alloc(128)
        memset("pool", ident, 0.0)
        S.add("pool", lambda e: e.affine_select(out=ident, in_=ident, pattern=[[-1, 128]],
                                                compare_op=ALU.not_equal, fill=1.0, base=0,
                                                channel_multiplier=1),
              reads=[ident], writes=[ident])
        identb = AB.alloc(128)
        cp("dve", identb, ident)
        onesb = AB.alloc(128)
        memset("pool", onesb, 1.0)
        onesf = AFa.alloc(128)
        memset("pool", onesf, 1.0)
        mask0 = AFa.alloc(16)[:, 0:1]
        memset("pool", mask0, 1.0)
        memset("pool", mask0[0:1, :], 0.0)
        rope_c = AFa.alloc(2048)
        rope_s = AFa.alloc(2048)
        S.dma("sp", rope_c, dr["rope_c"][:, :])
        S.dma("sp", rope_s, dr["rope_s"][:, :])

        VROW = {}
        rows = []

        def vadd(key, src_ap, n):
            VROW[key] = len(rows)
            for j in range(n):
                rows.append((src_ap, j))

        for l in range(DEPTH):
            vadd(("mixg", l), dr["norm_mix_g"][l].rearrange("(j p) -> j p", p=128), 8)
            vadd(("ffng", l), dr["norm_ffn_g"][l].rearrange("(j p) -> j p", p=128), 8)
            vadd(("bmod", l), dr["b_mod"][l].rearrange("(j p) -> j p", p=128), 48)
            vadd(("qg", l), dr["mla_q_norm_g"][l].rearrange("(j p) -> j p", p=128), 2)
            vadd(("kvg", l), dr["mla_kv_norm_g"][l].rearrange("(j p) -> j p", p=128), 1)
            vadd(("cw", l), dr["hy_conv_w"][l].rearrange("t (j p) -> (t j) p", p=128), 18)
            vadd(("cb", l), dr["hy_conv_b"][l].rearrange("(j p) -> j p", p=128), 6)
            vadd(("skip", l), dr["hy_skip"][l].rearrange("(j p) -> j p", p=128), 2)
        vadd(("fing",), dr["final_norm_g"].rearrange("(j p) -> j p", p=128), 8)
        NV = len(rows)
        vecT = AFa.alloc(NV)
        mark_f = AFa.top
        vrows = AFa.alloc(128)
        g0 = 0
        while g0 < NV:
            n = min(128, NV - g0)
            i = g0
            while i < g0 + n:
                src, j = rows[i]
                k = i
                while k + 1 < g0 + n and rows[k + 1][0] is src and rows[k + 1][1] == rows[k][1] + 1:
                    k += 1
                cnt = k - i + 1
                S.dma("sp", vrows[i - g0:i - g0 + cnt, :], src[j:j + cnt, :])
                i = k + 1
            pb = bank()
            tr(pb[:, 0:n], vrows[0:n, :], ident[0:n, 0:n])
            cp("dve", vecT[:, g0:g0 + n], pb[:, 0:n])
            g0 += n
        AFa.top = mark_f

        def vcol(key, j=0):
            c = VROW[key] + j
            return vecT[:, c:c + 1]

        smallv = AFa.alloc(DEPTH * 8)
        SV = {}
        for l in range(DEPTH):
            for i, nm in enumerate(["hy_b1", "hy_freq", "hy_b2", "df_subln_g"]):
                colv = smallv[0:64, l * 8 + i:l * 8 + i + 1]
                S.dma("sp", colv, dr[nm][l].rearrange("(p o) -> p o", o=1))
                SV[(nm, l)] = colv
            for i, (a, b) in enumerate([("hy_freq", "hy_b1"), ("hy_freq", "hy_b2")]):
                colv = smallv[0:64, l * 8 + 4 + i:l * 8 + 5 + i]
                tt("dve", colv, SV[(a, l)], SV[(b, l)], ALU.mult)
                SV[("fb%d" % (i + 1), l)] = colv
            colv = smallv[0:64, l * 8 + 6:l * 8 + 7]
            ts("dve", colv, SV[("df_subln_g", l)], 1.0 - lam_init_of(l), None, ALU.mult)
            SV[("sublng", l)] = colv

        neglamT = AFa.alloc(16)
        mark_f = AFa.top
        lamrow = AFa.alloc(1024)
        for i, nm in enumerate(["df_lq1", "df_lk1", "df_lq2", "df_lk2"]):
            S.dma("sp", lamrow[0:1, i * 128:(i + 1) * 128], dr[nm].rearrange("(o l) d -> o (l d)", o=1))
        tt("dve", lamrow[0:1, 512:640], lamrow[0:1, 0:128], lamrow[0:1, 128:256], ALU.mult)
        tt("dve", lamrow[0:1, 640:768], lamrow[0:1, 256:384], lamrow[0:1, 384:512], ALU.mult)
        for i in range(2):
            src = lamrow[0:1, 512 + i * 128:640 + i * 128].rearrange("p (l d) -> p l d", l=4)
            dst = lamrow[0:1, 768 + i * 4:772 + i * 4]
            S.add("dve", (lambda s_, d_: (lambda e: e.reduce_sum(out=d_, in_=s_, axis=AX.X)))(src, dst),
                  reads=[src], writes=[dst])
        act(lamrow[0:1, 768:776], lamrow[0:1, 768:776], AF.Exp)
        tt("dve", lamrow[0:1, 776:780], lamrow[0:1, 772:776], lamrow[0:1, 768:772], ALU.subtract)
        for l in range(DEPTH):
            ts("dve", lamrow[0:1, 780 + l:781 + l], lamrow[0:1, 776 + l:777 + l], -lam_init_of(l), None, ALU.add)
        pb = bank()
        mm(pb[0:64, 0:4], onesf[0:1, 0:64], lamrow[0:1, 780:784])
        cp("dve", neglamT[0:64, 0:4], pb[0:64, 0:4])
        AFa.top = mark_f

        modT = AFa.alloc(DEPTH * 48 * 5).rearrange("p (l j c) -> p l j c", l=DEPTH, j=48)
        A1 = AFa.alloc(DEPTH * 8 * 5).rearrange("p (l j c) -> p l j c", l=DEPTH, j=8)
        A2 = AFa.alloc(DEPTH * 8 * 5).rearrange("p (l j c) -> p l j c", l=DEPTH, j=8)
        mark_f = AFa.top
        mark_b = AB.top
        cs_rows = AFa.alloc(1024)
        S.dma("sp", cs_rows[0:4, :], dr["c"][:, :])
        S.dma("sp", cs_rows[4:5, :], dr["c_ctx"].rearrange("(o d) -> o d", o=1))
        act(cs_rows[0:5, :], cs_rows[0:5, :], AF.Silu)
        sT = AFa.alloc(48).rearrange("p (k c) -> p k c", k=8)
        for k in range(8):
            pb = bank()
            tr(pb[:, 0:5], cs_rows[0:5, k * 128:(k + 1) * 128], ident[0:5, 0:5])
            cp("dve", sT[:, k, 0:5], pb[:, 0:5])
        wmbuf = [AFa.alloc(4096).rearrange("p (k c) -> p k c", k=8) for _ in range(2)]
        it = 0
        for l in range(n_layers):
            wm_v = dr["w_mod"][l].rearrange("(k p) c -> p k c", p=128)
            for half in range(12):
                wb = wmbuf[it % 2]
                it += 1
                S.dma("sp", wb, wm_v[:, :, half * 512:(half + 1) * 512])
                for jj in range(4):
                    j = half * 4 + jj
                    pb = bank()
                    for k in range(8):
                        mm(pb[:, 0:5], wb[:, k, jj * 128:(jj + 1) * 128], sT[:, k, 0:5], start=(k == 0), stop=(k == 7))
                    ts("dve", modT[:, l, j, :], pb[:, 0:5], vcol(("bmod", l), j), None, ALU.add)
            for j in range(8):
                ts("dve", A1[:, l, j, :], modT[:, l, 8 + j, :], 1.0, vcol(("mixg", l), j), ALU.add, ALU.mult)
                ts("dve", A2[:, l, j, :], modT[:, l, 32 + j, :], 1.0, vcol(("ffng", l), j), ALU.add, ALU.mult)
        AFa.top = mark_f
        AB.top = mark_b

        mark_f = AFa.top
        xin = [AFa.alloc(1024) for _ in range(2)]
        xst = [AFa.a3(8, 512) for _ in range(1)]
        xT_v = fm(xT)
        ti = 0
        for blk in range(NBLK):
            stg = xst[0]
            for t4 in range(4):
                tok0 = blk * 512 + t4 * 128
                if tok0 < NB * CTX:
                    b, r0 = divmod(tok0, CTX)
                    src = dr["ctx"][b, r0:r0 + 128, :]
                else:
                    b, r0 = divmod(tok0 - NB * CTX, SEQ)
                    src = dr["x"][b, r0:r0 + 128, :]
                xi = xin[ti % 2]
                ti += 1
                S.dma("sp", xi, src)
                for half in range(2):
                    pb = bank()
                    for q in range(4):
                        j = half * 4 + q
                        tr(pb[:, q * 128:(q + 1) * 128], xi[:, j * 128:(j + 1) * 128], ident)
                    evac(stg[:, half * 4:half * 4 + 4, t4 * 128:(t4 + 1) * 128],
                         pb[:, :].rearrange("p (q t) -> p q t", q=4))
            S.dma("pool", xT_v[:, :, blk * 512:(blk + 1) * 512], stg)
        AFa.top = mark_f

        def blk_info(blk):
            if blk < 2:
                return 4, False, 0
            b = (blk - 2) // 4
            return b, True, ((blk - 2) % 4) * 512

        def keycols(b):
            return [(b * CTX, CTX), (NB * CTX + b * SEQ, SEQ)]

        for l in range(n_layers):
            with_ctx = l < DEPTH - 1
            mark_f = AFa.top
            mark_b = AB.top
            winA = AB.a3(8, NA)
            win_v = dr["w_in"][l].rearrange("(k p) c -> p k c", p=128)
            for k in range(8):
                S.dma("pool", winA[:, k, 0:1952], win_v[:, k, 0:1952])
            for k in range(8):
                cp("pool", winA[:, k, C_KRP:C_KRP + 16], winA[:, k, 400:416])
                cp("pool", winA[:, k, C_KRP + 16:C_KRP + 32], winA[:, k, 384:400])
                for (src0, dst0) in ((1184, C_DQP), (1440, C_DKP)):
                    sv = winA[:, k, src0:src0 + 256].rearrange("p (s h i) -> p s h i", s=8, h=2)
                    dv_ = winA[:, k, dst0:dst0 + 256].rearrange("p (s h i) -> p s h i", s=8, h=2)
                    cp("pool", dv_[:, :, 0, :], sv[:, :, 1, :])
                    cp("pool", dv_[:, :, 1, :], sv[:, :, 0, :])
            wuq = AB.a3(2, 768)
            S.dma("pool", wuq, dr["mla_w_uq"][l].rearrange("(k p) c -> p k c", p=128))
            wuq_n = AB.a3(2, 512)
            wuq_r = AB.a3(2, 256)
            wuq_p = AB.a3(2, 256)
            for k in range(2):
                sv = wuq[:, k, :].rearrange("p (h d) -> p h d", h=8)
                cp("pool", wuq_n[:, k, :].rearrange("p (h d) -> p h d", h=8), sv[:, :, 0:64])
                cp("pool", wuq_r[:, k, :].rearrange("p (h d) -> p h d", h=8), sv[:, :, 64:96])
                pv = wuq_p[:, k, :].rearrange("p (h d) -> p h d", h=8)
                cp("pool", pv[:, :, 0:16], sv[:, :, 80:96])
                cp("pool", pv[:, :, 16:32], sv[:, :, 64:80])
            wukv = AB.alloc(1024)
            S.dma("pool", wukv, dr["mla_w_ukv"][l])
            wkn = AB.alloc(512)
            wv = AB.alloc(512)
            sv = wukv.rearrange("p (h d) -> p h d", h=8)
            cp("pool", wkn.rearrange("p (h d) -> p h d", h=8), sv[:, :, 0:64])
            cp("pool", wv.rearrange("p (h d) -> p h d", h=8), sv[:, :, 64:128])

            xb_ = [AFa.a3(8, 512) for _ in range(2)]
            rstd_ = [AFa.alloc(512) for _ in range(2)]
            tmpf = [AFa.alloc(512) for _ in range(4)]
            sqb = AB.a3(8, 512)
            hTb = [AB.a3(8, 512) for _ in range(2)]
            qn = AB.a3(2, 512)
            kvn = AB.alloc(512)
            stg_q = AB.a3(4, 512)
            stg_k = AB.a3(4, 512)
            stg_r = AB.a3(2, 512)
            stg_kr = AB.alloc(512)
            stg_v = AB.a3(4, 512)
            stg_hy = AB.a3(6, 512)
            stg_d = AB.a3(4, 512)
            stg_dv = AB.a3(4, 256)
            tq = {"i": 0}

            def tmp():
                tq["i"] = (tq["i"] + 1) % 4
                return tmpf[tq["i"]]

            def rope_out(o, P, Pp, rows, rt0):
                t1 = tmp()
                t2 = tmp()
                tt("dve", t1[0:rows, :], P, rope_c[0:rows, rt0:rt0 + 512], ALU.mult)
                tt("dve", t2[0:rows, :], Pp, rope_s[0:rows, rt0:rt0 + 512], ALU.mult)
                tt("pool", o, t1[0:rows, :], t2[0:rows, :], ALU.add)

            for blk in range(NBLK):
                mc, islat, rt0 = blk_info(blk)
                c0 = blk * 512
                xb = xb_[blk % 2]
                hb = hTb[blk % 2]
                rs_ = rstd_[blk % 2]
                S.dma("sp", xb, xT_v[:, :, c0:c0 + 512])
                for j in range(8):
                    act(sqb[:, j, :], xb[:, j, :], AF.Square)
                pss = bank()
                for j in range(8):
                    mm(pss, onesb, sqb[:, j, :], start=(j == 0), stop=(j == 7))
                rstd_from(pss, rs_, 1024.0)
                for j in range(8):
                    t1 = tmp()
                    tt("dve", t1, xb[:, j, :], rs_, ALU.mult)
                    ts("pool", hb[:, j, :], t1, A1[:, l, j, mc:mc + 1], modT[:, l, j, mc:mc + 1], ALU.mult, ALU.add)
                S.dma("pool", fm(hT)[:, :, c0:c0 + 512], hb)

                def proj(col0, M=128, pb=None):
                    pb = pb or bank()
                    for k in range(8):
                        mm(pb[0:M, :], winA[:, k, col0:col0 + M], hb[:, k, :], start=(k == 0), stop=(k == 7))
                    return pb

                pq = [proj(0), proj(128)]
                for i in range(2):
                    act(sqb[:, i, :], pq[i], AF.Square)
                pss = bank()
                for i in range(2):
                    mm(pss, onesb, sqb[:, i, :], start=(i == 0), stop=(i == 1))
                rq = tmp()
                rstd_from(pss, rq, 256.0)
                for i in range(2):
                    t1 = tmp()
                    tt("dve", t1, pq[i], rq, ALU.mult)
                    ts("pool", qn[:, i, :], t1, vcol(("qg", l), i), None, ALU.mult)
                for ch in range(4):
                    pb = bank()
                    for k in range(2):
                        mm(pb, wuq_n[:, k, ch * 128:(ch + 1) * 128], qn[:, k, :], start=(k == 0), stop=(k == 1))
                    evac(stg_q[:, ch, :], pb)
                S.dma("pool", fm(QnT)[:, :, c0:c0 + 512], stg_q)
                for ch in range(2):
                    pb = bank()
                    for k in range(2):
                        mm(pb, wuq_r[:, k, ch * 128:(ch + 1) * 128], qn[:, k, :], start=(k == 0), stop=(k == 1))
                    if islat:
                        pb2 = bank()
                        for k in range(2):
                            mm(pb2, wuq_p[:, k, ch * 128:(ch + 1) * 128], qn[:, k, :], start=(k == 0), stop=(k == 1))
                        rope_out(stg_r[:, ch, :], pb, pb2, 128, rt0)
                    else:
                        evac(stg_r[:, ch, :], pb)
                S.dma("pool", fm(QrT)[:, :, c0:c0 + 512], stg_r)
                pkv = proj(256)
                act(sqb[:, 0, :], pkv, AF.Square)
                pss = bank()
                mm(pss, onesb, sqb[:, 0, :])
                rk = tmp()
                rstd_from(pss, rk, 128.0)
                t1 = tmp()
                tt("dve", t1, pkv, rk, ALU.mult)
                ts("pool", kvn, t1, vcol(("kvg", l), 0), None, ALU.mult)
                for ch in range(4):
                    pb = bank()
                    mm(pb, wkn[:, ch * 128:(ch + 1) * 128], kvn)
                    evac(stg_k[:, ch, :], pb)
                S.dma("pool", fm(KnT)[:, :, c0:c0 + 512], stg_k)
                for t4 in range(4):
                    pb = bank()
                    mm(pb, kvn[:, t4 * 128:(t4 + 1) * 128], wv)
                    evac(stg_v[:, t4, :], pb)
                S.dma("pool", Vm.rearrange("(n p) c -> p n c", p=128)[:, blk * 4:(blk + 1) * 4, :], stg_v)
                pb = proj(384, M=32)
                if islat:
                    pb2 = proj(C_KRP, M=32)
                    rope_out(stg_kr[0:32, :], pb[0:32, :], pb2[0:32, :], 32, rt0)
                else:
                    evac(stg_kr[0:32, :], pb[0:32, :])
                S.dma("pool", KrT[:, c0:c0 + 512], stg_kr[0:32, :])
                for ch in range(6):
                    pb = proj(416 + ch * 128)
                    evac(stg_hy[:, ch, :], pb)
                S.dma("pool", fm(phyT)[:, :, c0:c0 + 512], stg_hy)
                for qi, (cbase, pbase, dst) in enumerate(((1184, C_DQP, DqT), (1440, C_DKP, DkT))):
                    for ch in range(2):
                        pb = proj(cbase + ch * 128)
                        if islat:
                            pb2 = proj(pbase + ch * 128)
                            rope_out(stg_d[:, qi * 2 + ch, :], pb, pb2, 128, rt0)
                        else:
                            evac(stg_d[:, qi * 2 + ch, :], pb)
                    S.dma("pool", fm(dst)[:, :, c0:c0 + 512], stg_d[:, qi * 2:qi * 2 + 2, :])
                for t4 in range(4):
                    pb = bank()
                    for k in range(8):
                        mm(pb[:, 0:256], hb[:, k, t4 * 128:(t4 + 1) * 128], winA[:, k, 1696:1952],
                           start=(k == 0), stop=(k == 7))
                    evac(stg_dv[:, t4, :], pb[:, 0:256])
                S.dma("pool", Dv.rearrange("(n p) c -> p n c", p=128)[:, blk * 4:(blk + 1) * 4, :], stg_dv)
            AFa.top = mark_f
            AB.top = mark_b

            def hyena(tag, L, col_of_b):
                mark_f0 = AFa.top
                mark_b0 = AB.top
                ntt = L // 128
                nfc = L // 128
                N = 2 * L
                tbw = 512 if L >= 512 else L
                ntb = L // tbw
                GB = 2
                Ec = AB.a3(ntt, 256)
                Es = AB.a3(ntt, 256)
                mark_f = AFa.top
                w1 = AFa.alloc(64)
                w2 = AFa.alloc(64)
                w3e = AFa.alloc(512)
                S.dma("sp", w1[0:33, 0:64], dr["hy_w1"][l])
                S.dma("sp", w2[0:64, 0:64], dr["hy_w2"][l])
                S.dma("sp", w3e[0:64, :], dr["hy_w3"][l])
                S.dma("sp", w3e[64:65, :], dr["hy_b3"][l].rearrange("(o d) -> o d", o=1))
                zTb = AFa.alloc(512)
                a1b = AFa.alloc(512)
                a2T = AFa.alloc(L)
                memset("pool", a2T[64:65, :], 1.0)
                tf = [AFa.alloc(512) for _ in range(2)]
                tfk = AFa.alloc(512)
                for tb in range(ntb):
                    cs_ = slice(tb * tbw, (tb + 1) * tbw)
                    S.dma("sp", zTb[0:33, 0:tbw], dr["zT_" + tag][:, cs_])
                    for (wm_, src, dst, fbk) in ((w1[0:33, 0:64], zTb[0:33, 0:tbw], a1b[0:64, 0:tbw], "fb1"),
                                                 (w2[0:64, 0:64], a1b[0:64, 0:tbw], a2T[0:64, cs_], "fb2")):
                        pb = bank()
                        mm(pb[0:64, 0:tbw], wm_, src)
                        t_ = tf[0][0:64, 0:tbw]
                        ts("dve", t_, pb[0:64, 0:tbw], SV[("hy_freq", l)], SV[(fbk, l)], ALU.mult, ALU.add)
                        ts("dve", t_, t_, 1.0 / (2.0 * math.pi), 16.0, ALU.mult, ALU.add)
                        ki = tf[1][0:64, 0:tbw].bitcast(mybir.dt.int32)
                        cp("dve", ki, t_)
                        kf = tfk[0:64, 0:tbw]
                        cp("dve", kf, ki)
                        tt("dve", t_, t_, kf, ALU.subtract)
                        ts("dve", kf, t_, 0.5, None, ALU.is_gt)
                        tt("dve", t_, t_, kf, ALU.subtract)
                        ts("dve", kf, t_, -0.5, None, ALU.is_lt)
                        tt("dve", t_, t_, kf, ALU.add)
                        act(dst, t_, AF.Sin, scale=2.0 * math.pi)
                win = AFa.a3(ntt, 256)
                S.dma("sp", win, dr["win_" + tag][:, :, :])
                for t_i in range(ntt):
                    pb = bank()
                    mm(pb, a2T[0:65, t_i * 128:(t_i + 1) * 128], w3e[0:65, :])
                    hf = tf[0][:, 0:256]
                    hbk = tf[1][:, 0:256]
                    tt("dve", hf, pb[:, 0:256], win[:, t_i, :], ALU.mult)
                    tt("dve", hbk, pb[:, 256:512], win[:, t_i, :], ALU.mult)
                    tt("pool", Es[:, t_i, :], hbk, hf, ALU.subtract)
                    if t_i == 0:
                        stt("dve", Ec[:, t_i, :], hbk, mask0, hf, ALU.mult, ALU.add)
                    else:
                        tt("pool", Ec[:, t_i, :], hbk, hf, ALU.add)
                AFa.top = mark_f
                cw = lambda tap, ch: vcol(("cw", l), tap * 6 + ch)
                Y = AB.alloc(nfc * 2 * GB * 256).rearrange("p (q b c) -> p q b c", q=nfc * 2, b=GB)
                tf = [AFa.alloc(512) for _ in range(2)]
                mark_f1 = AFa.top
                mark_b1 = AB.top
                for g in range(NB // GB):
                    AFa.top = mark_f1
                    AB.top = mark_b1
                    uTok = AB.alloc(ntt * GB * 256).rearrange("p (t b c) -> p t b c", t=ntt, b=GB)
                    pt = [AB.alloc(L) for _ in range(3)]
                    cf = [AFa.alloc(L) for _ in range(2)]
                    uTb = AB.alloc(L)
                    x0b = AB.alloc(L)

                    def conv(dst, src, ch):
                        ts("dve", dst, src, cw(1, ch), vcol(("cb", l), ch), ALU.mult, ALU.add)
                        stt("dve", dst[:, 1:L], src[:, 0:L - 1], cw(0, ch), dst[:, 1:L], ALU.mult, ALU.add)
                        stt("dve", dst[:, 0:L - 1], src[:, 1:L], cw(2, ch), dst[:, 0:L - 1], ALU.mult, ALU.add)

                    for bl in range(GB):
                        b = g * GB + bl
                        cb0 = col_of_b(b)
                        for i in range(2):
                            for q, ch in enumerate((i, 2 + i, 4 + i)):
                                S.dma("sp", pt[q], phyT[ch * 128:(ch + 1) * 128, cb0:cb0 + L])
                            conv(cf[0], pt[0], i)
                            cp("pool", x0b, cf[0])
                            S.dma("pool", hx0T[i * 128:(i + 1) * 128, cb0:cb0 + L], x0b)
                            conv(cf[0], pt[1], 2 + i)
                            conv(cf[1], pt[2], 4 + i)
                            tt("dve", uTb, cf[0], cf[1], ALU.mult)
                            S.dma("pool", huT[i * 128:(i + 1) * 128, cb0:cb0 + L], uTb)
                            for t0 in range(0, ntt, 4):
                                nt_ = min(4, ntt - t0)
                                for q in range(nt_):
                                    tr(bankT[:, q * 128:(q + 1) * 128], uTb[:, (t0 + q) * 128:(t0 + q + 1) * 128], identb)
                                evac(uTok[:, t0:t0 + nt_, bl, i * 128:(i + 1) * 128],
                                     bankT[:, 0:nt_ * 128].rearrange("p (q c) -> p q c", q=nt_))
                    FWb = [AB.alloc(2 * ntt * 128).rearrange("p (r t f) -> p r t f", r=2, t=ntt) for _ in range(2)]
                    Hb = [AFa.a3(2, 256) for _ in range(2)]
                    yt = [AFa.alloc(256) for _ in range(4)]
                    for fc in range(nfc):
                        Fw = FWb[fc % 2]
                        H = Hb[fc % 2]
                        S.dma("sp", Fw, dr["FW_" + tag][fc].rearrange("p (r t f) -> p r t f", r=2, t=ntt))
                        for ri, E in ((0, Ec), (1, Es)):
                            pb = bank()
                            for t_i in range(ntt):
                                mm(pb[:, 0:256], Fw[:, ri, t_i, :], E[:, t_i, :], start=(t_i == 0), stop=(t_i == ntt - 1))
                            evac(H[:, ri, :], pb[:, 0:256])
                        for bl in range(GB):
                            pr = bank()
                            pi = bank()
                            for ri, pb in ((0, pr), (1, pi)):
                                for t_i in range(ntt):
                                    mm(pb[:, 0:256], Fw[:, ri, t_i, :], uTok[:, t_i, bl, :],
                                       start=(t_i == 0), stop=(t_i == ntt - 1))
                            tt("dve", yt[0], pr[:, 0:256], H[:, 0, :], ALU.mult)
                            tt("dve", yt[1], pi[:, 0:256], H[:, 1, :], ALU.mult)
                            tt("pool", Y[:, fc * 2, bl, :], yt[0], yt[1], ALU.add)
                            tt("dve", yt[2], pi[:, 0:256], H[:, 0, :], ALU.mult)
                            tt("dve", yt[3], pr[:, 0:256], H[:, 1, :], ALU.mult)
                            tt("pool", Y[:, fc * 2 + 1, bl, :], yt[2], yt[3], ALU.subtract)
                    AFa.top = mark_f1
                    AB.top = mark_b1
                    GIb = AB.alloc(nfc * 2 * tbw).rearrange("p (q t) -> p q t", q=nfc * 2)
                    x0l = [AB.alloc(512) for _ in range(2)]
                    ul = [AB.alloc(512) for _ in range(2)]
                    ybs = [AB.alloc(512) for _ in range(2)]
                    it_ = 0
                    for tb in range(ntb):
                        S.dma("sp", GIb, dr["GI_" + tag][tb].rearrange("p (q t) -> p q t", q=nfc * 2))
                        for bl in range(GB):
                            b = g * GB + bl
                            cb0 = col_of_b(b) + tb * tbw
                            for cc in range(2):
                                x0_ = x0l[it_ % 2][:, 0:tbw]
                                u_ = ul[it_ % 2][:, 0:tbw]
                                yo = ybs[it_ % 2][:, 0:tbw]
                                it_ += 1
                                S.dma("sp", x0_, hx0T[cc * 128:(cc + 1) * 128, cb0:cb0 + tbw])
                                S.dma("sp", u_, huT[cc * 128:(cc + 1) * 128, cb0:cb0 + tbw])
                                pb = bank()
                                for q in range(nfc * 2):
                                    mm(pb[:, 0:tbw], Y[:, q, bl, cc * 128:(cc + 1) * 128], GIb[:, q, :],
                                       start=(q == 0), stop=(q == nfc * 2 - 1))
                                t1 = tf[0][:, 0:tbw]
                                t2 = tf[1][:, 0:tbw]
                                ts("pool", t1, u_, vcol(("skip", l), cc), None, ALU.mult)
                                stt("dve", t2, pb[:, 0:tbw], 2.0 / N, t1, ALU.mult, ALU.add)
                                tt("dve", yo, t2, x0_, ALU.mult)
                                S.dma("pool", ybT[cc * 128:(cc + 1) * 128, cb0:cb0 + tbw], yo)
                AFa.top = mark_f0
                AB.top = mark_b0

            hyena("l", SEQ, lambda b: NB * CTX + b * SEQ)
            if with_ctx:
                hyena("c", CTX, lambda b: b * CTX)

            def attention(kind):
                mark_f = AFa.top
                mark_b = AB.top
                NK = SEQ + CTX
                nsub = 1 if kind == "mla" else 2
                kbuf = [[AB.alloc(NK) for _ in range(nsub)] for _ in range(2)]
                qbuf = [[AB.alloc(NK) for _ in range(nsub)] for _ in range(2)]
                krb = [AB.alloc(NK) for _ in range(2)] if kind == "mla" else None
                qrb = [AB.alloc(NK) for _ in range(2)] if kind == "mla" else None
                vaug = [AB.alloc(18 * 128).rearrange("p (t c) -> p t c", t=18) for _ in range(2)]
                for v_ in vaug:
                    memset("pool", v_[:, :, 64:128], 1.0)
                pT = [AB.alloc(512) for _ in range(3)]
                ost = [AB.alloc(512) for _ in range(2)]
                rrb = [AFa.alloc(512) for _ in range(2)]
                onf = [AFa.alloc(512) for _ in range(2)]
                of_ = AFa.alloc(512)
                sqd = AB.alloc(512)
                scale = MLA_SCALE if kind == "mla" else DF_SCALE
                nh = 8 if kind == "mla" else 4
                cnt = 0
                pi_ = 0
                for b in range(NB):
                    kc = keycols(b)
                    if kind == "mla":
                        Kr = krb[b % 2]
                        o_ = 0
                        for (cc0, n_) in kc:
                            S.dma("sp", Kr[0:32, o_:o_ + n_], KrT[:, cc0:cc0 + n_])
                            o_ += n_
                    for h in range(nh):
                        par = cnt % 2
                        cnt += 1
                        Ks = kbuf[par]
                        Qs = qbuf[par]
                        Va = vaug[par]
                        o_ = 0
                        for (cc0, n_) in kc:
                            if kind == "mla":
                                S.dma("sp", Ks[0][0:64, o_:o_ + n_], KnT[h * 64:(h + 1) * 64, cc0:cc0 + n_])
                                S.dma("sp", Qs[0][0:64, o_:o_ + n_], QnT[h * 64:(h + 1) * 64, cc0:cc0 + n_])
                                S.dma("sp", qrb[par][0:32, o_:o_ + n_], QrT[h * 32:(h + 1) * 32, cc0:cc0 + n_])
                                vsrc = Vm[cc0:cc0 + n_, h * 64:(h + 1) * 64]
                            else:
                                for s_ in range(2):
                                    r0 = (2 * h + s_) * 32
                                    S.dma("sp", Ks[s_][0:32, o_:o_ + n_], DkT[r0:r0 + 32, cc0:cc0 + n_])
                                    S.dma("sp", Qs[s_][0:32, o_:o_ + n_], DqT[r0:r0 + 32, cc0:cc0 + n_])
                                vsrc = Dv[cc0:cc0 + n_, h * 64:(h + 1) * 64]
                            S.dma("sp", Va[:, o_ // 128:(o_ + n_) // 128, 0:64],
                                  vsrc.rearrange("(t p) c -> p t c", p=128))
                            o_ += n_
                        qblocks = [(CTX + i * 512, 512, 18) for i in range(4)]
                        if with_ctx:
                            qblocks.append((0, CTX, 2))
                        for (q0, qw, nkt) in qblocks:
                            accs = []
                            for s_ in range(nsub):
                                acc = banks[4 + (pi_ + s_) % 3]
                                accs.append(acc)
                                for kt in range(nkt):
                                    ps = banks[(pi_ * 2 + kt) % 4]
                                    if kind == "mla":
                                        mm(ps[:, 0:qw], Ks[0][0:64, kt * 128:(kt + 1) * 128], Qs[0][0:64, q0:q0 + qw],
                                           start=True, stop=False)
                                        mm(ps[:, 0:qw], Kr[0:32, kt * 128:(kt + 1) * 128], qrb[par][0:32, q0:q0 + qw],
                                           start=False, stop=True)
                                    else:
                                        mm(ps[:, 0:qw], Ks[s_][0:32, kt * 128:(kt + 1) * 128], Qs[s_][0:32, q0:q0 + qw])
                                    p_ = pT[(kt + s_) % 3]
                                    act(p_[:, 0:qw], ps[:, 0:qw], AF.Exp, scale=scale)
                                    mm(acc[:, 0:qw], Va[:, kt, :], p_[:, 0:qw], start=(kt == 0), stop=(kt == nkt - 1))
                            pi_ += nsub
                            if q0 >= CTX:
                                gcol = NB * CTX + b * SEQ + (q0 - CTX)
                            else:
                                gcol = b * CTX
                            os_ = ost[pi_ % 2]
                            if kind == "mla":
                                r_ = rrb[pi_ % 2]
                                recip(r_[0:64, 0:qw], accs[0][64:128, 0:qw])
                                tt("dve", os_[0:64, 0:qw], accs[0][0:64, 0:qw], r_[0:64, 0:qw], ALU.mult)
                                S.dma("pool", yaT[h * 64:(h + 1) * 64, gcol:gcol + qw], os_[0:64, 0:qw])
                            else:
                                for s_ in range(2):
                                    r_ = rrb[s_]
                                    recip(r_[0:64, 0:qw], accs[s_][64:128, 0:qw])
                                    tt("dve", onf[s_][0:64, 0:qw], accs[s_][0:64, 0:qw], r_[0:64, 0:qw], ALU.mult)
                                o = of_[0:64, 0:qw]
                                stt("dve", o, onf[1][0:64, 0:qw], neglamT[0:64, l:l + 1], onf[0][0:64, 0:qw],
                                    ALU.mult, ALU.add)
                                tt("pool", sqd[0:64, 0:qw], o, o, ALU.mult)
                                pn = banks[(pi_ * 2) % 4]
                                mm(pn[0:64, 0:qw], onesb[0:64, 0:64], sqd[0:64, 0:qw])
                                rn = rrb[0][0:64, 0:qw]
                                rstd_from(pn[0:64, 0:qw], rn, 64.0)
                                tt("dve", o, o, rn, ALU.mult)
                                ts("pool", os_[0:64, 0:qw], o, SV[("sublng", l)], None, ALU.mult)
                                S.dma("pool", ycT[h * 64:(h + 1) * 64, gcol:gcol + qw], os_[0:64, 0:qw])
                AFa.top = mark_f
                AB.top = mark_b

            attention("mla")
            attention("diff")

            mark_f = AFa.top
            mark_b = AB.top
            blks = list(range(NBLK)) if with_ctx else list(range(2, NBLK))
            wg = AB.a3(8, 3072)
            for k in range(8):
                S.dma("pool", wg[:, k, :], win_v[:, k, 1952:5024])
            wbr = AB.a3(8, 1024)
            S.dma("pool", wbr[:, 0:4, :], dr["w_br_a"][l].rearrange("(k p) c -> p k c", p=128))
            S.dma("pool", wbr[:, 4:6, :], dr["w_br_b"][l].rearrange("(k p) c -> p k c", p=128))
            S.dma("pool", wbr[:, 6:8, :], dr["w_br_c"][l].rearrange("(k p) c -> p k c", p=128))
            wout = AB.a3(8, 1024)
            S.dma("pool", wout, dr["w_out"][l].rearrange("(k p) c -> p k c", p=128))
            hb_ = AB.a3(8, 512)
            yb_ = AB.a3(8, 512)
            mT = AB.a3(8, 512)
            h2b = AB.a3(8, 512)
            sqb = h2b
            xb = AFa.a3(8, 512)
            sg = [AFa.alloc(512) for _ in range(3)]
            tm = [AFa.alloc(512) for _ in range(3)]
            rs_ = AFa.alloc(512)
            brk = ((0, 4), (4, 2), (6, 2))
            for blk in blks:
                mc, islat, rt0 = blk_info(blk)
                c0 = blk * 512
                S.dma("sp", hb_, fm(hT)[:, :, c0:c0 + 512])
                S.dma("sp", yb_[:, 0:4, :], fm(yaT)[:, :, c0:c0 + 512])
                S.dma("sp", yb_[:, 4:6, :], fm(ybT)[:, :, c0:c0 + 512])
                S.dma("sp", yb_[:, 6:8, :], fm(ycT)[:, :, c0:c0 + 512])
                S.dma("sp", xb, xT_v[:, :, c0:c0 + 512])
                for j in range(8):
                    for bi, (k0, nk) in enumerate(brk):
                        pg = bank()
                        for k in range(8):
                            mm(pg, wg[:, k, bi * 1024 + j * 128:bi * 1024 + (j + 1) * 128], hb_[:, k, :],
                               start=(k == 0), stop=(k == 7))
                        act(sg[bi], pg, AF.Sigmoid)
                        pp = bank()
                        for k in range(nk):
                            mm(pp, wbr[:, k0 + k, j * 128:(j + 1) * 128], yb_[:, k0 + k, :],
                               start=(k == 0), stop=(k == nk - 1))
                        tt("dve", tm[bi], pp, sg[bi], ALU.mult)
                    tt("pool", tm[0], tm[0], tm[1], ALU.add)
                    tt("pool", mT[:, j, :], tm[0], tm[2], ALU.add)
                for j in range(8):
                    py = bank()
                    for k in range(8):
                        mm(py, wout[:, k, j * 128:(j + 1) * 128], mT[:, k, :], start=(k == 0), stop=(k == 7))
                    stt("dve", xb[:, j, :], py, modT[:, l, 16 + j, mc:mc + 1], xb[:, j, :], ALU.mult, ALU.add)
                    act(sqb[:, j, :], xb[:, j, :], AF.Square)
                S.dma("pool", xT_v[:, :, c0:c0 + 512], xb)
                pss = bank()
                for j in range(8):
                    mm(pss, onesb, sqb[:, j, :], start=(j == 0), stop=(j == 7))
                rstd_from(pss, rs_, 1024.0)
                for j in range(8):
                    t1 = tm[j % 3]
                    tt("dve", t1, xb[:, j, :], rs_, ALU.mult)
                    ts("pool", h2b[:, j, :], t1, A2[:, l, j, mc:mc + 1], modT[:, l, 24 + j, mc:mc + 1], ALU.mult, ALU.add)
                S.dma("pool", fm(h2T)[:, :, c0:c0 + 512], h2b)
            AFa.top = mark_f
            AB.top = mark_b

            for half in range(2):
                mark_f = AFa.top
                mark_b = AB.top
                w1h = AB.a3(8, 2048)
                w1_v = dr["w_fc1"][l].rearrange("(k p) c -> p k c", p=128)
                for k in range(8):
                    S.dma("pool", w1h[:, k, :], w1_v[:, k, half * 2048:(half + 1) * 2048])
                w2h = AB.a3(16, 1024)
                w2_v = dr["w_fc2"][l].rearrange("(k p) c -> p k c", p=128)
                for k4 in range(4):
                    S.dma("pool", w2h[:, k4 * 4:(k4 + 1) * 4, :], w2_v[:, half * 16 + k4 * 4:half * 16 + (k4 + 1) * 4, :])
                h2l = [AB.a3(8, 512) for _ in range(2)]
                aT = AB.a3(16, 512)
                xl = [AFa.a3(8, 512) for _ in range(2)]
                rl = [AFa.alloc(512) for _ in range(3)]
                for bi_, blk in enumerate(blks):
                    mc, islat, rt0 = blk_info(blk)
                    c0 = blk * 512
                    h2_ = h2l[bi_ % 2]
                    xb = xl[bi_ % 2]
                    S.dma("sp", h2_, fm(h2T)[:, :, c0:c0 + 512])
                    S.dma("sp", xb, xT_v[:, :, c0:c0 + 512])
                    for jf in range(16):
                        pf = bank()
                        for k in range(8):
                            mm(pf, w1h[:, k, jf * 128:(jf + 1) * 128], h2_[:, k, :], start=(k == 0), stop=(k == 7))
                        r_ = rl[jf % 3]
                        if jf % 2 == 0:
                            act(r_, pf, AF.Relu)
                        else:
                            ts("dve", r_, pf, 0.0, None, ALU.max)
                        tt("pool", aT[:, jf, :], r_, r_, ALU.mult)
                    for j in range(8):
                        py = bank()
                        for k in range(16):
                            mm(py, w2h[:, k, j * 128:(j + 1) * 128], aT[:, k, :], start=(k == 0), stop=(k == 15))
                        stt("dve", xb[:, j, :], py, modT[:, l, 40 + j, mc:mc + 1], xb[:, j, :], ALU.mult, ALU.add)
                    S.dma("pool", xT_v[:, :, c0:c0 + 512], xb)
                AFa.top = mark_f
                AB.top = mark_b

        mark_f = AFa.top
        mark_b = AB.top
        xl = [AFa.a3(8, 512) for _ in range(2)]
        sqb = AB.a3(8, 512)
        rs_ = AFa.alloc(512)
        ob = [AFa.alloc(1024) for _ in range(2)]
        oi = 0
        for blk in range(2, NBLK):
            b, islat, rt0 = blk_info(blk)
            c0 = blk * 512
            xb = xl[blk % 2]
            S.dma("sp", xb, xT_v[:, :, c0:c0 + 512])
            for j in range(8):
                act(sqb[:, j, :], xb[:, j, :], AF.Square)
            pss = bank()
            for j in range(8):
                mm(pss, onesb, sqb[:, j, :], start=(j == 0), stop=(j == 7))
            rstd_from(pss, rs_, 1024.0)
            for j in range(8):
                tt("dve", xb[:, j, :], xb[:, j, :], rs_, ALU.mult)
                ts("pool", xb[:, j, :], xb[:, j, :], vcol(("fing",), j), None, ALU.mult)
            for t4 in range(4):
                o_ = ob[oi % 2]
                oi += 1
                for half in range(2):
                    pb = bank()
                    for q in range(4):
                        j = half * 4 + q
                        tr(pb[:, q * 128:(q + 1) * 128], xb[:, j, t4 * 128:(t4 + 1) * 128], ident)
                    evac(o_[:, half * 512:(half + 1) * 512], pb)
                r0 = rt0 + t4 * 128
                S.dma("pool", out[b, r0:r0 + 128, :], o_)
        AFa.top = mark_f
        AB.top = mark_b

        S.emit()
    return nc, S


_CACHE = {}


def kernel(**inputs):
    n_cores = 8
    if "nc" not in _CACHE:
        _CACHE["nc"] = build_program()[0]
        _CACHE["consts"] = host_constants()
    nc = _CACHE["nc"]
    consts = _CACHE["consts"]
    x = np.ascontiguousarray(np.asarray(inputs["x"], dtype=np.float32))
    c = np.ascontiguousarray(np.asarray(inputs["c"], dtype=np.float32))
    ctx = np.ascontiguousarray(np.asarray(inputs["ctx"], dtype=np.float32))
    shared = {nm: np.ascontiguousarray(np.asarray(inputs[nm], dtype=np.float32)) for nm in WEIGHT_NAMES}
    shared.update(consts)
    in_maps = []
    for i in range(n_cores):
        m = dict(shared)
        m["x"] = x[i * NB:(i + 1) * NB]
        m["c"] = c[i * NB:(i + 1) * NB]
        m["ctx"] = ctx[i * NB:(i + 1) * NB]
        in_maps.append(m)
    res = run_bass_kernel_spmd(nc, in_maps, core_ids=list(range(n_cores)))
    return np.concatenate([np.asarray(r["out"], dtype=np.float32) for r in res.results], axis=0)
```

```python
import math
import contextlib
from collections import deque
import numpy as np
import ml_dtypes
import concourse.bass as bass
import concourse.mybir as mybir
from concourse.bass_utils import run_bass_kernel_spmd

F32 = mybir.dt.float32
BF16 = mybir.dt.bfloat16
ALU = mybir.AluOpType
AF = mybir.ActivationFunctionType
AX = mybir.AxisListType

ENGS = ("pe", "dve", "act", "pool", "sp")
NDMASEM = 8


def _prod(xs):
    r = 1
    for v in xs:
        r *= int(v)
    return r


_RS_CACHE = {}


def region_of(ap):
    t = ap.tensor
    nm = t.name
    rs = _RS_CACHE.get(nm)
    if rs is None:
        rs = _prod(list(t.shape)[1:])
        _RS_CACHE[nm] = rs
    off = int(ap.offset)
    r0, c0 = divmod(off, rs)
    r1, c1 = r0, c0
    ne = 1
    for step, cnt in ap.ap:
        step = int(step)
        cnt = int(cnt)
        if cnt <= 1 or step == 0:
            continue
        ne *= cnt
        a, b = divmod(step, rs)
        r1 += a * (cnt - 1)
        c1 += b * (cnt - 1)
    if c1 >= rs:
        r1 += c1 // rs
        c0, c1 = 0, rs - 1
    dense = ne >= (r1 + 1 - r0) * (c1 + 1 - c0)
    return (nm, r0, r1 + 1, c0, c1 + 1, dense)


def _ovl(a, b):
    return a[1] < b[2] and b[1] < a[2] and a[3] < b[4] and b[3] < a[4]


def _cov(a, b):
    return a[5] and a[1] <= b[1] and a[2] >= b[2] and a[3] <= b[3] and a[4] >= b[4]


class Op:
    __slots__ = ("eng", "fn", "idx", "cdeps", "ddeps", "dma", "inc", "semval", "ownwait")

    def __init__(self, eng, fn, idx):
        self.eng = eng
        self.fn = fn
        self.idx = idx
        self.cdeps = {}
        self.ddeps = {}
        self.dma = None
        self.inc = False
        self.semval = 0
        self.ownwait = None


class Sched:
    def __init__(self, nc):
        self.nc = nc
        self.ops = {e: [] for e in ENGS}
        self.order = []
        self.bufs = {}
        self.dma_use = {e: [0] * NDMASEM for e in ENGS}
        self.dma_rr = {e: 0 for e in ENGS}

    def _adddep(self, op, prod, kind):
        if prod[0] == "c":
            _, e, i = prod
            if e == op.eng:
                if kind != "raw":
                    return
                if e == "pe":
                    return
            if op.cdeps.get(e, -1) < i:
                op.cdeps[e] = i
        else:
            _, q, si, val = prod
            k = (q, si)
            if op.ddeps.get(k, 0) < val:
                op.ddeps[k] = val

    def add(self, eng, fn, reads=(), writes=(), dma=False):
        op = Op(eng, fn, len(self.ops[eng]))
        if dma:
            si = self.dma_rr[eng]
            self.dma_rr[eng] = (si + 1) % NDMASEM
            prev = self.dma_use[eng][si]
            self.dma_use[eng][si] = prev + 16
            op.dma = (eng, si, prev + 16)
            if prev:
                op.ownwait = (eng, si, prev)
            me = ("d", eng, si, prev + 16)
        else:
            me = ("c", eng, op.idx)
        rregs = [region_of(a) for a in reads]
        wregs = [region_of(a) for a in writes]
        for rg in rregs:
            b = self.bufs.setdefault(rg[0], {"w": [], "r": []})
            for (wr, prod) in b["w"]:
                if _ovl(wr, rg):
                    self._adddep(op, prod, "raw")
        for rg in wregs:
            b = self.bufs.setdefault(rg[0], {"w": [], "r": []})
            for (wr, prod) in b["w"]:
                if _ovl(wr, rg):
                    self._adddep(op, prod, "waw")
            for (rr, cons) in b["r"]:
                if _ovl(rr, rg):
                    self._adddep(op, cons, "war")
        for rg in rregs:
            b = self.bufs[rg[0]]
            if not dma:
                b["r"] = [(rr, c) for (rr, c) in b["r"]
                          if not (c[0] == "c" and c[1] == eng and _cov(rg, rr))]
            b["r"].append((rg, me))
        for rg in wregs:
            b = self.bufs[rg[0]]
            b["w"] = [(wr, p) for (wr, p) in b["w"] if not _cov(rg, wr)]
            b["r"] = [(rr, c) for (rr, c) in b["r"] if not _cov(rg, rr)]
            b["w"].append((rg, me))
        self.ops[eng].append(op)
        self.order.append(op)
        return op

    def dma(self, q, out, in_, **kw):
        return self.add(q, lambda e: e.dma_start(out=out, in_=in_, **kw),
                        reads=[in_], writes=[out], dma=True)

    def emit(self):
        nc = self.nc
        seen_c = {e: {f: -1 for f in ENGS} for e in ENGS}
        seen_d = {e: {} for e in ENGS}
        waits = {}
        for op in self.order:
            w = []
            e = op.eng
            for f, i in op.cdeps.items():
                if seen_c[e][f] < i:
                    seen_c[e][f] = i
                    w.append(("c", f, i))
                    self.ops[f][i].inc = True
            dd = dict(op.ddeps)
            if op.ownwait is not None:
                q, si, val = op.ownwait
                if dd.get((q, si), 0) < val:
                    dd[(q, si)] = val
            for (q, si), val in dd.items():
                if seen_d[e].get((q, si), 0) < val:
                    seen_d[e][(q, si)] = val
                    w.append(("d", q, si, val))
            waits[id(op)] = w
        for e in ENGS:
            n = 0
            for op in self.ops[e]:
                if op.inc:
                    n += 1
                op.semval = n
        self.stats = {e: (len(self.ops[e]), sum(1 for o in self.ops[e] if o.inc)) for e in ENGS}
        with contextlib.ExitStack() as st:
            csem = {e: st.enter_context(nc.semaphore("cs_" + e)) for e in ENGS}
            dsem = {e: [st.enter_context(nc.semaphore("ds_%s%d" % (e, i))) for i in range(NDMASEM)]
                    for e in ENGS if any(o.dma for o in self.ops[e])}
            block = st.enter_context(nc.Block())

            def run(e):
                def body(eng):
                    for op in self.ops[e]:
                        for w in waits[id(op)]:
                            if w[0] == "c":
                                eng.wait_ge(csem[w[1]], self.ops[w[1]][w[2]].semval)
                            else:
                                eng.wait_ge(dsem[w[1]][w[2]], w[3])
                        ins = op.fn(eng)
                        if op.dma is not None:
                            ins.then_inc(dsem[op.dma[0]][op.dma[1]], 16)
                        elif op.inc:
                            ins.then_inc(csem[e], 1)
                    if e in dsem:
                        for si in range(NDMASEM):
                            v = self.dma_use[e][si]
                            if v and seen_d[e].get((e, si), 0) < v:
                                eng.wait_ge(dsem[e][si], v)
                return body

            block.tensor(run("pe"))
            block.vector(run("dve"))
            block.scalar(run("act"))
            block.gpsimd(run("pool"))
            block.sync(run("sp"))


D = 1024
SEQ = 2048
CTX = 256
DEPTH = 4
NB = 4
TT = NB * (SEQ + CTX)
NBLK = TT // 512
D_IN = 5024
EPS = 1e-6
MLA_SCALE = 96 ** -0.5
DF_SCALE = 32 ** -0.5
NA = 2496
C_KRP, C_DQP, C_DKP = 1952, 1984, 2240
WEIGHT_NAMES = ["norm_mix_g", "norm_ffn_g", "w_mod", "b_mod", "w_in", "mla_q_norm_g", "mla_w_uq",
                "mla_kv_norm_g", "mla_w_ukv", "hy_conv_w", "hy_conv_b", "hy_w1", "hy_b1", "hy_freq",
                "hy_w2", "hy_b2", "hy_w3", "hy_b3", "hy_skip", "df_lq1", "df_lk1", "df_lq2", "df_lk2",
                "df_subln_g", "w_br_a", "w_br_b", "w_br_c", "w_out", "w_fc1", "w_fc2", "final_norm_g",
                "c_ctx"]
WEIGHT_SHAPES = {
    "norm_mix_g": [4, 1024], "norm_ffn_g": [4, 1024], "w_mod": [4, 1024, 6144], "b_mod": [4, 6144],
    "w_in": [4, 1024, 5024], "mla_q_norm_g": [4, 256], "mla_w_uq": [4, 256, 768],
    "mla_kv_norm_g": [4, 128], "mla_w_ukv": [4, 128, 1024], "hy_conv_w": [4, 3, 768],
    "hy_conv_b": [4, 768], "hy_w1": [4, 33, 64], "hy_b1": [4, 64], "hy_freq": [4, 64],
    "hy_w2": [4, 64, 64], "hy_b2": [4, 64], "hy_w3": [4, 64, 512], "hy_b3": [4, 512],
    "hy_skip": [4, 256], "df_lq1": [4, 32], "df_lk1": [4, 32], "df_lq2": [4, 32], "df_lk2": [4, 32],
    "df_subln_g": [4, 64], "w_br_a": [4, 512, 1024], "w_br_b": [4, 256, 1024], "w_br_c": [4, 256, 1024],
    "w_out": [4, 1024, 1024], "w_fc1": [4, 1024, 4096], "w_fc2": [4, 4096, 1024], "final_norm_g": [1024],
    "c_ctx": [1024],
}


def lam_init_of(l):
    return 0.8 - 0.6 * math.exp(-0.3 * l)


def host_constants():
    f32 = np.float32
    cs = {}
    L = SEQ
    t = np.arange(L)
    row = (t // 64).astype(f32)
    col = (t % 64).astype(f32)
    inv = (10000.0 ** (-np.arange(8, dtype=f32) / 8)).astype(f32)
    ang = np.concatenate([row[:, None] * inv, col[:, None] * inv], axis=-1).astype(f32)
    cosT = np.cos(ang).astype(f32).T
    sinT = np.sin(ang).astype(f32).T
    rc = np.zeros((128, L), f32)
    rsn = np.zeros((128, L), f32)
    for p in range(128):
        q = p % 32
        i = q % 16
        rc[p] = cosT[i]
        rsn[p] = -sinT[i] if q < 16 else sinT[i]
    cs["rope_c"] = rc
    cs["rope_s"] = rsn
    for tag, L in (("l", SEQ), ("c", CTX)):
        tt_ = np.linspace(0.0, 1.0, L, dtype=f32)[:, None]
        w = ((2.0 * math.pi / L) * np.arange(L, dtype=f32))[:, None].astype(f32)
        bands = np.linspace(1e-4, 15.0, 16, dtype=f32)[None]
        z = np.concatenate([tt_, np.cos(bands * w), -np.sin(bands * w)], axis=-1).astype(f32)
        cs["zT_" + tag] = np.ascontiguousarray(z.T)
        deltas = np.linspace(math.log(1e-2) / 1.5, math.log(1e-2) / 0.3, 256, dtype=f32)
        win = (np.exp(-tt_ * np.abs(deltas)[None]) + 0.05).astype(f32)
        ntt = L // 128
        cs["win_" + tag] = np.ascontiguousarray(win.reshape(ntt, 128, 256).transpose(1, 0, 2))
        N = 2 * L
        ti = np.arange(L, dtype=np.float64)[:, None]
        fi = np.arange(L, dtype=np.float64)[None, :]
        th = math.pi * (2 * fi + 1) / N
        Cm = np.cos(th * ti)
        Sm = np.sin(th * ti)
        nfc = L // 128
        M = np.stack([Cm, Sm], 0)
        FW = M.reshape(2, ntt, 128, nfc, 128).transpose(3, 2, 0, 1, 4)
        cs["FW_" + tag] = np.ascontiguousarray(FW).astype(ml_dtypes.bfloat16).reshape(nfc, 128, 2 * ntt * 128)
        tbw = 512 if L >= 512 else L
        ntb = L // tbw
        GI = M.reshape(2, ntb, tbw, nfc, 128).transpose(1, 4, 3, 0, 2)
        cs["GI_" + tag] = np.ascontiguousarray(GI).astype(ml_dtypes.bfloat16).reshape(ntb, 128, nfc * 2 * tbw)
    return cs


CONST_SPECS = {
    "rope_c": ([128, 2048], F32), "rope_s": ([128, 2048], F32),
    "zT_l": ([33, 2048], F32), "zT_c": ([33, 256], F32),
    "win_l": ([128, 16, 256], F32), "win_c": ([128, 2, 256], F32),
    "FW_l": ([16, 128, 4096], BF16), "FW_c": ([2, 128, 512], BF16),
    "GI_l": ([4, 128, 16384], BF16), "GI_c": ([1, 128, 1024], BF16),
}


class Arena:
    def __init__(self, handle, n):
        self.h = handle
        self.n = n
        self.top = 0

    def alloc(self, n, shape=None):
        n = (n + 15) // 16 * 16
        assert self.top + n <= self.n, ("arena overflow", self.h.name, self.top, n, self.n)
        v = self.h[:, self.top:self.top + n]
        self.top += n
        return v

    def a3(self, a, b):
        v = self.alloc(a * b)
        return v.rearrange("p (a b) -> p a b", a=a)


def build_program(n_layers=DEPTH, dbg=None):
    nc = bass.Bass("TRN2", target_bir_lowering=False)
    S = Sched(nc)
    dbg = dbg or []
    dr = {}
    dr["x"] = nc.dram_tensor("x", [NB, SEQ, D], F32, kind="ExternalInput")
    dr["c"] = nc.dram_tensor("c", [NB, D], F32, kind="ExternalInput")
    dr["ctx"] = nc.dram_tensor("ctx", [NB, CTX, D], F32, kind="ExternalInput")
    for nm in WEIGHT_NAMES:
        dr[nm] = nc.dram_tensor(nm, WEIGHT_SHAPES[nm], F32, kind="ExternalInput")
    for nm, (shp, dt) in CONST_SPECS.items():
        dr[nm] = nc.dram_tensor(nm, shp, dt, kind="ExternalInput")
    out = nc.dram_tensor("out", [NB, SEQ, D], F32, kind="ExternalOutput")

    def scratch(nm, shape, dt):
        kind = "ExternalOutput" if nm in dbg else "Internal"
        dr[nm] = nc.dram_tensor(nm, shape, dt, kind=kind)
        return dr[nm]

    xT = scratch("xT", [D, TT], F32)
    hT = scratch("hT", [D, TT], BF16)
    h2T = scratch("h2T", [D, TT], BF16)
    QnT = scratch("QnT", [512, TT], BF16)
    QrT = scratch("QrT", [256, TT], BF16)
    KnT = scratch("KnT", [512, TT], BF16)
    KrT = scratch("KrT", [32, TT], BF16)
    Vm = scratch("Vm", [TT, 512], BF16)
    phyT = scratch("phyT", [768, TT], BF16)
    DqT = scratch("DqT", [256, TT], BF16)
    DkT = scratch("DkT", [256, TT], BF16)
    Dv = scratch("Dv", [TT, 256], BF16)
    hx0T = scratch("hx0T", [256, TT], BF16)
    huT = scratch("huT", [256, TT], BF16)
    yaT = scratch("yaT", [512, TT], BF16)
    ybT = scratch("ybT", [256, TT], BF16)
    ycT = scratch("ycT", [256, TT], BF16)

    def fm(t):
        return t.rearrange("(c p) t -> p c t", p=128)

    st = contextlib.ExitStack()
    with st:
        AB_N = 58 * 1024
        AF_N = 18 * 1024
        abh = st.enter_context(nc.sbuf_tensor("arena_bf", [128, AB_N], BF16))
        afh = st.enter_context(nc.sbuf_tensor("arena_f", [128, AF_N], F32))
        AB = Arena(abh, AB_N)
        AFa = Arena(afh, AF_N)
        banks = [st.enter_context(nc.psum_tensor("bank%d" % i, [128, 512], F32)) for i in range(7)]
        bankT = st.enter_context(nc.psum_tensor("bankT", [128, 1024], BF16))
        rr = {"i": 0}

        def bank(lo=0, hi=7):
            n = hi - lo
            rr["i"] = (rr["i"] + 1) % n
            return banks[lo + rr["i"]][:, :]

        def mm(o, lhsT, rhs, start=True, stop=True):
            S.add("pe", lambda e: e.matmul(o, lhsT=lhsT, rhs=rhs, start=start, stop=stop),
                  reads=[lhsT, rhs], writes=[o])

        def tr(o, in_, ident_ap):
            S.add("pe", lambda e: e.transpose(o, in_, ident_ap), reads=[in_, ident_ap], writes=[o])

        def act(o, in_, func, scale=1.0, bias=0.0):
            rd = [in_]
            if not isinstance(scale, (int, float)):
                rd.append(scale)
            if not isinstance(bias, (int, float)):
                rd.append(bias)
            S.add("act", lambda e: e.activation(out=o, in_=in_, func=func, bias=bias, scale=scale),
                  reads=rd, writes=[o])

        def cp(eng, o, in_):
            if eng == "act":
                S.add("act", lambda e: e.copy(out=o, in_=in_), reads=[in_], writes=[o])
            else:
                S.add(eng, lambda e: e.tensor_copy(out=o, in_=in_), reads=[in_], writes=[o])

        def tt(eng, o, a, b, op):
            S.add(eng, lambda e: e.tensor_tensor(out=o, in0=a, in1=b, op=op), reads=[a, b], writes=[o])

        def ts(eng, o, a, s1, s2, op0, op1=None):
            rd = [a]
            if not isinstance(s1, (int, float)):
                rd.append(s1)
            if s2 is not None and not isinstance(s2, (int, float)):
                rd.append(s2)
            if op1 is None:
                S.add(eng, lambda e: e.tensor_scalar(out=o, in0=a, scalar1=s1, scalar2=None, op0=op0),
                      reads=rd, writes=[o])
            else:
                S.add(eng, lambda e: e.tensor_scalar(out=o, in0=a, scalar1=s1, scalar2=s2, op0=op0, op1=op1),
                      reads=rd, writes=[o])

        def stt(eng, o, a, s, b, op0, op1):
            rd = [a, b]
            if not isinstance(s, (int, float)):
                rd.append(s)
            S.add(eng, lambda e: e.scalar_tensor_tensor(out=o, in0=a, scalar=s, in1=b, op0=op0, op1=op1),
                  reads=rd, writes=[o])

        def recip(o, in_):
            S.add("dve", lambda e: e.reciprocal(out=o, in_=in_), reads=[in_], writes=[o])

        def memset(eng, o, v):
            S.add(eng, lambda e: e.memset(o, v), writes=[o])

        evq = {"i": 0}

        def evac(o, in_):
            evq["i"] ^= 1
            cp("dve" if evq["i"] else "act", o, in_)

        def rstd_from(ps_ap, o, n):
            act(o, ps_ap, AF.Sqrt, scale=1.0 / n, bias=EPS)
            recip(o, o)

        ident = AFa.alloc(128)
        memset("pool", ident, 0.0)
        S.add("pool", lambda e: e.affine_select(out=ident, in_=ident, pattern=[[-1, 128]],
                                                compare_op=ALU.not_equal, fill=1.0, base=0,
                                                channel_multiplier=1),
              reads=[ident], writes=[ident])
        identb = AB.alloc(128)
        cp("dve", identb, ident)
        onesb = AB.alloc(128)
        memset("pool", onesb, 1.0)
        onesf = AFa.alloc(128)
        memset("pool", onesf, 1.0)
        mask0 = AFa.alloc(16)[:, 0:1]
        memset("pool", mask0, 1.0)
        memset("pool", mask0[0:1, :], 0.0)
        rope_c = AFa.alloc(2048)
        rope_s = AFa.alloc(2048)
        S.dma("sp", rope_c, dr["rope_c"][:, :])
        S.dma("sp", rope_s, dr["rope_s"][:, :])

        VROW = {}
        rows = []

        def vadd(key, src_ap, n):
            VROW[key] = len(rows)
            for j in range(n):
                rows.append((src_ap, j))

        for l in range(DEPTH):
            vadd(("mixg", l), dr["norm_mix_g"][l].rearrange("(j p) -> j p", p=128), 8)
            vadd(("ffng", l), dr["norm_ffn_g"][l].rearrange("(j p) -> j p", p=128), 8)
            vadd(("bmod", l), dr["b_mod"][l].rearrange("(j p) -> j p", p=128), 48)
            vadd(("qg", l), dr["mla_q_norm_g"][l].rearrange("(j p) -> j p", p=128), 2)
            vadd(("kvg", l), dr["mla_kv_norm_g"][l].rearrange("(j p) -> j p", p=128), 1)
            vadd(("cw", l), dr["hy_conv_w"][l].rearrange("t (j p) -> (t j) p", p=128), 18)
            vadd(("cb", l), dr["hy_conv_b"][l].rearrange("(j p) -> j p", p=128), 6)
            vadd(("skip", l), dr["hy_skip"][l].rearrange("(j p) -> j p", p=128), 2)
        vadd(("fing",), dr["final_norm_g"].rearrange("(j p) -> j p", p=128), 8)
        NV = len(rows)
        vecT = AFa.alloc(NV)
        mark_f = AFa.top
        vrows = AFa.alloc(128)
        g0 = 0
        while g0 < NV:
            n = min(128, NV - g0)
            i = g0
            while i < g0 + n:
                src, j = rows[i]
                k = i
                while k + 1 < g0 + n and rows[k + 1][0] is src and rows[k + 1][1] == rows[k][1] + 1:
                    k += 1
                cnt = k - i + 1
                S.dma("sp", vrows[i - g0:i - g0 + cnt, :], src[j:j + cnt, :])
                i = k + 1
            pb = bank()
            tr(pb[:, 0:n], vrows[0:n, :], ident[0:n, 0:n])
            cp("dve", vecT[:, g0:g0 + n], pb[:, 0:n])
            g0 += n
        AFa.top = mark_f

        def vcol(key, j=0):
            c = VROW[key] + j
            return vecT[:, c:c + 1]

        smallv = AFa.alloc(DEPTH * 8)
        SV = {}
        for l in range(DEPTH):
            for i, nm in enumerate(["hy_b1", "hy_freq", "hy_b2", "df_subln_g"]):
                colv = smallv[0:64, l * 8 + i:l * 8 + i + 1]
                S.dma("sp", colv, dr[nm][l].rearrange("(p o) -> p o", o=1))
                SV[(nm, l)] = colv
            for i, (a, b) in enumerate([("hy_freq", "hy_b1"), ("hy_freq", "hy_b2")]):
                colv = smallv[0:64, l * 8 + 4 + i:l * 8 + 5 + i]
                tt("dve", colv, SV[(a, l)], SV[(b, l)], ALU.mult)
                SV[("fb%d" % (i + 1), l)] = colv
            colv = smallv[0:64, l * 8 + 6:l * 8 + 7]
            ts("dve", colv, SV[("df_subln_g", l)], 1.0 - lam_init_of(l), None, ALU.mult)
            SV[("sublng", l)] = colv

        neglamT = AFa.alloc(16)
        mark_f = AFa.top
        lamrow = AFa.alloc(1024)
        for i, nm in enumerate(["df_lq1", "df_lk1", "df_lq2", "df_lk2"]):
            S.dma("sp", lamrow[0:1, i * 128:(i + 1) * 128], dr[nm].rearrange("(o l) d -> o (l d)", o=1))
        tt("dve", lamrow[0:1, 512:640], lamrow[0:1, 0:128], lamrow[0:1, 128:256], ALU.mult)
        tt("dve", lamrow[0:1, 640:768], lamrow[0:1, 256:384], lamrow[0:1, 384:512], ALU.mult)
        for i in range(2):
            src = lamrow[0:1, 512 + i * 128:640 + i * 128].rearrange("p (l d) -> p l d", l=4)
            dst = lamrow[0:1, 768 + i * 4:772 + i * 4]
            S.add("dve", (lambda s_, d_: (lambda e: e.reduce_sum(out=d_, in_=s_, axis=AX.X)))(src, dst),
                  reads=[src], writes=[dst])
        act(lamrow[0:1, 768:776], lamrow[0:1, 768:776], AF.Exp)
        tt("dve", lamrow[0:1, 776:780], lamrow[0:1, 772:776], lamrow[0:1, 768:772], ALU.subtract)
        for l in range(DEPTH):
            ts("dve", lamrow[0:1, 780 + l:781 + l], lamrow[0:1, 776 + l:777 + l], -lam_init_of(l), None, ALU.add)
        pb = bank()
        mm(pb[0:64, 0:4], onesf[0:1, 0:64], lamrow[0:1, 780:784])
        cp("dve", neglamT[0:64, 0:4], pb[0:64, 0:4])
        AFa.top = mark_f

        modT = AFa.alloc(DEPTH * 48 * 5).rearrange("p (l j c) -> p l j c", l=DEPTH, j=48)
        A1 = AFa.alloc(DEPTH * 8 * 5).rearrange("p (l j c) -> p l j c", l=DEPTH, j=8)
        A2 = AFa.alloc(DEPTH * 8 * 5).rearrange("p (l j c) -> p l j c", l=DEPTH, j=8)
        mark_f = AFa.top
        mark_b = AB.top
        cs_rows = AFa.alloc(1024)
        S.dma("sp", cs_rows[0:4, :], dr["c"][:, :])
        S.dma("sp", cs_rows[4:5, :], dr["c_ctx"].rearrange("(o d) -> o d", o=1))
        act(cs_rows[0:5, :], cs_rows[0:5, :], AF.Silu)
        sT = AFa.alloc(48).rearrange("p (k c) -> p k c", k=8)
        for k in range(8):
            pb = bank()
            tr(pb[:, 0:5], cs_rows[0:5, k * 128:(k + 1) * 128], ident[0:5, 0:5])
            cp("dve", sT[:, k, 0:5], pb[:, 0:5])
        wmbuf = [AFa.alloc(4096).rearrange("p (k c) -> p k c", k=8) for _ in range(2)]
        it = 0
        for l in range(n_layers):
            wm_v = dr["w_mod"][l].rearrange("(k p) c -> p k c", p=128)
            for half in range(12):
                wb = wmbuf[it % 2]
                it += 1
                S.dma("sp", wb, wm_v[:, :, half * 512:(half + 1) * 512])
                for jj in range(4):
                    j = half * 4 + jj
                    pb = bank()
                    for k in range(8):
                        mm(pb[:, 0:5], wb[:, k, jj * 128:(jj + 1) * 128], sT[:, k, 0:5], start=(k == 0), stop=(k == 7))
                    ts("dve", modT[:, l, j, :], pb[:, 0:5], vcol(("bmod", l), j), None, ALU.add)
            for j in range(8):
                ts("dve", A1[:, l, j, :], modT[:, l, 8 + j, :], 1.0, vcol(("mixg", l), j), ALU.add, ALU.mult)
                ts("dve", A2[:, l, j, :], modT[:, l, 32 + j, :], 1.0, vcol(("ffng", l), j), ALU.add, ALU.mult)
        AFa.top = mark_f
        AB.top = mark_b

        mark_f = AFa.top
        xin = [AFa.alloc(1024) for _ in range(2)]
        xst = [AFa.a3(8, 512) for _ in range(1)]
        xT_v = fm(xT)
        ti = 0
        for blk in range(NBLK):
            stg = xst[0]
            for t4 in range(4):
                tok0 = blk * 512 + t4 * 128
                if tok0 < NB * CTX:
                    b, r0 = divmod(tok0, CTX)
                    src = dr["ctx"][b, r0:r0 + 128, :]
                else:
                    b, r0 = divmod(tok0 - NB * CTX, SEQ)
                    src = dr["x"][b, r0:r0 + 128, :]
                xi = xin[ti % 2]
                ti += 1
                S.dma("sp", xi, src)
                for half in range(2):
                    pb = bank()
                    for q in range(4):
                        j = half * 4 + q
                        tr(pb[:, q * 128:(q + 1) * 128], xi[:, j * 128:(j + 1) * 128], ident)
                    evac(stg[:, half * 4:half * 4 + 4, t4 * 128:(t4 + 1) * 128],
                         pb[:, :].rearrange("p (q t) -> p q t", q=4))
            S.dma("pool", xT_v[:, :, blk * 512:(blk + 1) * 512], stg)
        AFa.top = mark_f

        def blk_info(blk):
            if blk < 2:
                return 4, False, 0
            b = (blk - 2) // 4
            return b, True, ((blk - 2) % 4) * 512

        def keycols(b):
            return [(b * CTX, CTX), (NB * CTX + b * SEQ, SEQ)]

        for l in range(n_layers):
            with_ctx = l < DEPTH - 1
            mark_f = AFa.top
            mark_b = AB.top
            winA = AB.a3(8, NA)
            win_v = dr["w_in"][l].rearrange("(k p) c -> p k c", p=128)
            for k in range(8):
                S.dma("pool", winA[:, k, 0:1952], win_v[:, k, 0:1952])
            for k in range(8):
                cp("pool", winA[:, k, C_KRP:C_KRP + 16], winA[:, k, 400:416])
                cp("pool", winA[:, k, C_KRP + 16:C_KRP + 32], winA[:, k, 384:400])
                for (src0, dst0) in ((1184, C_DQP), (1440, C_DKP)):
                    sv = winA[:, k, src0:src0 + 256].rearrange("p (s h i) -> p s h i", s=8, h=2)
                    dv_ = winA[:, k, dst0:dst0 + 256].rearrange("p (s h i) -> p s h i", s=8, h=2)
                    cp("pool", dv_[:, :, 0, :], sv[:, :, 1, :])
                    cp("pool", dv_[:, :, 1, :], sv[:, :, 0, :])
            wuq = AB.a3(2, 768)
            S.dma("pool", wuq, dr["mla_w_uq"][l].rearrange("(k p) c -> p k c", p=128))
            wuq_n = AB.a3(2, 512)
            wuq_r = AB.a3(2, 256)
            wuq_p = AB.a3(2, 256)
            for k in range(2):
                sv = wuq[:, k, :].rearrange("p (h d) -> p h d", h=8)
                cp("pool", wuq_n[:, k, :].rearrange("p (h d) -> p h d", h=8), sv[:, :, 0:64])
                cp("pool", wuq_r[:, k, :].rearrange("p (h d) -> p h d", h=8), sv[:, :, 64:96])
                pv = wuq_p[:, k, :].rearrange("p (h d) -> p h d", h=8)
                cp("pool", pv[:, :, 0:16], sv[:, :, 80:96])
                cp("pool", pv[:, :, 16:32], sv[:, :, 64:80])
            wukv = AB.alloc(1024)
            S.dma("pool", wukv, dr["mla_w_ukv"][l])
            wkn = AB.alloc(512)
            wv = AB.alloc(512)
            sv = wukv.rearrange("p (h d) -> p h d", h=8)
            cp("pool", wkn.rearrange("p (h d) -> p h d", h=8), sv[:, :, 0:64])
            cp("pool", wv.rearrange("p (h d) -> p h d", h=8), sv[:, :, 64:128])

            xb_ = [AFa.a3(8, 512) for _ in range(2)]
            rstd_ = [AFa.alloc(512) for _ in range(2)]
            tmpf = [AFa.alloc(512) for _ in range(4)]
            sqb = AB.a3(8, 512)
            hTb = [AB.a3(8, 512) for _ in range(2)]
            qn = AB.a3(2, 512)
            kvn = AB.alloc(512)
            stg_q = AB.a3(4, 512)
            stg_k = AB.a3(4, 512)
            stg_r = AB.a3(2, 512)
            stg_kr = AB.alloc(512)
            stg_v = AB.a3(4, 512)
            stg_hy = AB.a3(6, 512)
            stg_d = AB.a3(4, 512)
            stg_dv = AB.a3(4, 256)
            tq = {"i": 0}

            def tmp():
                tq["i"] = (tq["i"] + 1) % 4
                return tmpf[tq["i"]]

            def rope_out(o, P, Pp, rows, rt0):
                t1 = tmp()
                t2 = tmp()
                tt("dve", t1[0:rows, :], P, rope_c[0:rows, rt0:rt0 + 512], ALU.mult)
                tt("dve", t2[0:rows, :], Pp, rope_s[0:rows, rt0:rt0 + 512], ALU.mult)
                tt("pool", o, t1[0:rows, :], t2[0:rows, :], ALU.add)

            sqq = AB.a3(3, 512)

            def norm_stage(blk):
                mc, islat, rt0 = blk_info(blk)
                c0 = blk * 512
                xb = xb_[blk % 2]
                hb = hTb[blk % 2]
                rs_ = rstd_[blk % 2]
                S.dma("sp", xb, xT_v[:, :, c0:c0 + 512])
                for j in range(8):
                    act(sqb[:, j, :], xb[:, j, :], AF.Square)
                pss = bank()
                for j in range(8):
                    mm(pss, onesb, sqb[:, j, :], start=(j == 0), stop=(j == 7))
                rstd_from(pss, rs_, 1024.0)
                for j in range(8):
                    t1 = tmp()
                    tt("dve", t1, xb[:, j, :], rs_, ALU.mult)
                    ts("pool", hb[:, j, :], t1, A1[:, l, j, mc:mc + 1], modT[:, l, j, mc:mc + 1], ALU.mult, ALU.add)
                S.dma("pool", fm(hT)[:, :, c0:c0 + 512], hb)


            norm_stage(0)
            for blk in range(NBLK):
                mc, islat, rt0 = blk_info(blk)
                c0 = blk * 512
                hb = hTb[blk % 2]
                def proj(col0, M=128, pb=None):
                    pb = pb or bank()
                    for k in range(8):
                        mm(pb[0:M, :], winA[:, k, col0:col0 + M], hb[:, k, :], start=(k == 0), stop=(k == 7))
                    return pb

                pq = [proj(0), proj(128)]
                for i in range(2):
                    act(sqq[:, i, :], pq[i], AF.Square)
                pss = bank()
                for i in range(2):
                    mm(pss, onesb, sqq[:, i, :], start=(i == 0), stop=(i == 1))
                rq = tmp()
                rstd_from(pss, rq, 256.0)
                for i in range(2):
                    t1 = tmp()
                    tt("dve", t1, pq[i], rq, ALU.mult)
                    ts("pool", qn[:, i, :], t1, vcol(("qg", l), i), None, ALU.mult)
                pkv = proj(256)
                act(sqq[:, 2, :], pkv, AF.Square)
                pss = bank()
                mm(pss, onesb, sqq[:, 2, :])
                rk = tmp()
                rstd_from(pss, rk, 128.0)
                t1 = tmp()
                tt("dve", t1, pkv, rk, ALU.mult)
                ts("pool", kvn, t1, vcol(("kvg", l), 0), None, ALU.mult)
                if blk + 1 < NBLK:
                    norm_stage(blk + 1)
                pb = proj(384, M=32)
                if islat:
                    pb2 = proj(C_KRP, M=32)
                    rope_out(stg_kr[0:32, :], pb[0:32, :], pb2[0:32, :], 32, rt0)
                else:
                    evac(stg_kr[0:32, :], pb[0:32, :])
                S.dma("pool", KrT[:, c0:c0 + 512], stg_kr[0:32, :])
                for ch in range(6):
                    pb = proj(416 + ch * 128)
                    evac(stg_hy[:, ch, :], pb)
                S.dma("pool", fm(phyT)[:, :, c0:c0 + 512], stg_hy)
                for qi, (cbase, pbase, dst) in enumerate(((1184, C_DQP, DqT), (1440, C_DKP, DkT))):
                    for ch in range(2):
                        pb = proj(cbase + ch * 128)
                        if islat:
                            pb2 = proj(pbase + ch * 128)
                            rope_out(stg_d[:, qi * 2 + ch, :], pb, pb2, 128, rt0)
                        else:
                            evac(stg_d[:, qi * 2 + ch, :], pb)
                    S.dma("pool", fm(dst)[:, :, c0:c0 + 512], stg_d[:, qi * 2:qi * 2 + 2, :])
                for t4 in range(4):
                    pb = bank()
                    for k in range(8):
                        mm(pb[:, 0:256], hb[:, k, t4 * 128:(t4 + 1) * 128], winA[:, k, 1696:1952],
                           start=(k == 0), stop=(k == 7))
                    evac(stg_dv[:, t4, :], pb[:, 0:256])
                S.dma("pool", Dv.rearrange("(n p) c -> p n c", p=128)[:, blk * 4:(blk + 1) * 4, :], stg_dv)
                for ch in range(4):
                    pb = bank()
                    for k in range(2):
                        mm(pb, wuq_n[:, k, ch * 128:(ch + 1) * 128], qn[:, k, :], start=(k == 0), stop=(k == 1))
                    evac(stg_q[:, ch, :], pb)
                S.dma("pool", fm(QnT)[:, :, c0:c0 + 512], stg_q)
                for ch in range(2):
                    pb = bank()
                    for k in range(2):
                        mm(pb, wuq_r[:, k, ch * 128:(ch + 1) * 128], qn[:, k, :], start=(k == 0), stop=(k == 1))
                    if islat:
                        pb2 = bank()
                        for k in range(2):
                            mm(pb2, wuq_p[:, k, ch * 128:(ch + 1) * 128], qn[:, k, :], start=(k == 0), stop=(k == 1))
                        rope_out(stg_r[:, ch, :], pb, pb2, 128, rt0)
                    else:
                        evac(stg_r[:, ch, :], pb)
                S.dma("pool", fm(QrT)[:, :, c0:c0 + 512], stg_r)
                for ch in range(4):
                    pb = bank()
                    mm(pb, wkn[:, ch * 128:(ch + 1) * 128], kvn)
                    evac(stg_k[:, ch, :], pb)
                S.dma("pool", fm(KnT)[:, :, c0:c0 + 512], stg_k)
                for t4 in range(4):
                    pb = bank()
                    mm(pb, kvn[:, t4 * 128:(t4 + 1) * 128], wv)
                    evac(stg_v[:, t4, :], pb)
                S.dma("pool", Vm.rearrange("(n p) c -> p n c", p=128)[:, blk * 4:(blk + 1) * 4, :], stg_v)
            AFa.top = mark_f
            AB.top = mark_b

            def hyena(tag, L, col_of_b):
                mark_f0 = AFa.top
                mark_b0 = AB.top
                ntt = L // 128
                nfc = L // 128
                N = 2 * L
                tbw = 512 if L >= 512 else L
                ntb = L // tbw
                GB = 2
                Ec = AB.a3(ntt, 256)
                Es = AB.a3(ntt, 256)
                mark_f = AFa.top
                w1 = AFa.alloc(64)
                w2 = AFa.alloc(64)
                w3e = AFa.alloc(512)
                S.dma("sp", w1[0:33, 0:64], dr["hy_w1"][l])
                S.dma("sp", w2[0:64, 0:64], dr["hy_w2"][l])
                S.dma("sp", w3e[0:64, :], dr["hy_w3"][l])
                S.dma("sp", w3e[64:65, :], dr["hy_b3"][l].rearrange("(o d) -> o d", o=1))
                zTb = AFa.alloc(512)
                a1b = AFa.alloc(512)
                a2T = AFa.alloc(L)
                memset("pool", a2T[64:65, :], 1.0)
                tf = [AFa.alloc(512) for _ in range(2)]
                tfk = AFa.alloc(512)
                for tb in range(ntb):
                    cs_ = slice(tb * tbw, (tb + 1) * tbw)
                    S.dma("sp", zTb[0:33, 0:tbw], dr["zT_" + tag][:, cs_])
                    for (wm_, src, dst, fbk) in ((w1[0:33, 0:64], zTb[0:33, 0:tbw], a1b[0:64, 0:tbw], "fb1"),
                                                 (w2[0:64, 0:64], a1b[0:64, 0:tbw], a2T[0:64, cs_], "fb2")):
                        pb = bank()
                        mm(pb[0:64, 0:tbw], wm_, src)
                        t_ = tf[0][0:64, 0:tbw]
                        ts("dve", t_, pb[0:64, 0:tbw], SV[("hy_freq", l)], SV[(fbk, l)], ALU.mult, ALU.add)
                        ts("dve", t_, t_, 1.0 / (2.0 * math.pi), 16.0, ALU.mult, ALU.add)
                        ki = tf[1][0:64, 0:tbw].bitcast(mybir.dt.int32)
                        cp("dve", ki, t_)
                        kf = tfk[0:64, 0:tbw]
                        cp("dve", kf, ki)
                        tt("dve", t_, t_, kf, ALU.subtract)
                        ts("dve", kf, t_, 0.5, None, ALU.is_gt)
                        tt("dve", t_, t_, kf, ALU.subtract)
                        ts("dve", kf, t_, -0.5, None, ALU.is_lt)
                        tt("dve", t_, t_, kf, ALU.add)
                        act(dst, t_, AF.Sin, scale=2.0 * math.pi)
                win = AFa.a3(ntt, 256)
                S.dma("sp", win, dr["win_" + tag][:, :, :])
                for t_i in range(ntt):
                    pb = bank()
                    mm(pb, a2T[0:65, t_i * 128:(t_i + 1) * 128], w3e[0:65, :])
                    hf = tf[0][:, 0:256]
                    hbk = tf[1][:, 0:256]
                    tt("dve", hf, pb[:, 0:256], win[:, t_i, :], ALU.mult)
                    tt("dve", hbk, pb[:, 256:512], win[:, t_i, :], ALU.mult)
                    tt("pool", Es[:, t_i, :], hbk, hf, ALU.subtract)
                    if t_i == 0:
                        stt("dve", Ec[:, t_i, :], hbk, mask0, hf, ALU.mult, ALU.add)
                    else:
                        tt("pool", Ec[:, t_i, :], hbk, hf, ALU.add)
                AFa.top = mark_f
                cw = lambda tap, ch: vcol(("cw", l), tap * 6 + ch)
                Y = AB.alloc(nfc * 2 * GB * 256).rearrange("p (q b c) -> p q b c", q=nfc * 2, b=GB)
                tf = [AFa.alloc(512) for _ in range(2)]
                mark_f1 = AFa.top
                mark_b1 = AB.top
                for g in range(NB // GB):
                    AFa.top = mark_f1
                    AB.top = mark_b1
                    uTok = AB.alloc(ntt * GB * 256).rearrange("p (t b c) -> p t b c", t=ntt, b=GB)
                    pt = [AB.alloc(L) for _ in range(3)]
                    cf = [AFa.alloc(L) for _ in range(2)]
                    uTb = AB.alloc(L)
                    x0b = AB.alloc(L)

                    def conv(dst, src, ch):
                        ts("dve", dst, src, cw(1, ch), vcol(("cb", l), ch), ALU.mult, ALU.add)
                        stt("dve", dst[:, 1:L], src[:, 0:L - 1], cw(0, ch), dst[:, 1:L], ALU.mult, ALU.add)
                        stt("dve", dst[:, 0:L - 1], src[:, 1:L], cw(2, ch), dst[:, 0:L - 1], ALU.mult, ALU.add)

                    for bl in range(GB):
                        b = g * GB + bl
                        cb0 = col_of_b(b)
                        for i in range(2):
                            for q, ch in enumerate((i, 2 + i, 4 + i)):
                                S.dma("sp", pt[q], phyT[ch * 128:(ch + 1) * 128, cb0:cb0 + L])
                            conv(cf[0], pt[0], i)
                            cp("pool", x0b, cf[0])
                            S.dma("pool", hx0T[i * 128:(i + 1) * 128, cb0:cb0 + L], x0b)
                            conv(cf[0], pt[1], 2 + i)
                            conv(cf[1], pt[2], 4 + i)
                            tt("dve", uTb, cf[0], cf[1], ALU.mult)
                            S.dma("pool", huT[i * 128:(i + 1) * 128, cb0:cb0 + L], uTb)
                            for t0 in range(0, ntt, 4):
                                nt_ = min(4, ntt - t0)
                                for q in range(nt_):
                                    tr(bankT[:, q * 128:(q + 1) * 128], uTb[:, (t0 + q) * 128:(t0 + q + 1) * 128], identb)
                                evac(uTok[:, t0:t0 + nt_, bl, i * 128:(i + 1) * 128],
                                     bankT[:, 0:nt_ * 128].rearrange("p (q c) -> p q c", q=nt_))
                    FWb = [AB.alloc(2 * ntt * 128).rearrange("p (r t f) -> p r t f", r=2, t=ntt) for _ in range(2)]
                    Hb = [AFa.a3(2, 256) for _ in range(2)]
                    yt = [AFa.alloc(256) for _ in range(4)]
                    for fc in range(nfc):
                        Fw = FWb[fc % 2]
                        H = Hb[fc % 2]
                        S.dma("sp", Fw, dr["FW_" + tag][fc].rearrange("p (r t f) -> p r t f", r=2, t=ntt))
                        for ri, E in ((0, Ec), (1, Es)):
                            pb = bank()
                            for t_i in range(ntt):
                                mm(pb[:, 0:256], Fw[:, ri, t_i, :], E[:, t_i, :], start=(t_i == 0), stop=(t_i == ntt - 1))
                            evac(H[:, ri, :], pb[:, 0:256])
                        for bl in range(GB):
                            pr = bank()
                            pi = bank()
                            for ri, pb in ((0, pr), (1, pi)):
                                for t_i in range(ntt):
                                    mm(pb[:, 0:256], Fw[:, ri, t_i, :], uTok[:, t_i, bl, :],
                                       start=(t_i == 0), stop=(t_i == ntt - 1))
                            tt("dve", yt[0], pr[:, 0:256], H[:, 0, :], ALU.mult)
                            tt("dve", yt[1], pi[:, 0:256], H[:, 1, :], ALU.mult)
                            tt("pool", Y[:, fc * 2, bl, :], yt[0], yt[1], ALU.add)
                            tt("dve", yt[2], pi[:, 0:256], H[:, 0, :], ALU.mult)
                            tt("dve", yt[3], pr[:, 0:256], H[:, 1, :], ALU.mult)
                            tt("pool", Y[:, fc * 2 + 1, bl, :], yt[2], yt[3], ALU.subtract)
                    AFa.top = mark_f1
                    AB.top = mark_b1
                    GIb = AB.alloc(nfc * 2 * tbw).rearrange("p (q t) -> p q t", q=nfc * 2)
                    x0l = [AB.alloc(512) for _ in range(2)]
                    ul = [AB.alloc(512) for _ in range(2)]
                    ybs = [AB.alloc(512) for _ in range(2)]
                    it_ = 0
                    for tb in range(ntb):
                        S.dma("sp", GIb, dr["GI_" + tag][tb].rearrange("p (q t) -> p q t", q=nfc * 2))
                        for bl in range(GB):
                            b = g * GB + bl
                            cb0 = col_of_b(b) + tb * tbw
                            for cc in range(2):
                                x0_ = x0l[it_ % 2][:, 0:tbw]
                                u_ = ul[it_ % 2][:, 0:tbw]
                                yo = ybs[it_ % 2][:, 0:tbw]
                                it_ += 1
                                S.dma("sp", x0_, hx0T[cc * 128:(cc + 1) * 128, cb0:cb0 + tbw])
                                S.dma("sp", u_, huT[cc * 128:(cc + 1) * 128, cb0:cb0 + tbw])
                                pb = bank()
                                for q in range(nfc * 2):
                                    mm(pb[:, 0:tbw], Y[:, q, bl, cc * 128:(cc + 1) * 128], GIb[:, q, :],
                                       start=(q == 0), stop=(q == nfc * 2 - 1))
                                t1 = tf[0][:, 0:tbw]
                                t2 = tf[1][:, 0:tbw]
                                ts("pool", t1, u_, vcol(("skip", l), cc), None, ALU.mult)
                                stt("dve", t2, pb[:, 0:tbw], 2.0 / N, t1, ALU.mult, ALU.add)
                                tt("dve", yo, t2, x0_, ALU.mult)
                                S.dma("pool", ybT[cc * 128:(cc + 1) * 128, cb0:cb0 + tbw], yo)
                AFa.top = mark_f0
                AB.top = mark_b0

            hyena("l", SEQ, lambda b: NB * CTX + b * SEQ)
            if with_ctx:
                hyena("c", CTX, lambda b: b * CTX)

            def attention(kind):
                mark_f = AFa.top
                mark_b = AB.top
                NK = SEQ + CTX
                nsub = 1 if kind == "mla" else 2
                kbuf = [[AB.alloc(NK) for _ in range(nsub)] for _ in range(2)]
                qbuf = [[AB.alloc(NK) for _ in range(nsub)] for _ in range(2)]
                krb = [AB.alloc(NK) for _ in range(2)] if kind == "mla" else None
                qrb = [AB.alloc(NK) for _ in range(2)] if kind == "mla" else None
                vaug = [AB.alloc(18 * 128).rearrange("p (t c) -> p t c", t=18) for _ in range(2)]
                for v_ in vaug:
                    memset("pool", v_[:, :, 64:128], 1.0)
                pT = [AB.alloc(512) for _ in range(4)]
                ost = [AB.alloc(512) for _ in range(2)]
                rrb = [AFa.alloc(512) for _ in range(2)]
                onf = [AFa.alloc(512) for _ in range(2)]
                of_ = AFa.alloc(512)
                sqd = AB.alloc(512)
                scale = MLA_SCALE if kind == "mla" else DF_SCALE
                nh = 8 if kind == "mla" else 4
                cnt = 0
                pi_ = 0
                sidx = [0]
                pend = deque()
                LAG = 2

                def flush(n):
                    while len(pend) > n:
                        pend.popleft()()

                def make_pv(acc, Va, kt, p_, qw, nkt):
                    return lambda: mm(acc[:, 0:qw], Va[:, kt, :], p_[:, 0:qw], start=(kt == 0), stop=(kt == nkt - 1))

                def make_epi(accs, h, gcol, qw, ei):
                    def f():
                        os_ = ost[ei % 2]
                        if kind == "mla":
                            r_ = rrb[ei % 2]
                            recip(r_[0:64, 0:qw], accs[0][64:128, 0:qw])
                            tt("dve", os_[0:64, 0:qw], accs[0][0:64, 0:qw], r_[0:64, 0:qw], ALU.mult)
                            S.dma("pool", yaT[h * 64:(h + 1) * 64, gcol:gcol + qw], os_[0:64, 0:qw])
                        else:
                            for s_ in range(2):
                                r_ = rrb[s_]
                                recip(r_[0:64, 0:qw], accs[s_][64:128, 0:qw])
                                tt("dve", onf[s_][0:64, 0:qw], accs[s_][0:64, 0:qw], r_[0:64, 0:qw], ALU.mult)
                            o = of_[0:64, 0:qw]
                            stt("dve", o, onf[1][0:64, 0:qw], neglamT[0:64, l:l + 1], onf[0][0:64, 0:qw],
                                ALU.mult, ALU.add)
                            tt("pool", sqd[0:64, 0:qw], o, o, ALU.mult)
                            pn = banks[sidx[0] % 4]
                            sidx[0] += 1
                            mm(pn[0:64, 0:qw], onesb[0:64, 0:64], sqd[0:64, 0:qw])
                            rn = rrb[0][0:64, 0:qw]
                            rstd_from(pn[0:64, 0:qw], rn, 64.0)
                            tt("dve", o, o, rn, ALU.mult)
                            ts("pool", os_[0:64, 0:qw], o, SV[("sublng", l)], None, ALU.mult)
                            S.dma("pool", ycT[h * 64:(h + 1) * 64, gcol:gcol + qw], os_[0:64, 0:qw])
                    return f

                for b in range(NB):
                    kc = keycols(b)
                    if kind == "mla":
                        Kr = krb[b % 2]
                        o_ = 0
                        for (cc0, n_) in kc:
                            S.dma("sp", Kr[0:32, o_:o_ + n_], KrT[:, cc0:cc0 + n_])
                            o_ += n_
                    for h in range(nh):
                        par = cnt % 2
                        cnt += 1
                        Ks = kbuf[par]
                        Qs = qbuf[par]
                        Va = vaug[par]
                        o_ = 0
                        for (cc0, n_) in kc:
                            if kind == "mla":
                                S.dma("sp", Ks[0][0:64, o_:o_ + n_], KnT[h * 64:(h + 1) * 64, cc0:cc0 + n_])
                                S.dma("sp", Qs[0][0:64, o_:o_ + n_], QnT[h * 64:(h + 1) * 64, cc0:cc0 + n_])
                                S.dma("sp", qrb[par][0:32, o_:o_ + n_], QrT[h * 32:(h + 1) * 32, cc0:cc0 + n_])
                                vsrc = Vm[cc0:cc0 + n_, h * 64:(h + 1) * 64]
                            else:
                                for s_ in range(2):
                                    r0 = (2 * h + s_) * 32
                                    S.dma("sp", Ks[s_][0:32, o_:o_ + n_], DkT[r0:r0 + 32, cc0:cc0 + n_])
                                    S.dma("sp", Qs[s_][0:32, o_:o_ + n_], DqT[r0:r0 + 32, cc0:cc0 + n_])
                                vsrc = Dv[cc0:cc0 + n_, h * 64:(h + 1) * 64]
                            S.dma("sp", Va[:, o_ // 128:(o_ + n_) // 128, 0:64],
                                  vsrc.rearrange("(t p) c -> p t c", p=128))
                            o_ += n_
                        qblocks = [(CTX + i * 512, 512, 18) for i in range(4)]
                        if with_ctx:
                            qblocks.append((0, CTX, 2))
                        for (q0, qw, nkt) in qblocks:
                            accs = []
                            for s_ in range(nsub):
                                acc = banks[4 + (pi_ + s_) % 3]
                                accs.append(acc)
                                for kt in range(nkt):
                                    ps = banks[sidx[0] % 4]
                                    p_ = pT[sidx[0] % 4]
                                    sidx[0] += 1
                                    if kind == "mla":
                                        mm(ps[:, 0:qw], Ks[0][0:64, kt * 128:(kt + 1) * 128], Qs[0][0:64, q0:q0 + qw],
                                           start=True, stop=False)
                                        mm(ps[:, 0:qw], Kr[0:32, kt * 128:(kt + 1) * 128], qrb[par][0:32, q0:q0 + qw],
                                           start=False, stop=True)
                                    else:
                                        mm(ps[:, 0:qw], Ks[s_][0:32, kt * 128:(kt + 1) * 128], Qs[s_][0:32, q0:q0 + qw])
                                    act(p_[:, 0:qw], ps[:, 0:qw], AF.Exp, scale=scale)
                                    pend.append(make_pv(acc, Va, kt, p_, qw, nkt))
                                    flush(LAG)
                            pi_ += nsub
                            if q0 >= CTX:
                                gcol = NB * CTX + b * SEQ + (q0 - CTX)
                            else:
                                gcol = b * CTX
                            pend.append(make_epi(accs, h, gcol, qw, pi_))
                flush(0)
                AFa.top = mark_f
                AB.top = mark_b

            attention("mla")
            attention("diff")

            mark_f = AFa.top
            mark_b = AB.top
            blks = list(range(NBLK)) if with_ctx else list(range(2, NBLK))
            wg = AB.a3(8, 3072)
            for k in range(8):
                S.dma("pool", wg[:, k, :], win_v[:, k, 1952:5024])
            wbr = AB.a3(8, 1024)
            S.dma("pool", wbr[:, 0:4, :], dr["w_br_a"][l].rearrange("(k p) c -> p k c", p=128))
            S.dma("pool", wbr[:, 4:6, :], dr["w_br_b"][l].rearrange("(k p) c -> p k c", p=128))
            S.dma("pool", wbr[:, 6:8, :], dr["w_br_c"][l].rearrange("(k p) c -> p k c", p=128))
            wout = AB.a3(8, 1024)
            S.dma("pool", wout, dr["w_out"][l].rearrange("(k p) c -> p k c", p=128))
            hb_ = AB.a3(8, 512)
            yb_ = AB.a3(8, 512)
            mT = AB.a3(8, 512)
            h2b = AB.a3(8, 512)
            sqb = h2b
            xb = AFa.a3(8, 512)
            sg = [AFa.alloc(512) for _ in range(3)]
            tm = [AFa.alloc(512) for _ in range(3)]
            rs_ = AFa.alloc(512)
            brk = ((0, 4), (4, 2), (6, 2))
            for blk in blks:
                mc, islat, rt0 = blk_info(blk)
                c0 = blk * 512
                S.dma("sp", hb_, fm(hT)[:, :, c0:c0 + 512])
                S.dma("sp", yb_[:, 0:4, :], fm(yaT)[:, :, c0:c0 + 512])
                S.dma("sp", yb_[:, 4:6, :], fm(ybT)[:, :, c0:c0 + 512])
                S.dma("sp", yb_[:, 6:8, :], fm(ycT)[:, :, c0:c0 + 512])
                S.dma("sp", xb, xT_v[:, :, c0:c0 + 512])
                for j in range(8):
                    for bi, (k0, nk) in enumerate(brk):
                        pg = bank()
                        for k in range(8):
                            mm(pg, wg[:, k, bi * 1024 + j * 128:bi * 1024 + (j + 1) * 128], hb_[:, k, :],
                               start=(k == 0), stop=(k == 7))
                        act(sg[bi], pg, AF.Sigmoid)
                        pp = bank()
                        for k in range(nk):
                            mm(pp, wbr[:, k0 + k, j * 128:(j + 1) * 128], yb_[:, k0 + k, :],
                               start=(k == 0), stop=(k == nk - 1))
                        tt("dve", tm[bi], pp, sg[bi], ALU.mult)
                    tt("pool", tm[0], tm[0], tm[1], ALU.add)
                    tt("pool", mT[:, j, :], tm[0], tm[2], ALU.add)
                for j in range(8):
                    py = bank()
                    for k in range(8):
                        mm(py, wout[:, k, j * 128:(j + 1) * 128], mT[:, k, :], start=(k == 0), stop=(k == 7))
                    stt("dve", xb[:, j, :], py, modT[:, l, 16 + j, mc:mc + 1], xb[:, j, :], ALU.mult, ALU.add)
                    act(sqb[:, j, :], xb[:, j, :], AF.Square)
                S.dma("pool", xT_v[:, :, c0:c0 + 512], xb)
                pss = bank()
                for j in range(8):
                    mm(pss, onesb, sqb[:, j, :], start=(j == 0), stop=(j == 7))
                rstd_from(pss, rs_, 1024.0)
                for j in range(8):
                    t1 = tm[j % 3]
                    tt("dve", t1, xb[:, j, :], rs_, ALU.mult)
                    ts("pool", h2b[:, j, :], t1, A2[:, l, j, mc:mc + 1], modT[:, l, 24 + j, mc:mc + 1], ALU.mult, ALU.add)
                S.dma("pool", fm(h2T)[:, :, c0:c0 + 512], h2b)
            AFa.top = mark_f
            AB.top = mark_b

            for half in range(2):
                mark_f = AFa.top
                mark_b = AB.top
                w1h = AB.a3(8, 2048)
                w1_v = dr["w_fc1"][l].rearrange("(k p) c -> p k c", p=128)
                for k in range(8):
                    S.dma("pool", w1h[:, k, :], w1_v[:, k, half * 2048:(half + 1) * 2048])
                w2h = AB.a3(16, 1024)
                w2_v = dr["w_fc2"][l].rearrange("(k p) c -> p k c", p=128)
                for k4 in range(4):
                    S.dma("pool", w2h[:, k4 * 4:(k4 + 1) * 4, :], w2_v[:, half * 16 + k4 * 4:half * 16 + (k4 + 1) * 4, :])
                h2l = [AB.a3(8, 512) for _ in range(2)]
                aT = AB.a3(16, 512)
                xl = [AFa.a3(8, 512) for _ in range(2)]
                rl = [AFa.alloc(512) for _ in range(3)]
                for bi_, blk in enumerate(blks):
                    mc, islat, rt0 = blk_info(blk)
                    c0 = blk * 512
                    h2_ = h2l[bi_ % 2]
                    xb = xl[bi_ % 2]
                    S.dma("sp", h2_, fm(h2T)[:, :, c0:c0 + 512])
                    S.dma("sp", xb, xT_v[:, :, c0:c0 + 512])
                    for jf in range(16):
                        pf = bank()
                        for k in range(8):
                            mm(pf, w1h[:, k, jf * 128:(jf + 1) * 128], h2_[:, k, :], start=(k == 0), stop=(k == 7))
                        r_ = rl[jf % 3]
                        if jf % 2 == 0:
                            act(r_, pf, AF.Relu)
                        else:
                            ts("dve", r_, pf, 0.0, None, ALU.max)
                        tt("pool", aT[:, jf, :], r_, r_, ALU.mult)
                    for j in range(8):
                        py = bank()
                        for k in range(16):
                            mm(py, w2h[:, k, j * 128:(j + 1) * 128], aT[:, k, :], start=(k == 0), stop=(k == 15))
                        stt("dve", xb[:, j, :], py, modT[:, l, 40 + j, mc:mc + 1], xb[:, j, :], ALU.mult, ALU.add)
                    S.dma("pool", xT_v[:, :, c0:c0 + 512], xb)
                AFa.top = mark_f
                AB.top = mark_b

        mark_f = AFa.top
        mark_b = AB.top
        xl = [AFa.a3(8, 512) for _ in range(2)]
        sqb = AB.a3(8, 512)
        rs_ = AFa.alloc(512)
        ob = [AFa.alloc(1024) for _ in range(2)]
        oi = 0
        for blk in range(2, NBLK):
            b, islat, rt0 = blk_info(blk)
            c0 = blk * 512
            xb = xl[blk % 2]
            S.dma("sp", xb, xT_v[:, :, c0:c0 + 512])
            for j in range(8):
                act(sqb[:, j, :], xb[:, j, :], AF.Square)
            pss = bank()
            for j in range(8):
                mm(pss, onesb, sqb[:, j, :], start=(j == 0), stop=(j == 7))
            rstd_from(pss, rs_, 1024.0)
            for j in range(8):
                tt("dve", xb[:, j, :], xb[:, j, :], rs_, ALU.mult)
                ts("pool", xb[:, j, :], xb[:, j, :], vcol(("fing",), j), None, ALU.mult)
            for t4 in range(4):
                o_ = ob[oi % 2]
                oi += 1
                for half in range(2):
                    pb = bank()
                    for q in range(4):
                        j = half * 4 + q
                        tr(pb[:, q * 128:(q + 1) * 128], xb[:, j, t4 * 128:(t4 + 1) * 128], ident)
                    evac(o_[:, half * 512:(half + 1) * 512], pb)
                r0 = rt0 + t4 * 128
                S.dma("pool", out[b, r0:r0 + 128, :], o_)
        AFa.top = mark_f
        AB.top = mark_b

        S.emit()
    return nc, S


_CACHE = {}


def kernel(**inputs):
    n_cores = 8
    if "nc" not in _CACHE:
        _CACHE["nc"] = build_program()[0]
        _CACHE["consts"] = host_constants()
    nc = _CACHE["nc"]
    consts = _CACHE["consts"]
    x = np.ascontiguousarray(np.asarray(inputs["x"], dtype=np.float32))
    c = np.ascontiguousarray(np.asarray(inputs["c"], dtype=np.float32))
    ctx = np.ascontiguousarray(np.asarray(inputs["ctx"], dtype=np.float32))
    shared = {nm: np.ascontiguousarray(np.asarray(inputs[nm], dtype=np.float32)) for nm in WEIGHT_NAMES}
    shared.update(consts)
    in_maps = []
    for i in range(n_cores):
        m = dict(shared)
        m["x"] = x[i * NB:(i + 1) * NB]
        m["c"] = c[i * NB:(i + 1) * NB]
        m["ctx"] = ctx[i * NB:(i + 1) * NB]
        in_maps.append(m)
    res = run_bass_kernel_spmd(nc, in_maps, core_ids=list(range(n_cores)))
    return np.concatenate([np.asarray(r["out"], dtype=np.float32) for r in res.results], axis=0)
```

```python
import math
import contextlib
from collections import deque
import numpy as np
import ml_dtypes
import concourse.bass as bass
import concourse.mybir as mybir
from concourse.bass_utils import run_bass_kernel_spmd

F32 = mybir.dt.float32
BF16 = mybir.dt.bfloat16
ALU = mybir.AluOpType
AF = mybir.ActivationFunctionType
AX = mybir.AxisListType

ENGS = ("pe", "dve", "act", "pool", "sp")
NDMASEM = 8


def _prod(xs):
    r = 1
    for v in xs:
        r *= int(v)
    return r


_RS_CACHE = {}


def region_of(ap):
    t = ap.tensor
    nm = t.name
    rs = _RS_CACHE.get(nm)
    if rs is None:
        rs = _prod(list(t.shape)[1:])
        _RS_CACHE[nm] = rs
    off = int(ap.offset)
    r0, c0 = divmod(off, rs)
    r1, c1 = r0, c0
    ne = 1
    for step, cnt in ap.ap:
        step = int(step)
        cnt = int(cnt)
        if cnt <= 1 or step == 0:
            continue
        ne *= cnt
        a, b = divmod(step, rs)
        r1 += a * (cnt - 1)
        c1 += b * (cnt - 1)
    if c1 >= rs:
        r1 += c1 // rs
        c0, c1 = 0, rs - 1
    dense = ne >= (r1 + 1 - r0) * (c1 + 1 - c0)
    return (nm, r0, r1 + 1, c0, c1 + 1, dense)


def _ovl(a, b):
    return a[1] < b[2] and b[1] < a[2] and a[3] < b[4] and b[3] < a[4]


def _cov(a, b):
    return a[5] and a[1] <= b[1] and a[2] >= b[2] and a[3] <= b[3] and a[4] >= b[4]


class Op:
    __slots__ = ("eng", "fn", "idx", "cdeps", "ddeps", "dma", "inc", "semval", "ownwait")

    def __init__(self, eng, fn, idx):
        self.eng = eng
        self.fn = fn
        self.idx = idx
        self.cdeps = {}
        self.ddeps = {}
        self.dma = None
        self.inc = False
        self.semval = 0
        self.ownwait = None


class Sched:
    def __init__(self, nc):
        self.nc = nc
        self.ops = {e: [] for e in ENGS}
        self.order = []
        self.bufs = {}
        self.dma_use = {e: [0] * NDMASEM for e in ENGS}
        self.dma_rr = {e: 0 for e in ENGS}

    def _adddep(self, op, prod, kind):
        if prod[0] == "c":
            _, e, i = prod
            if e == op.eng:
                if kind != "raw":
                    return
                if e == "pe":
                    return
            if op.cdeps.get(e, -1) < i:
                op.cdeps[e] = i
        else:
            _, q, si, val = prod
            k = (q, si)
            if op.ddeps.get(k, 0) < val:
                op.ddeps[k] = val

    def add(self, eng, fn, reads=(), writes=(), dma=False):
        op = Op(eng, fn, len(self.ops[eng]))
        if dma:
            si = self.dma_rr[eng]
            self.dma_rr[eng] = (si + 1) % NDMASEM
            prev = self.dma_use[eng][si]
            self.dma_use[eng][si] = prev + 16
            op.dma = (eng, si, prev + 16)
            if prev:
                op.ownwait = (eng, si, prev)
            me = ("d", eng, si, prev + 16)
        else:
            me = ("c", eng, op.idx)
        rregs = [region_of(a) for a in reads]
        wregs = [region_of(a) for a in writes]
        for rg in rregs:
            b = self.bufs.setdefault(rg[0], {"w": [], "r": []})
            for (wr, prod) in b["w"]:
                if _ovl(wr, rg):
                    self._adddep(op, prod, "raw")
        for rg in wregs:
            b = self.bufs.setdefault(rg[0], {"w": [], "r": []})
            for (wr, prod) in b["w"]:
                if _ovl(wr, rg):
                    self._adddep(op, prod, "waw")
            for (rr, cons) in b["r"]:
                if _ovl(rr, rg):
                    self._adddep(op, cons, "war")
        for rg in rregs:
            b = self.bufs[rg[0]]
            if not dma:
                b["r"] = [(rr, c) for (rr, c) in b["r"]
                          if not (c[0] == "c" and c[1] == eng and _cov(rg, rr))]
            b["r"].append((rg, me))
        for rg in wregs:
            b = self.bufs[rg[0]]
            b["w"] = [(wr, p) for (wr, p) in b["w"] if not _cov(rg, wr)]
            b["r"] = [(rr, c) for (rr, c) in b["r"] if not _cov(rg, rr)]
            b["w"].append((rg, me))
        self.ops[eng].append(op)
        self.order.append(op)
        return op

    def dma(self, q, out, in_, **kw):
        return self.add(q, lambda e: e.dma_start(out=out, in_=in_, **kw),
                        reads=[in_], writes=[out], dma=True)

    def emit(self):
        nc = self.nc
        seen_c = {e: {f: -1 for f in ENGS} for e in ENGS}
        seen_d = {e: {} for e in ENGS}
        waits = {}
        for op in self.order:
            w = []
            e = op.eng
            for f, i in op.cdeps.items():
                if seen_c[e][f] < i:
                    seen_c[e][f] = i
                    w.append(("c", f, i))
                    self.ops[f][i].inc = True
            dd = dict(op.ddeps)
            if op.ownwait is not None:
                q, si, val = op.ownwait
                if dd.get((q, si), 0) < val:
                    dd[(q, si)] = val
            for (q, si), val in dd.items():
                if seen_d[e].get((q, si), 0) < val:
                    seen_d[e][(q, si)] = val
                    w.append(("d", q, si, val))
            waits[id(op)] = w
        for e in ENGS:
            n = 0
            for op in self.ops[e]:
                if op.inc:
                    n += 1
                op.semval = n
        self.stats = {e: (len(self.ops[e]), sum(1 for o in self.ops[e] if o.inc)) for e in ENGS}
        with contextlib.ExitStack() as st:
            csem = {e: st.enter_context(nc.semaphore("cs_" + e)) for e in ENGS}
            dsem = {e: [st.enter_context(nc.semaphore("ds_%s%d" % (e, i))) for i in range(NDMASEM)]
                    for e in ENGS if any(o.dma for o in self.ops[e])}
            block = st.enter_context(nc.Block())

            def run(e):
                def body(eng):
                    for op in self.ops[e]:
                        for w in waits[id(op)]:
                            if w[0] == "c":
                                eng.wait_ge(csem[w[1]], self.ops[w[1]][w[2]].semval)
                            else:
                                eng.wait_ge(dsem[w[1]][w[2]], w[3])
                        ins = op.fn(eng)
                        if op.dma is not None:
                            ins.then_inc(dsem[op.dma[0]][op.dma[1]], 16)
                        elif op.inc:
                            ins.then_inc(csem[e], 1)
                    if e in dsem:
                        for si in range(NDMASEM):
                            v = self.dma_use[e][si]
                            if v and seen_d[e].get((e, si), 0) < v:
                                eng.wait_ge(dsem[e][si], v)
                return body

            block.tensor(run("pe"))
            block.vector(run("dve"))
            block.scalar(run("act"))
            block.gpsimd(run("pool"))
            block.sync(run("sp"))


D = 1024
SEQ = 2048
CTX = 256
DEPTH = 4
NB = 4
TT = NB * (SEQ + CTX)
NBLK = TT // 512
D_IN = 5024
EPS = 1e-6
MLA_SCALE = 96 ** -0.5
DF_SCALE = 32 ** -0.5
NA = 2496
C_KRP, C_DQP, C_DKP = 1952, 1984, 2240
WEIGHT_NAMES = ["norm_mix_g", "norm_ffn_g", "w_mod", "b_mod", "w_in", "mla_q_norm_g", "mla_w_uq",
                "mla_kv_norm_g", "mla_w_ukv", "hy_conv_w", "hy_conv_b", "hy_w1", "hy_b1", "hy_freq",
                "hy_w2", "hy_b2", "hy_w3", "hy_b3", "hy_skip", "df_lq1", "df_lk1", "df_lq2", "df_lk2",
                "df_subln_g", "w_br_a", "w_br_b", "w_br_c", "w_out", "w_fc1", "w_fc2", "final_norm_g",
                "c_ctx"]
WEIGHT_SHAPES = {
    "norm_mix_g": [4, 1024], "norm_ffn_g": [4, 1024], "w_mod": [4, 1024, 6144], "b_mod": [4, 6144],
    "w_in": [4, 1024, 5024], "mla_q_norm_g": [4, 256], "mla_w_uq": [4, 256, 768],
    "mla_kv_norm_g": [4, 128], "mla_w_ukv": [4, 128, 1024], "hy_conv_w": [4, 3, 768],
    "hy_conv_b": [4, 768], "hy_w1": [4, 33, 64], "hy_b1": [4, 64], "hy_freq": [4, 64],
    "hy_w2": [4, 64, 64], "hy_b2": [4, 64], "hy_w3": [4, 64, 512], "hy_b3": [4, 512],
    "hy_skip": [4, 256], "df_lq1": [4, 32], "df_lk1": [4, 32], "df_lq2": [4, 32], "df_lk2": [4, 32],
    "df_subln_g": [4, 64], "w_br_a": [4, 512, 1024], "w_br_b": [4, 256, 1024], "w_br_c": [4, 256, 1024],
    "w_out": [4, 1024, 1024], "w_fc1": [4, 1024, 4096], "w_fc2": [4, 4096, 1024], "final_norm_g": [1024],
    "c_ctx": [1024],
}


def lam_init_of(l):
    return 0.8 - 0.6 * math.exp(-0.3 * l)


def host_constants():
    f32 = np.float32
    cs = {}
    L = SEQ
    t = np.arange(L)
    row = (t // 64).astype(f32)
    col = (t % 64).astype(f32)
    inv = (10000.0 ** (-np.arange(8, dtype=f32) / 8)).astype(f32)
    ang = np.concatenate([row[:, None] * inv, col[:, None] * inv], axis=-1).astype(f32)
    cosT = np.cos(ang).astype(f32).T
    sinT = np.sin(ang).astype(f32).T
    rc = np.zeros((128, L), f32)
    rsn = np.zeros((128, L), f32)
    for p in range(128):
        q = p % 32
        i = q % 16
        rc[p] = cosT[i]
        rsn[p] = -sinT[i] if q < 16 else sinT[i]
    cs["rope_c"] = rc
    cs["rope_s"] = rsn
    for tag, L in (("l", SEQ), ("c", CTX)):
        tt_ = np.linspace(0.0, 1.0, L, dtype=f32)[:, None]
        w = ((2.0 * math.pi / L) * np.arange(L, dtype=f32))[:, None].astype(f32)
        bands = np.linspace(1e-4, 15.0, 16, dtype=f32)[None]
        z = np.concatenate([tt_, np.cos(bands * w), -np.sin(bands * w)], axis=-1).astype(f32)
        cs["zT_" + tag] = np.ascontiguousarray(z.T)
        deltas = np.linspace(math.log(1e-2) / 1.5, math.log(1e-2) / 0.3, 256, dtype=f32)
        win = (np.exp(-tt_ * np.abs(deltas)[None]) + 0.05).astype(f32)
        ntt = L // 128
        cs["win_" + tag] = np.ascontiguousarray(win.reshape(ntt, 128, 256).transpose(1, 0, 2))
        N = 2 * L
        ti = np.arange(L, dtype=np.float64)[:, None]
        fi = np.arange(L, dtype=np.float64)[None, :]
        th = math.pi * (2 * fi + 1) / N
        Cm = np.cos(th * ti)
        Sm = np.sin(th * ti)
        nfc = L // 128
        M = np.stack([Cm, Sm], 0)
        FW = M.reshape(2, ntt, 128, nfc, 128).transpose(3, 2, 0, 1, 4)
        cs["FW_" + tag] = np.ascontiguousarray(FW).astype(ml_dtypes.bfloat16).reshape(nfc, 128, 2 * ntt * 128)
        tbw = 512 if L >= 512 else L
        ntb = L // tbw
        GI = M.reshape(2, ntb, tbw, nfc, 128).transpose(1, 4, 3, 0, 2)
        cs["GI_" + tag] = np.ascontiguousarray(GI).astype(ml_dtypes.bfloat16).reshape(ntb, 128, nfc * 2 * tbw)
    return cs


CONST_SPECS = {
    "rope_c": ([128, 2048], F32), "rope_s": ([128, 2048], F32),
    "zT_l": ([33, 2048], F32), "zT_c": ([33, 256], F32),
    "win_l": ([128, 16, 256], F32), "win_c": ([128, 2, 256], F32),
    "FW_l": ([16, 128, 4096], BF16), "FW_c": ([2, 128, 512], BF16),
    "GI_l": ([4, 128, 16384], BF16), "GI_c": ([1, 128, 1024], BF16),
}


class Arena:
    def __init__(self, handle, n):
        self.h = handle
        self.n = n
        self.top = 0

    def alloc(self, n, shape=None):
        n = (n + 15) // 16 * 16
        assert self.top + n <= self.n, ("arena overflow", self.h.name, self.top, n, self.n)
        v = self.h[:, self.top:self.top + n]
        self.top += n
        return v

    def a3(self, a, b):
        v = self.alloc(a * b)
        return v.rearrange("p (a b) -> p a b", a=a)


def build_program(n_layers=DEPTH, dbg=None):
    nc = bass.Bass("TRN2", target_bir_lowering=False)
    S = Sched(nc)
    dbg = dbg or []
    dr = {}
    dr["x"] = nc.dram_tensor("x", [NB, SEQ, D], F32, kind="ExternalInput")
    dr["c"] = nc.dram_tensor("c", [NB, D], F32, kind="ExternalInput")
    dr["ctx"] = nc.dram_tensor("ctx", [NB, CTX, D], F32, kind="ExternalInput")
    for nm in WEIGHT_NAMES:
        dr[nm] = nc.dram_tensor(nm, WEIGHT_SHAPES[nm], F32, kind="ExternalInput")
    for nm, (shp, dt) in CONST_SPECS.items():
        dr[nm] = nc.dram_tensor(nm, shp, dt, kind="ExternalInput")
    out = nc.dram_tensor("out", [NB, SEQ, D], F32, kind="ExternalOutput")

    def scratch(nm, shape, dt):
        kind = "ExternalOutput" if nm in dbg else "Internal"
        dr[nm] = nc.dram_tensor(nm, shape, dt, kind=kind)
        return dr[nm]

    xT = scratch("xT", [D, TT], F32)
    hT = scratch("hT", [D, TT], BF16)
    h2T = scratch("h2T", [D, TT], BF16)
    QnT = scratch("QnT", [512, TT], BF16)
    QrT = scratch("QrT", [256, TT], BF16)
    KnT = scratch("KnT", [512, TT], BF16)
    KrT = scratch("KrT", [32, TT], BF16)
    Vm = scratch("Vm", [TT, 512], BF16)
    phyT = scratch("phyT", [768, TT], BF16)
    DqT = scratch("DqT", [256, TT], BF16)
    DkT = scratch("DkT", [256, TT], BF16)
    Dv = scratch("Dv", [TT, 256], BF16)
    hx0T = scratch("hx0T", [256, TT], BF16)
    huT = scratch("huT", [256, TT], BF16)
    yaT = scratch("yaT", [512, TT], BF16)
    ybT = scratch("ybT", [256, TT], BF16)
    ycT = scratch("ycT", [256, TT], BF16)

    def fm(t):
        return t.rearrange("(c p) t -> p c t", p=128)

    st = contextlib.ExitStack()
    with st:
        AB_N = 58 * 1024
        AF_N = 18 * 1024
        abh = st.enter_context(nc.sbuf_tensor("arena_bf", [128, AB_N], BF16))
        afh = st.enter_context(nc.sbuf_tensor("arena_f", [128, AF_N], F32))
        AB = Arena(abh, AB_N)
        AFa = Arena(afh, AF_N)
        banks = [st.enter_context(nc.psum_tensor("bank%d" % i, [128, 512], F32)) for i in range(7)]
        bankT = st.enter_context(nc.psum_tensor("bankT", [128, 1024], BF16))
        rr = {"i": 0}

        def bank(lo=0, hi=7):
            n = hi - lo
            rr["i"] = (rr["i"] + 1) % n
            return banks[lo + rr["i"]][:, :]

        def mm(o, lhsT, rhs, start=True, stop=True):
            S.add("pe", lambda e: e.matmul(o, lhsT=lhsT, rhs=rhs, start=start, stop=stop),
                  reads=[lhsT, rhs], writes=[o])

        def tr(o, in_, ident_ap):
            S.add("pe", lambda e: e.transpose(o, in_, ident_ap), reads=[in_, ident_ap], writes=[o])

        def act(o, in_, func, scale=1.0, bias=0.0):
            rd = [in_]
            if not isinstance(scale, (int, float)):
                rd.append(scale)
            if not isinstance(bias, (int, float)):
                rd.append(bias)
            S.add("act", lambda e: e.activation(out=o, in_=in_, func=func, bias=bias, scale=scale),
                  reads=rd, writes=[o])

        def cp(eng, o, in_):
            if eng == "act":
                S.add("act", lambda e: e.copy(out=o, in_=in_), reads=[in_], writes=[o])
            else:
                S.add(eng, lambda e: e.tensor_copy(out=o, in_=in_), reads=[in_], writes=[o])

        def tt(eng, o, a, b, op):
            S.add(eng, lambda e: e.tensor_tensor(out=o, in0=a, in1=b, op=op), reads=[a, b], writes=[o])

        def ts(eng, o, a, s1, s2, op0, op1=None):
            rd = [a]
            if not isinstance(s1, (int, float)):
                rd.append(s1)
            if s2 is not None and not isinstance(s2, (int, float)):
                rd.append(s2)
            if op1 is None:
                S.add(eng, lambda e: e.tensor_scalar(out=o, in0=a, scalar1=s1, scalar2=None, op0=op0),
                      reads=rd, writes=[o])
            else:
                S.add(eng, lambda e: e.tensor_scalar(out=o, in0=a, scalar1=s1, scalar2=s2, op0=op0, op1=op1),
                      reads=rd, writes=[o])

        def stt(eng, o, a, s, b, op0, op1):
            rd = [a, b]
            if not isinstance(s, (int, float)):
                rd.append(s)
            S.add(eng, lambda e: e.scalar_tensor_tensor(out=o, in0=a, scalar=s, in1=b, op0=op0, op1=op1),
                  reads=rd, writes=[o])

        def recip(o, in_):
            S.add("dve", lambda e: e.reciprocal(out=o, in_=in_), reads=[in_], writes=[o])

        def memset(eng, o, v):
            S.add(eng, lambda e: e.memset(o, v), writes=[o])

        evq = {"i": 0}

        def evac(o, in_):
            evq["i"] ^= 1
            cp("dve" if evq["i"] else "act", o, in_)

        def rstd_from(ps_ap, o, n):
            act(o, ps_ap, AF.Sqrt, scale=1.0 / n, bias=EPS)
            recip(o, o)

        ident = AFa.alloc(128)
        memset("pool", ident, 0.0)
        S.add("pool", lambda e: e.affine_select(out=ident, in_=ident, pattern=[[-1, 128]],
                                                compare_op=ALU.not_equal, fill=1.0, base=0,
                                                channel_multiplier=1),
              reads=[ident], writes=[ident])
        identb = AB.alloc(128)
        cp("dve", identb, ident)
        onesb = AB.alloc(128)
        memset("pool", onesb, 1.0)
        onesf = AFa.alloc(128)
        memset("pool", onesf, 1.0)
        mask0 = AFa.alloc(16)[:, 0:1]
        memset("pool", mask0, 1.0)
        memset("pool", mask0[0:1, :], 0.0)
        rope_c = AFa.alloc(2048)
        rope_s = AFa.alloc(2048)
        S.dma("sp", rope_c, dr["rope_c"][:, :])
        S.dma("sp", rope_s, dr["rope_s"][:, :])

        VROW = {}
        rows = []

        def vadd(key, src_ap, n):
            VROW[key] = len(rows)
            for j in range(n):
                rows.append((src_ap, j))

        for l in range(DEPTH):
            vadd(("mixg", l), dr["norm_mix_g"][l].rearrange("(j p) -> j p", p=128), 8)
            vadd(("ffng", l), dr["norm_ffn_g"][l].rearrange("(j p) -> j p", p=128), 8)
            vadd(("bmod", l), dr["b_mod"][l].rearrange("(j p) -> j p", p=128), 48)
            vadd(("qg", l), dr["mla_q_norm_g"][l].rearrange("(j p) -> j p", p=128), 2)
            vadd(("kvg", l), dr["mla_kv_norm_g"][l].rearrange("(j p) -> j p", p=128), 1)
            vadd(("cw", l), dr["hy_conv_w"][l].rearrange("t (j p) -> (t j) p", p=128), 18)
            vadd(("cb", l), dr["hy_conv_b"][l].rearrange("(j p) -> j p", p=128), 6)
            vadd(("skip", l), dr["hy_skip"][l].rearrange("(j p) -> j p", p=128), 2)
        vadd(("fing",), dr["final_norm_g"].rearrange("(j p) -> j p", p=128), 8)
        NV = len(rows)
        vecT = AFa.alloc(NV)
        mark_f = AFa.top
        vrows = AFa.alloc(128)
        g0 = 0
        while g0 < NV:
            n = min(128, NV - g0)
            i = g0
            while i < g0 + n:
                src, j = rows[i]
                k = i
                while k + 1 < g0 + n and rows[k + 1][0] is src and rows[k + 1][1] == rows[k][1] + 1:
                    k += 1
                cnt = k - i + 1
                S.dma("sp", vrows[i - g0:i - g0 + cnt, :], src[j:j + cnt, :])
                i = k + 1
            pb = bank()
            tr(pb[:, 0:n], vrows[0:n, :], ident[0:n, 0:n])
            cp("dve", vecT[:, g0:g0 + n], pb[:, 0:n])
            g0 += n
        AFa.top = mark_f

        def vcol(key, j=0):
            c = VROW[key] + j
            return vecT[:, c:c + 1]

        smallv = AFa.alloc(DEPTH * 8)
        SV = {}
        for l in range(DEPTH):
            for i, nm in enumerate(["hy_b1", "hy_freq", "hy_b2", "df_subln_g"]):
                colv = smallv[0:64, l * 8 + i:l * 8 + i + 1]
                S.dma("sp", colv, dr[nm][l].rearrange("(p o) -> p o", o=1))
                SV[(nm, l)] = colv
            for i, (a, b) in enumerate([("hy_freq", "hy_b1"), ("hy_freq", "hy_b2")]):
                colv = smallv[0:64, l * 8 + 4 + i:l * 8 + 5 + i]
                tt("dve", colv, SV[(a, l)], SV[(b, l)], ALU.mult)
                SV[("fb%d" % (i + 1), l)] = colv
            colv = smallv[0:64, l * 8 + 6:l * 8 + 7]
            ts("dve", colv, SV[("df_subln_g", l)], 1.0 - lam_init_of(l), None, ALU.mult)
            SV[("sublng", l)] = colv

        neglamT = AFa.alloc(16)
        mark_f = AFa.top
        lamrow = AFa.alloc(1024)
        for i, nm in enumerate(["df_lq1", "df_lk1", "df_lq2", "df_lk2"]):
            S.dma("sp", lamrow[0:1, i * 128:(i + 1) * 128], dr[nm].rearrange("(o l) d -> o (l d)", o=1))
        tt("dve", lamrow[0:1, 512:640], lamrow[0:1, 0:128], lamrow[0:1, 128:256], ALU.mult)
        tt("dve", lamrow[0:1, 640:768], lamrow[0:1, 256:384], lamrow[0:1, 384:512], ALU.mult)
        for i in range(2):
            src = lamrow[0:1, 512 + i * 128:640 + i * 128].rearrange("p (l d) -> p l d", l=4)
            dst = lamrow[0:1, 768 + i * 4:772 + i * 4]
            S.add("dve", (lambda s_, d_: (lambda e: e.reduce_sum(out=d_, in_=s_, axis=AX.X)))(src, dst),
                  reads=[src], writes=[dst])
        act(lamrow[0:1, 768:776], lamrow[0:1, 768:776], AF.Exp)
        tt("dve", lamrow[0:1, 776:780], lamrow[0:1, 772:776], lamrow[0:1, 768:772], ALU.subtract)
        for l in range(DEPTH):
            ts("dve", lamrow[0:1, 780 + l:781 + l], lamrow[0:1, 776 + l:777 + l], -lam_init_of(l), None, ALU.add)
        pb = bank()
        mm(pb[0:64, 0:4], onesf[0:1, 0:64], lamrow[0:1, 780:784])
        cp("dve", neglamT[0:64, 0:4], pb[0:64, 0:4])
        AFa.top = mark_f

        modT = AFa.alloc(DEPTH * 48 * 5).rearrange("p (l j c) -> p l j c", l=DEPTH, j=48)
        A1 = AFa.alloc(DEPTH * 8 * 5).rearrange("p (l j c) -> p l j c", l=DEPTH, j=8)
        A2 = AFa.alloc(DEPTH * 8 * 5).rearrange("p (l j c) -> p l j c", l=DEPTH, j=8)
        mark_f = AFa.top
        mark_b = AB.top
        cs_rows = AFa.alloc(1024)
        S.dma("sp", cs_rows[0:4, :], dr["c"][:, :])
        S.dma("sp", cs_rows[4:5, :], dr["c_ctx"].rearrange("(o d) -> o d", o=1))
        act(cs_rows[0:5, :], cs_rows[0:5, :], AF.Silu)
        sT = AFa.alloc(48).rearrange("p (k c) -> p k c", k=8)
        for k in range(8):
            pb = bank()
            tr(pb[:, 0:5], cs_rows[0:5, k * 128:(k + 1) * 128], ident[0:5, 0:5])
            cp("dve", sT[:, k, 0:5], pb[:, 0:5])
        wmbuf = [AFa.alloc(4096).rearrange("p (k c) -> p k c", k=8) for _ in range(2)]
        it = 0
        for l in range(n_layers):
            wm_v = dr["w_mod"][l].rearrange("(k p) c -> p k c", p=128)
            for half in range(12):
                wb = wmbuf[it % 2]
                it += 1
                S.dma("sp", wb, wm_v[:, :, half * 512:(half + 1) * 512])
                for jj in range(4):
                    j = half * 4 + jj
                    pb = bank()
                    for k in range(8):
                        mm(pb[:, 0:5], wb[:, k, jj * 128:(jj + 1) * 128], sT[:, k, 0:5], start=(k == 0), stop=(k == 7))
                    ts("dve", modT[:, l, j, :], pb[:, 0:5], vcol(("bmod", l), j), None, ALU.add)
            for j in range(8):
                ts("dve", A1[:, l, j, :], modT[:, l, 8 + j, :], 1.0, vcol(("mixg", l), j), ALU.add, ALU.mult)
                ts("dve", A2[:, l, j, :], modT[:, l, 32 + j, :], 1.0, vcol(("ffng", l), j), ALU.add, ALU.mult)
        AFa.top = mark_f
        AB.top = mark_b

        mark_f = AFa.top
        xin = [AFa.alloc(1024) for _ in range(2)]
        xst = [AFa.a3(8, 512) for _ in range(1)]
        xT_v = fm(xT)
        ti = 0
        for blk in range(NBLK):
            stg = xst[0]
            for t4 in range(4):
                tok0 = blk * 512 + t4 * 128
                if tok0 < NB * CTX:
                    b, r0 = divmod(tok0, CTX)
                    src = dr["ctx"][b, r0:r0 + 128, :]
                else:
                    b, r0 = divmod(tok0 - NB * CTX, SEQ)
                    src = dr["x"][b, r0:r0 + 128, :]
                xi = xin[ti % 2]
                ti += 1
                S.dma("sp", xi, src)
                for half in range(2):
                    pb = bank()
                    for q in range(4):
                        j = half * 4 + q
                        tr(pb[:, q * 128:(q + 1) * 128], xi[:, j * 128:(j + 1) * 128], ident)
                    evac(stg[:, half * 4:half * 4 + 4, t4 * 128:(t4 + 1) * 128],
                         pb[:, :].rearrange("p (q t) -> p q t", q=4))
            S.dma("pool", xT_v[:, :, blk * 512:(blk + 1) * 512], stg)
        AFa.top = mark_f

        def blk_info(blk):
            if blk < 2:
                return 4, False, 0
            b = (blk - 2) // 4
            return b, True, ((blk - 2) % 4) * 512

        def keycols(b):
            return [(b * CTX, CTX), (NB * CTX + b * SEQ, SEQ)]

        for l in range(n_layers):
            with_ctx = l < DEPTH - 1
            mark_f = AFa.top
            mark_b = AB.top
            winA = AB.a3(8, NA)
            win_v = dr["w_in"][l].rearrange("(k p) c -> p k c", p=128)
            for k in range(8):
                S.dma("pool", winA[:, k, 0:1952], win_v[:, k, 0:1952])
            for k in range(8):
                cp("pool", winA[:, k, C_KRP:C_KRP + 16], winA[:, k, 400:416])
                cp("pool", winA[:, k, C_KRP + 16:C_KRP + 32], winA[:, k, 384:400])
                for (src0, dst0) in ((1184, C_DQP), (1440, C_DKP)):
                    sv = winA[:, k, src0:src0 + 256].rearrange("p (s h i) -> p s h i", s=8, h=2)
                    dv_ = winA[:, k, dst0:dst0 + 256].rearrange("p (s h i) -> p s h i", s=8, h=2)
                    cp("pool", dv_[:, :, 0, :], sv[:, :, 1, :])
                    cp("pool", dv_[:, :, 1, :], sv[:, :, 0, :])
            wuq = AB.a3(2, 768)
            S.dma("pool", wuq, dr["mla_w_uq"][l].rearrange("(k p) c -> p k c", p=128))
            wuq_n = AB.a3(2, 512)
            wuq_r = AB.a3(2, 256)
            wuq_p = AB.a3(2, 256)
            for k in range(2):
                sv = wuq[:, k, :].rearrange("p (h d) -> p h d", h=8)
                cp("pool", wuq_n[:, k, :].rearrange("p (h d) -> p h d", h=8), sv[:, :, 0:64])
                cp("pool", wuq_r[:, k, :].rearrange("p (h d) -> p h d", h=8), sv[:, :, 64:96])
                pv = wuq_p[:, k, :].rearrange("p (h d) -> p h d", h=8)
                cp("pool", pv[:, :, 0:16], sv[:, :, 80:96])
                cp("pool", pv[:, :, 16:32], sv[:, :, 64:80])
            wukv = AB.alloc(1024)
            S.dma("pool", wukv, dr["mla_w_ukv"][l])
            wkn = AB.alloc(512)
            wv = AB.alloc(512)
            sv = wukv.rearrange("p (h d) -> p h d", h=8)
            cp("pool", wkn.rearrange("p (h d) -> p h d", h=8), sv[:, :, 0:64])
            cp("pool", wv.rearrange("p (h d) -> p h d", h=8), sv[:, :, 64:128])

            xb_ = [AFa.a3(8, 512) for _ in range(2)]
            rstd_ = [AFa.alloc(512) for _ in range(2)]
            tmpf = [AFa.alloc(512) for _ in range(4)]
            sqb = AB.a3(8, 512)
            hTb = [AB.a3(8, 512) for _ in range(2)]
            qn = AB.a3(2, 512)
            kvn = AB.alloc(512)
            stg_q = AB.a3(4, 512)
            stg_k = AB.a3(4, 512)
            stg_r = AB.a3(2, 512)
            stg_kr = AB.alloc(512)
            stg_v = AB.a3(4, 512)
            stg_hy = AB.a3(6, 512)
            stg_d = AB.a3(4, 512)
            stg_dv = AB.a3(4, 256)
            tq = {"i": 0}

            def tmp():
                tq["i"] = (tq["i"] + 1) % 4
                return tmpf[tq["i"]]

            def rope_out(o, P, Pp, rows, rt0):
                t1 = tmp()
                t2 = tmp()
                tt("dve", t1[0:rows, :], P, rope_c[0:rows, rt0:rt0 + 512], ALU.mult)
                tt("dve", t2[0:rows, :], Pp, rope_s[0:rows, rt0:rt0 + 512], ALU.mult)
                tt("pool", o, t1[0:rows, :], t2[0:rows, :], ALU.add)

            sqq = AB.a3(3, 512)

            def norm_stage(blk):
                mc, islat, rt0 = blk_info(blk)
                c0 = blk * 512
                xb = xb_[blk % 2]
                hb = hTb[blk % 2]
                rs_ = rstd_[blk % 2]
                S.dma("sp", xb, xT_v[:, :, c0:c0 + 512])
                for j in range(8):
                    act(sqb[:, j, :], xb[:, j, :], AF.Square)
                pss = bank()
                for j in range(8):
                    mm(pss, onesb, sqb[:, j, :], start=(j == 0), stop=(j == 7))
                rstd_from(pss, rs_, 1024.0)
                for j in range(8):
                    t1 = tmp()
                    tt("dve", t1, xb[:, j, :], rs_, ALU.mult)
                    ts("pool", hb[:, j, :], t1, A1[:, l, j, mc:mc + 1], modT[:, l, j, mc:mc + 1], ALU.mult, ALU.add)
                S.dma("pool", fm(hT)[:, :, c0:c0 + 512], hb)


            norm_stage(0)
            for blk in range(NBLK):
                mc, islat, rt0 = blk_info(blk)
                c0 = blk * 512
                hb = hTb[blk % 2]
                def proj(col0, M=128, pb=None):
                    pb = pb or bank()
                    for k in range(8):
                        mm(pb[0:M, :], winA[:, k, col0:col0 + M], hb[:, k, :], start=(k == 0), stop=(k == 7))
                    return pb

                pq = [proj(0), proj(128)]
                for i in range(2):
                    act(sqq[:, i, :], pq[i], AF.Square)
                pss = bank()
                for i in range(2):
                    mm(pss, onesb, sqq[:, i, :], start=(i == 0), stop=(i == 1))
                rq = tmp()
                rstd_from(pss, rq, 256.0)
                for i in range(2):
                    t1 = tmp()
                    tt("dve", t1, pq[i], rq, ALU.mult)
                    ts("pool", qn[:, i, :], t1, vcol(("qg", l), i), None, ALU.mult)
                pkv = proj(256)
                act(sqq[:, 2, :], pkv, AF.Square)
                pss = bank()
                mm(pss, onesb, sqq[:, 2, :])
                rk = tmp()
                rstd_from(pss, rk, 128.0)
                t1 = tmp()
                tt("dve", t1, pkv, rk, ALU.mult)
                ts("pool", kvn, t1, vcol(("kvg", l), 0), None, ALU.mult)
                if blk + 1 < NBLK:
                    norm_stage(blk + 1)
                pb = proj(384, M=32)
                if islat:
                    pb2 = proj(C_KRP, M=32)
                    rope_out(stg_kr[0:32, :], pb[0:32, :], pb2[0:32, :], 32, rt0)
                else:
                    evac(stg_kr[0:32, :], pb[0:32, :])
                S.dma("pool", KrT[:, c0:c0 + 512], stg_kr[0:32, :])
                for ch in range(6):
                    pb = proj(416 + ch * 128)
                    evac(stg_hy[:, ch, :], pb)
                S.dma("pool", fm(phyT)[:, :, c0:c0 + 512], stg_hy)
                for qi, (cbase, pbase, dst) in enumerate(((1184, C_DQP, DqT), (1440, C_DKP, DkT))):
                    for ch in range(2):
                        pb = proj(cbase + ch * 128)
                        if islat:
                            pb2 = proj(pbase + ch * 128)
                            rope_out(stg_d[:, qi * 2 + ch, :], pb, pb2, 128, rt0)
                        else:
                            evac(stg_d[:, qi * 2 + ch, :], pb)
                    S.dma("pool", fm(dst)[:, :, c0:c0 + 512], stg_d[:, qi * 2:qi * 2 + 2, :])
                for t4 in range(4):
                    pb = bank()
                    for k in range(8):
                        mm(pb[:, 0:256], hb[:, k, t4 * 128:(t4 + 1) * 128], winA[:, k, 1696:1952],
                           start=(k == 0), stop=(k == 7))
                    evac(stg_dv[:, t4, :], pb[:, 0:256])
                S.dma("pool", Dv.rearrange("(n p) c -> p n c", p=128)[:, blk * 4:(blk + 1) * 4, :], stg_dv)
                for ch in range(4):
                    pb = bank()
                    for k in range(2):
                        mm(pb, wuq_n[:, k, ch * 128:(ch + 1) * 128], qn[:, k, :], start=(k == 0), stop=(k == 1))
                    evac(stg_q[:, ch, :], pb)
                S.dma("pool", fm(QnT)[:, :, c0:c0 + 512], stg_q)
                for ch in range(2):
                    pb = bank()
                    for k in range(2):
                        mm(pb, wuq_r[:, k, ch * 128:(ch + 1) * 128], qn[:, k, :], start=(k == 0), stop=(k == 1))
                    if islat:
                        pb2 = bank()
                        for k in range(2):
                            mm(pb2, wuq_p[:, k, ch * 128:(ch + 1) * 128], qn[:, k, :], start=(k == 0), stop=(k == 1))
                        rope_out(stg_r[:, ch, :], pb, pb2, 128, rt0)
                    else:
                        evac(stg_r[:, ch, :], pb)
                S.dma("pool", fm(QrT)[:, :, c0:c0 + 512], stg_r)
                for ch in range(4):
                    pb = bank()
                    mm(pb, wkn[:, ch * 128:(ch + 1) * 128], kvn)
                    evac(stg_k[:, ch, :], pb)
                S.dma("pool", fm(KnT)[:, :, c0:c0 + 512], stg_k)
                for t4 in range(4):
                    pb = bank()
                    mm(pb, kvn[:, t4 * 128:(t4 + 1) * 128], wv)
                    evac(stg_v[:, t4, :], pb)
                S.dma("pool", Vm.rearrange("(n p) c -> p n c", p=128)[:, blk * 4:(blk + 1) * 4, :], stg_v)
            AFa.top = mark_f
            AB.top = mark_b

            def hyena(tag, L, col_of_b):
                mark_f0 = AFa.top
                mark_b0 = AB.top
                ntt = L // 128
                nfc = L // 128
                N = 2 * L
                tbw = 512 if L >= 512 else L
                ntb = L // tbw
                GB = 2
                Ec = AB.a3(ntt, 256)
                Es = AB.a3(ntt, 256)
                mark_f = AFa.top
                w1 = AFa.alloc(64)
                w2 = AFa.alloc(64)
                w3e = AFa.alloc(512)
                S.dma("sp", w1[0:33, 0:64], dr["hy_w1"][l])
                S.dma("sp", w2[0:64, 0:64], dr["hy_w2"][l])
                S.dma("sp", w3e[0:64, :], dr["hy_w3"][l])
                S.dma("sp", w3e[64:65, :], dr["hy_b3"][l].rearrange("(o d) -> o d", o=1))
                zTb = AFa.alloc(512)
                a1b = AFa.alloc(512)
                a2T = AFa.alloc(L)
                memset("pool", a2T[64:65, :], 1.0)
                tf = [AFa.alloc(512) for _ in range(2)]
                tfk = AFa.alloc(512)
                for tb in range(ntb):
                    cs_ = slice(tb * tbw, (tb + 1) * tbw)
                    S.dma("sp", zTb[0:33, 0:tbw], dr["zT_" + tag][:, cs_])
                    for (wm_, src, dst, fbk) in ((w1[0:33, 0:64], zTb[0:33, 0:tbw], a1b[0:64, 0:tbw], "fb1"),
                                                 (w2[0:64, 0:64], a1b[0:64, 0:tbw], a2T[0:64, cs_], "fb2")):
                        pb = bank()
                        mm(pb[0:64, 0:tbw], wm_, src)
                        t_ = tf[0][0:64, 0:tbw]
                        ts("dve", t_, pb[0:64, 0:tbw], SV[("hy_freq", l)], SV[(fbk, l)], ALU.mult, ALU.add)
                        ts("dve", t_, t_, 1.0 / (2.0 * math.pi), 16.0, ALU.mult, ALU.add)
                        ki = tf[1][0:64, 0:tbw].bitcast(mybir.dt.int32)
                        cp("dve", ki, t_)
                        kf = tfk[0:64, 0:tbw]
                        cp("dve", kf, ki)
                        tt("dve", t_, t_, kf, ALU.subtract)
                        ts("dve", kf, t_, 0.5, None, ALU.is_gt)
                        tt("dve", t_, t_, kf, ALU.subtract)
                        ts("dve", kf, t_, -0.5, None, ALU.is_lt)
                        tt("dve", t_, t_, kf, ALU.add)
                        act(dst, t_, AF.Sin, scale=2.0 * math.pi)
                win = AFa.a3(ntt, 256)
                S.dma("sp", win, dr["win_" + tag][:, :, :])
                for t_i in range(ntt):
                    pb = bank()
                    mm(pb, a2T[0:65, t_i * 128:(t_i + 1) * 128], w3e[0:65, :])
                    hf = tf[0][:, 0:256]
                    hbk = tf[1][:, 0:256]
                    tt("dve", hf, pb[:, 0:256], win[:, t_i, :], ALU.mult)
                    tt("dve", hbk, pb[:, 256:512], win[:, t_i, :], ALU.mult)
                    tt("pool", Es[:, t_i, :], hbk, hf, ALU.subtract)
                    if t_i == 0:
                        stt("dve", Ec[:, t_i, :], hbk, mask0, hf, ALU.mult, ALU.add)
                    else:
                        tt("pool", Ec[:, t_i, :], hbk, hf, ALU.add)
                AFa.top = mark_f
                cw = lambda tap, ch: vcol(("cw", l), tap * 6 + ch)
                Y = AB.alloc(nfc * 2 * GB * 256).rearrange("p (q b c) -> p q b c", q=nfc * 2, b=GB)
                tf = [AFa.alloc(512) for _ in range(2)]
                mark_f1 = AFa.top
                mark_b1 = AB.top
                for g in range(NB // GB):
                    AFa.top = mark_f1
                    AB.top = mark_b1
                    uTok = AB.alloc(ntt * GB * 256).rearrange("p (t b c) -> p t b c", t=ntt, b=GB)
                    pt = [AB.alloc(L) for _ in range(3)]
                    cf = [AFa.alloc(L) for _ in range(2)]
                    uTb = AB.alloc(L)
                    x0b = AB.alloc(L)

                    def conv(dst, src, ch):
                        ts("dve", dst, src, cw(1, ch), vcol(("cb", l), ch), ALU.mult, ALU.add)
                        stt("dve", dst[:, 1:L], src[:, 0:L - 1], cw(0, ch), dst[:, 1:L], ALU.mult, ALU.add)
                        stt("dve", dst[:, 0:L - 1], src[:, 1:L], cw(2, ch), dst[:, 0:L - 1], ALU.mult, ALU.add)

                    for bl in range(GB):
                        b = g * GB + bl
                        cb0 = col_of_b(b)
                        for i in range(2):
                            for q, ch in enumerate((i, 2 + i, 4 + i)):
                                S.dma("sp", pt[q], phyT[ch * 128:(ch + 1) * 128, cb0:cb0 + L])
                            conv(cf[0], pt[0], i)
                            cp("pool", x0b, cf[0])
                            S.dma("pool", hx0T[i * 128:(i + 1) * 128, cb0:cb0 + L], x0b)
                            conv(cf[0], pt[1], 2 + i)
                            conv(cf[1], pt[2], 4 + i)
                            tt("dve", uTb, cf[0], cf[1], ALU.mult)
                            S.dma("pool", huT[i * 128:(i + 1) * 128, cb0:cb0 + L], uTb)
                            for t0 in range(0, ntt, 4):
                                nt_ = min(4, ntt - t0)
                                for q in range(nt_):
                                    tr(bankT[:, q * 128:(q + 1) * 128], uTb[:, (t0 + q) * 128:(t0 + q + 1) * 128], identb)
                                evac(uTok[:, t0:t0 + nt_, bl, i * 128:(i + 1) * 128],
                                     bankT[:, 0:nt_ * 128].rearrange("p (q c) -> p q c", q=nt_))
                    FWb = [AB.alloc(2 * ntt * 128).rearrange("p (r t f) -> p r t f", r=2, t=ntt) for _ in range(2)]
                    Hb = [AFa.a3(2, 256) for _ in range(2)]
                    yt = [AFa.alloc(256) for _ in range(4)]
                    for fc in range(nfc):
                        Fw = FWb[fc % 2]
                        H = Hb[fc % 2]
                        S.dma("sp", Fw, dr["FW_" + tag][fc].rearrange("p (r t f) -> p r t f", r=2, t=ntt))
                        for ri, E in ((0, Ec), (1, Es)):
                            pb = bank()
                            for t_i in range(ntt):
                                mm(pb[:, 0:256], Fw[:, ri, t_i, :], E[:, t_i, :], start=(t_i == 0), stop=(t_i == ntt - 1))
                            evac(H[:, ri, :], pb[:, 0:256])
                        for bl in range(GB):
                            pr = bank()
                            pi = bank()
                            for ri, pb in ((0, pr), (1, pi)):
                                for t_i in range(ntt):
                                    mm(pb[:, 0:256], Fw[:, ri, t_i, :], uTok[:, t_i, bl, :],
                                       start=(t_i == 0), stop=(t_i == ntt - 1))
                            tt("dve", yt[0], pr[:, 0:256], H[:, 0, :], ALU.mult)
                            tt("dve", yt[1], pi[:, 0:256], H[:, 1, :], ALU.mult)
                            tt("pool", Y[:, fc * 2, bl, :], yt[0], yt[1], ALU.add)
                            tt("dve", yt[2], pi[:, 0:256], H[:, 0, :], ALU.mult)
                            tt("dve", yt[3], pr[:, 0:256], H[:, 1, :], ALU.mult)
                            tt("pool", Y[:, fc * 2 + 1, bl, :], yt[2], yt[3], ALU.subtract)
                    AFa.top = mark_f1
                    AB.top = mark_b1
                    GIb = AB.alloc(nfc * 2 * tbw).rearrange("p (q t) -> p q t", q=nfc * 2)
                    x0l = [AB.alloc(512) for _ in range(2)]
                    ul = [AB.alloc(512) for _ in range(2)]
                    ybs = [AB.alloc(512) for _ in range(2)]
                    it_ = 0
                    for tb in range(ntb):
                        S.dma("sp", GIb, dr["GI_" + tag][tb].rearrange("p (q t) -> p q t", q=nfc * 2))
                        for bl in range(GB):
                            b = g * GB + bl
                            cb0 = col_of_b(b) + tb * tbw
                            for cc in range(2):
                                x0_ = x0l[it_ % 2][:, 0:tbw]
                                u_ = ul[it_ % 2][:, 0:tbw]
                                yo = ybs[it_ % 2][:, 0:tbw]
                                it_ += 1
                                S.dma("sp", x0_, hx0T[cc * 128:(cc + 1) * 128, cb0:cb0 + tbw])
                                S.dma("sp", u_, huT[cc * 128:(cc + 1) * 128, cb0:cb0 + tbw])
                                pb = bank()
                                for q in range(nfc * 2):
                                    mm(pb[:, 0:tbw], Y[:, q, bl, cc * 128:(cc + 1) * 128], GIb[:, q, :],
                                       start=(q == 0), stop=(q == nfc * 2 - 1))
                                t1 = tf[0][:, 0:tbw]
                                t2 = tf[1][:, 0:tbw]
                                ts("pool", t1, u_, vcol(("skip", l), cc), None, ALU.mult)
                                stt("dve", t2, pb[:, 0:tbw], 2.0 / N, t1, ALU.mult, ALU.add)
                                tt("dve", yo, t2, x0_, ALU.mult)
                                S.dma("pool", ybT[cc * 128:(cc + 1) * 128, cb0:cb0 + tbw], yo)
                AFa.top = mark_f0
                AB.top = mark_b0

            hyena("l", SEQ, lambda b: NB * CTX + b * SEQ)
            if with_ctx:
                hyena("c", CTX, lambda b: b * CTX)

            def attention(kind):
                mark_f = AFa.top
                mark_b = AB.top
                NK = SEQ + CTX
                nsub = 1 if kind == "mla" else 2
                kbuf = [[AB.alloc(NK) for _ in range(nsub)] for _ in range(2)]
                qbuf = [[AB.alloc(NK) for _ in range(nsub)] for _ in range(2)]
                vaug = [AB.alloc(18 * 128).rearrange("p (t c) -> p t c", t=18) for _ in range(2)]
                for v_ in vaug:
                    memset("pool", v_[:, :, 64:128], 1.0)
                for lst in kbuf + qbuf:
                    for t_ in lst:
                        memset("pool", t_, 0.0)
                pT = [AB.alloc(512) for _ in range(4)]
                ost = [AB.alloc(512) for _ in range(2)]
                rrb = [AFa.alloc(512) for _ in range(2)]
                onf = [AFa.alloc(512) for _ in range(2)]
                of_ = AFa.alloc(512)
                sqd = AB.alloc(512)
                scale = MLA_SCALE if kind == "mla" else DF_SCALE
                nh = 8 if kind == "mla" else 4
                cnt = 0
                pi_ = 0
                sidx = [0]
                pend = deque()
                LAG = 2

                def flush(n):
                    while len(pend) > n:
                        pend.popleft()()

                def make_pv(acc, Va, kt, p_, qw, nkt):
                    return lambda: mm(acc[:, 0:qw], Va[:, kt, :], p_[:, 0:qw], start=(kt == 0), stop=(kt == nkt - 1))

                def make_epi(accs, h, gcol, qw, ei):
                    def f():
                        os_ = ost[ei % 2]
                        if kind == "mla":
                            r_ = rrb[ei % 2]
                            recip(r_[0:64, 0:qw], accs[0][64:128, 0:qw])
                            tt("dve", os_[0:64, 0:qw], accs[0][0:64, 0:qw], r_[0:64, 0:qw], ALU.mult)
                            S.dma("pool", yaT[h * 64:(h + 1) * 64, gcol:gcol + qw], os_[0:64, 0:qw])
                        else:
                            for s_ in range(2):
                                r_ = rrb[s_]
                                recip(r_[0:64, 0:qw], accs[s_][64:128, 0:qw])
                                tt("dve", onf[s_][0:64, 0:qw], accs[s_][0:64, 0:qw], r_[0:64, 0:qw], ALU.mult)
                            o = of_[0:64, 0:qw]
                            stt("dve", o, onf[1][0:64, 0:qw], neglamT[0:64, l:l + 1], onf[0][0:64, 0:qw],
                                ALU.mult, ALU.add)
                            tt("pool", sqd[0:64, 0:qw], o, o, ALU.mult)
                            pn = banks[sidx[0] % 4]
                            sidx[0] += 1
                            mm(pn[0:64, 0:qw], onesb[0:64, 0:64], sqd[0:64, 0:qw])
                            rn = rrb[0][0:64, 0:qw]
                            rstd_from(pn[0:64, 0:qw], rn, 64.0)
                            tt("dve", o, o, rn, ALU.mult)
                            ts("pool", os_[0:64, 0:qw], o, SV[("sublng", l)], None, ALU.mult)
                            S.dma("pool", ycT[h * 64:(h + 1) * 64, gcol:gcol + qw], os_[0:64, 0:qw])
                    return f

                for b in range(NB):
                    kc = keycols(b)
                    for h in range(nh):
                        par = cnt % 2
                        cnt += 1
                        Ks = kbuf[par]
                        Qs = qbuf[par]
                        Va = vaug[par]
                        o_ = 0
                        for (cc0, n_) in kc:
                            if kind == "mla":
                                S.dma("sp", Ks[0][0:64, o_:o_ + n_], KnT[h * 64:(h + 1) * 64, cc0:cc0 + n_])
                                S.dma("sp", Qs[0][0:64, o_:o_ + n_], QnT[h * 64:(h + 1) * 64, cc0:cc0 + n_])
                                S.dma("sp", Qs[0][64:96, o_:o_ + n_], QrT[h * 32:(h + 1) * 32, cc0:cc0 + n_])
                                S.dma("sp", Ks[0][64:96, o_:o_ + n_], KrT[:, cc0:cc0 + n_])
                                vsrc = Vm[cc0:cc0 + n_, h * 64:(h + 1) * 64]
                            else:
                                for s_ in range(2):
                                    r0 = (2 * h + s_) * 32
                                    S.dma("sp", Ks[s_][0:32, o_:o_ + n_], DkT[r0:r0 + 32, cc0:cc0 + n_])
                                    S.dma("sp", Qs[s_][0:32, o_:o_ + n_], DqT[r0:r0 + 32, cc0:cc0 + n_])
                                vsrc = Dv[cc0:cc0 + n_, h * 64:(h + 1) * 64]
                            S.dma("sp", Va[:, o_ // 128:(o_ + n_) // 128, 0:64],
                                  vsrc.rearrange("(t p) c -> p t c", p=128))
                            o_ += n_
                        qblocks = [(CTX + i * 512, 512, 18) for i in range(4)]
                        if with_ctx:
                            qblocks.append((0, CTX, 2))
                        for (q0, qw, nkt) in qblocks:
                            accs = []
                            for s_ in range(nsub):
                                acc = banks[4 + (pi_ + s_) % 3]
                                accs.append(acc)
                                for kt in range(nkt):
                                    ps = banks[sidx[0] % 4]
                                    p_ = pT[sidx[0] % 4]
                                    sidx[0] += 1
                                    mm(ps[:, 0:qw], Ks[s_][:, kt * 128:(kt + 1) * 128], Qs[s_][:, q0:q0 + qw])
                                    act(p_[:, 0:qw], ps[:, 0:qw], AF.Exp, scale=scale)
                                    pend.append(make_pv(acc, Va, kt, p_, qw, nkt))
                                    flush(LAG)
                            pi_ += nsub
                            if q0 >= CTX:
                                gcol = NB * CTX + b * SEQ + (q0 - CTX)
                            else:
                                gcol = b * CTX
                            pend.append(make_epi(accs, h, gcol, qw, pi_))
                flush(0)
                AFa.top = mark_f
                AB.top = mark_b

            attention("mla")
            attention("diff")

            mark_f = AFa.top
            mark_b = AB.top
            blks = list(range(NBLK)) if with_ctx else list(range(2, NBLK))
            wg = AB.a3(8, 3072)
            for k in range(8):
                S.dma("pool", wg[:, k, :], win_v[:, k, 1952:5024])
            wbr = AB.a3(8, 1024)
            S.dma("pool", wbr[:, 0:4, :], dr["w_br_a"][l].rearrange("(k p) c -> p k c", p=128))
            S.dma("pool", wbr[:, 4:6, :], dr["w_br_b"][l].rearrange("(k p) c -> p k c", p=128))
            S.dma("pool", wbr[:, 6:8, :], dr["w_br_c"][l].rearrange("(k p) c -> p k c", p=128))
            wout = AB.a3(8, 1024)
            S.dma("pool", wout, dr["w_out"][l].rearrange("(k p) c -> p k c", p=128))
            hb_ = AB.a3(8, 512)
            yb_ = AB.a3(8, 512)
            mT = AB.a3(8, 512)
            h2b = AB.a3(8, 512)
            sqb = h2b
            xb = AFa.a3(8, 512)
            sg = [AFa.alloc(512) for _ in range(3)]
            tm = [AFa.alloc(512) for _ in range(3)]
            rs_ = AFa.alloc(512)
            brk = ((0, 4), (4, 2), (6, 2))
            for blk in blks:
                mc, islat, rt0 = blk_info(blk)
                c0 = blk * 512
                S.dma("sp", hb_, fm(hT)[:, :, c0:c0 + 512])
                S.dma("sp", yb_[:, 0:4, :], fm(yaT)[:, :, c0:c0 + 512])
                S.dma("sp", yb_[:, 4:6, :], fm(ybT)[:, :, c0:c0 + 512])
                S.dma("sp", yb_[:, 6:8, :], fm(ycT)[:, :, c0:c0 + 512])
                S.dma("sp", xb, xT_v[:, :, c0:c0 + 512])
                for j in range(8):
                    for bi, (k0, nk) in enumerate(brk):
                        pg = bank()
                        for k in range(8):
                            mm(pg, wg[:, k, bi * 1024 + j * 128:bi * 1024 + (j + 1) * 128], hb_[:, k, :],
                               start=(k == 0), stop=(k == 7))
                        act(sg[bi], pg, AF.Sigmoid)
                        pp = bank()
                        for k in range(nk):
                            mm(pp, wbr[:, k0 + k, j * 128:(j + 1) * 128], yb_[:, k0 + k, :],
                               start=(k == 0), stop=(k == nk - 1))
                        tt("dve", tm[bi], pp, sg[bi], ALU.mult)
                    tt("pool", tm[0], tm[0], tm[1], ALU.add)
                    tt("pool", mT[:, j, :], tm[0], tm[2], ALU.add)
                for j in range(8):
                    py = bank()
                    for k in range(8):
                        mm(py, wout[:, k, j * 128:(j + 1) * 128], mT[:, k, :], start=(k == 0), stop=(k == 7))
                    stt("dve", xb[:, j, :], py, modT[:, l, 16 + j, mc:mc + 1], xb[:, j, :], ALU.mult, ALU.add)
                    act(sqb[:, j, :], xb[:, j, :], AF.Square)
                S.dma("pool", xT_v[:, :, c0:c0 + 512], xb)
                pss = bank()
                for j in range(8):
                    mm(pss, onesb, sqb[:, j, :], start=(j == 0), stop=(j == 7))
                rstd_from(pss, rs_, 1024.0)
                for j in range(8):
                    t1 = tm[j % 3]
                    tt("dve", t1, xb[:, j, :], rs_, ALU.mult)
                    ts("pool", h2b[:, j, :], t1, A2[:, l, j, mc:mc + 1], modT[:, l, 24 + j, mc:mc + 1], ALU.mult, ALU.add)
                S.dma("pool", fm(h2T)[:, :, c0:c0 + 512], h2b)
            AFa.top = mark_f
            AB.top = mark_b

            for half in range(2):
                mark_f = AFa.top
                mark_b = AB.top
                w1h = AB.a3(8, 2048)
                w1_v = dr["w_fc1"][l].rearrange("(k p) c -> p k c", p=128)
                for k in range(8):
                    S.dma("pool", w1h[:, k, :], w1_v[:, k, half * 2048:(half + 1) * 2048])
                w2h = AB.a3(16, 1024)
                w2_v = dr["w_fc2"][l].rearrange("(k p) c -> p k c", p=128)
                for k4 in range(4):
                    S.dma("pool", w2h[:, k4 * 4:(k4 + 1) * 4, :], w2_v[:, half * 16 + k4 * 4:half * 16 + (k4 + 1) * 4, :])
                h2l = [AB.a3(8, 512) for _ in range(2)]
                aT = AB.a3(16, 512)
                xl = [AFa.a3(8, 512) for _ in range(2)]
                rl = [AFa.alloc(512) for _ in range(3)]
                for bi_, blk in enumerate(blks):
                    mc, islat, rt0 = blk_info(blk)
                    c0 = blk * 512
                    h2_ = h2l[bi_ % 2]
                    xb = xl[bi_ % 2]
                    S.dma("sp", h2_, fm(h2T)[:, :, c0:c0 + 512])
                    S.dma("sp", xb, xT_v[:, :, c0:c0 + 512])
                    for jf in range(16):
                        pf = bank()
                        for k in range(8):
                            mm(pf, w1h[:, k, jf * 128:(jf + 1) * 128], h2_[:, k, :], start=(k == 0), stop=(k == 7))
                        r_ = rl[jf % 3]
                        if jf % 2 == 0:
                            act(r_, pf, AF.Relu)
                        else:
                            ts("dve", r_, pf, 0.0, None, ALU.max)
                        tt("pool", aT[:, jf, :], r_, r_, ALU.mult)
                    for j in range(8):
                        py = bank()
                        for k in range(16):
                            mm(py, w2h[:, k, j * 128:(j + 1) * 128], aT[:, k, :], start=(k == 0), stop=(k == 15))
                        stt("dve", xb[:, j, :], py, modT[:, l, 40 + j, mc:mc + 1], xb[:, j, :], ALU.mult, ALU.add)
                    S.dma("pool", xT_v[:, :, c0:c0 + 512], xb)
                AFa.top = mark_f
                AB.top = mark_b

        mark_f = AFa.top
        mark_b = AB.top
        xl = [AFa.a3(8, 512) for _ in range(2)]
        sqb = AB.a3(8, 512)
        rs_ = AFa.alloc(512)
        ob = [AFa.alloc(1024) for _ in range(2)]
        oi = 0
        for blk in range(2, NBLK):
            b, islat, rt0 = blk_info(blk)
            c0 = blk * 512
            xb = xl[blk % 2]
            S.dma("sp", xb, xT_v[:, :, c0:c0 + 512])
            for j in range(8):
                act(sqb[:, j, :], xb[:, j, :], AF.Square)
            pss = bank()
            for j in range(8):
                mm(pss, onesb, sqb[:, j, :], start=(j == 0), stop=(j == 7))
            rstd_from(pss, rs_, 1024.0)
            for j in range(8):
                tt("dve", xb[:, j, :], xb[:, j, :], rs_, ALU.mult)
                ts("pool", xb[:, j, :], xb[:, j, :], vcol(("fing",), j), None, ALU.mult)
            for t4 in range(4):
                o_ = ob[oi % 2]
                oi += 1
                for half in range(2):
                    pb = bank()
                    for q in range(4):
                        j = half * 4 + q
                        tr(pb[:, q * 128:(q + 1) * 128], xb[:, j, t4 * 128:(t4 + 1) * 128], ident)
                    evac(o_[:, half * 512:(half + 1) * 512], pb)
                r0 = rt0 + t4 * 128
                S.dma("pool", out[b, r0:r0 + 128, :], o_)
        AFa.top = mark_f
        AB.top = mark_b

        S.emit()
    return nc, S


_CACHE = {}


def kernel(**inputs):
    n_cores = 8
    if "nc" not in _CACHE:
        _CACHE["nc"] = build_program()[0]
        _CACHE["consts"] = host_constants()
    nc = _CACHE["nc"]
    consts = _CACHE["consts"]
    x = np.ascontiguousarray(np.asarray(inputs["x"], dtype=np.float32))
    c = np.ascontiguousarray(np.asarray(inputs["c"], dtype=np.float32))
    ctx = np.ascontiguousarray(np.asarray(inputs["ctx"], dtype=np.float32))
    shared = {nm: np.ascontiguousarray(np.asarray(inputs[nm], dtype=np.float32)) for nm in WEIGHT_NAMES}
    shared.update(consts)
    in_maps = []
    for i in range(n_cores):
        m = dict(shared)
        m["x"] = x[i * NB:(i + 1) * NB]
        m["c"] = c[i * NB:(i + 1) * NB]
        m["ctx"] = ctx[i * NB:(i + 1) * NB]
        in_maps.append(m)
    res = run_bass_kernel_spmd(nc, in_maps, core_ids=list(range(n_cores)))
    return np.concatenate([np.asarray(r["out"], dtype=np.float32) for r in res.results], axis=0)
```

```python
import math
import contextlib
from collections import deque
import numpy as np
import ml_dtypes
import concourse.bass as bass
import concourse.mybir as mybir
from concourse.bass_utils import run_bass_kernel_spmd

F32 = mybir.dt.float32
BF16 = mybir.dt.bfloat16
ALU = mybir.AluOpType
AF = mybir.ActivationFunctionType
AX = mybir.AxisListType

ENGS = ("pe", "dve", "act", "pool", "sp")
NDMASEM = 8


def _prod(xs):
    r = 1
    for v in xs:
        r *= int(v)
    return r


_RS_CACHE = {}


def region_of(ap):
    t = ap.tensor
    nm = t.name
    rs = _RS_CACHE.get(nm)
    if rs is None:
        rs = _prod(list(t.shape)[1:])
        _RS_CACHE[nm] = rs
    off = int(ap.offset)
    r0, c0 = divmod(off, rs)
    r1, c1 = r0, c0
    ne = 1
    for step, cnt in ap.ap:
        step = int(step)
        cnt = int(cnt)
        if cnt <= 1 or step == 0:
            continue
        ne *= cnt
        a, b = divmod(step, rs)
        r1 += a * (cnt - 1)
        c1 += b * (cnt - 1)
    if c1 >= rs:
        r1 += c1 // rs
        c0, c1 = 0, rs - 1
    dense = ne >= (r1 + 1 - r0) * (c1 + 1 - c0)
    return (nm, r0, r1 + 1, c0, c1 + 1, dense)


def _ovl(a, b):
    return a[1] < b[2] and b[1] < a[2] and a[3] < b[4] and b[3] < a[4]


def _cov(a, b):
    return a[5] and a[1] <= b[1] and a[2] >= b[2] and a[3] <= b[3] and a[4] >= b[4]


class Op:
    __slots__ = ("eng", "fn", "idx", "cdeps", "ddeps", "dma", "inc", "semval", "ownwait")

    def __init__(self, eng, fn, idx):
        self.eng = eng
        self.fn = fn
        self.idx = idx
        self.cdeps = {}
        self.ddeps = {}
        self.dma = None
        self.inc = False
        self.semval = 0
        self.ownwait = None


class Sched:
    def __init__(self, nc):
        self.nc = nc
        self.ops = {e: [] for e in ENGS}
        self.order = []
        self.bufs = {}
        self.dma_use = {e: [0] * NDMASEM for e in ENGS}
        self.dma_rr = {e: 0 for e in ENGS}

    def _adddep(self, op, prod, kind):
        if prod[0] == "c":
            _, e, i = prod
            if e == op.eng:
                if e == "pe":
                    return
            if op.cdeps.get(e, -1) < i:
                op.cdeps[e] = i
        else:
            _, q, si, val = prod
            k = (q, si)
            if op.ddeps.get(k, 0) < val:
                op.ddeps[k] = val

    def add(self, eng, fn, reads=(), writes=(), dma=False):
        op = Op(eng, fn, len(self.ops[eng]))
        if dma:
            si = self.dma_rr[eng]
            self.dma_rr[eng] = (si + 1) % NDMASEM
            prev = self.dma_use[eng][si]
            self.dma_use[eng][si] = prev + 16
            op.dma = (eng, si, prev + 16)
            if prev:
                op.ownwait = (eng, si, prev)
            me = ("d", eng, si, prev + 16)
        else:
            me = ("c", eng, op.idx)
        rregs = [region_of(a) for a in reads]
        wregs = [region_of(a) for a in writes]
        for rg in rregs:
            b = self.bufs.setdefault(rg[0], {"w": [], "r": []})
            for (wr, prod) in b["w"]:
                if _ovl(wr, rg):
                    self._adddep(op, prod, "raw")
        for rg in wregs:
            b = self.bufs.setdefault(rg[0], {"w": [], "r": []})
            for (wr, prod) in b["w"]:
                if _ovl(wr, rg):
                    self._adddep(op, prod, "waw")
            for (rr, cons) in b["r"]:
                if _ovl(rr, rg):
                    self._adddep(op, cons, "war")
        for rg in rregs:
            b = self.bufs[rg[0]]
            if not dma:
                b["r"] = [(rr, c) for (rr, c) in b["r"]
                          if not (c[0] == "c" and c[1] == eng and _cov(rg, rr))]
            b["r"].append((rg, me))
        for rg in wregs:
            b = self.bufs[rg[0]]
            b["w"] = [(wr, p) for (wr, p) in b["w"] if not _cov(rg, wr)]
            b["r"] = [(rr, c) for (rr, c) in b["r"] if not _cov(rg, rr)]
            b["w"].append((rg, me))
        self.ops[eng].append(op)
        self.order.append(op)
        return op

    def dma(self, q, out, in_, **kw):
        return self.add(q, lambda e: e.dma_start(out=out, in_=in_, **kw),
                        reads=[in_], writes=[out], dma=True)

    def emit(self):
        nc = self.nc
        seen_c = {e: {f: -1 for f in ENGS} for e in ENGS}
        seen_d = {e: {} for e in ENGS}
        waits = {}
        for op in self.order:
            w = []
            e = op.eng
            for f, i in op.cdeps.items():
                if seen_c[e][f] < i:
                    seen_c[e][f] = i
                    w.append(("c", f, i))
                    self.ops[f][i].inc = True
            dd = dict(op.ddeps)
            if op.ownwait is not None:
                q, si, val = op.ownwait
                if dd.get((q, si), 0) < val:
                    dd[(q, si)] = val
            for (q, si), val in dd.items():
                if seen_d[e].get((q, si), 0) < val:
                    seen_d[e][(q, si)] = val
                    w.append(("d", q, si, val))
            waits[id(op)] = w
        for e in ENGS:
            n = 0
            for op in self.ops[e]:
                if op.inc:
                    n += 1
                op.semval = n
        self.stats = {e: (len(self.ops[e]), sum(1 for o in self.ops[e] if o.inc)) for e in ENGS}
        with contextlib.ExitStack() as st:
            csem = {e: st.enter_context(nc.semaphore("cs_" + e)) for e in ENGS}
            dsem = {e: [st.enter_context(nc.semaphore("ds_%s%d" % (e, i))) for i in range(NDMASEM)]
                    for e in ENGS if any(o.dma for o in self.ops[e])}
            block = st.enter_context(nc.Block())

            def run(e):
                def body(eng):
                    for op in self.ops[e]:
                        for w in waits[id(op)]:
                            if w[0] == "c":
                                eng.wait_ge(csem[w[1]], self.ops[w[1]][w[2]].semval)
                            else:
                                eng.wait_ge(dsem[w[1]][w[2]], w[3])
                        ins = op.fn(eng)
                        if op.dma is not None:
                            ins.then_inc(dsem[op.dma[0]][op.dma[1]], 16)
                        elif op.inc:
                            ins.then_inc(csem[e], 1)
                    if e in dsem:
                        for si in range(NDMASEM):
                            v = self.dma_use[e][si]
                            if v and seen_d[e].get((e, si), 0) < v:
                                eng.wait_ge(dsem[e][si], v)
                return body

            block.tensor(run("pe"))
            block.vector(run("dve"))
            block.scalar(run("act"))
            block.gpsimd(run("pool"))
            block.sync(run("sp"))


D = 1024
SEQ = 2048
CTX = 256
DEPTH = 4
NB = 4
TT = NB * (SEQ + CTX)
NBLK = TT // 512
D_IN = 5024
EPS = 1e-6
MLA_SCALE = 96 ** -0.5
DF_SCALE = 32 ** -0.5
NA = 2496
C_KRP, C_DQP, C_DKP = 1952, 1984, 2240
WEIGHT_NAMES = ["norm_mix_g", "norm_ffn_g", "w_mod", "b_mod", "w_in", "mla_q_norm_g", "mla_w_uq",
                "mla_kv_norm_g", "mla_w_ukv", "hy_conv_w", "hy_conv_b", "hy_w1", "hy_b1", "hy_freq",
                "hy_w2", "hy_b2", "hy_w3", "hy_b3", "hy_skip", "df_lq1", "df_lk1", "df_lq2", "df_lk2",
                "df_subln_g", "w_br_a", "w_br_b", "w_br_c", "w_out", "w_fc1", "w_fc2", "final_norm_g",
                "c_ctx"]
WEIGHT_SHAPES = {
    "norm_mix_g": [4, 1024], "norm_ffn_g": [4, 1024], "w_mod": [4, 1024, 6144], "b_mod": [4, 6144],
    "w_in": [4, 1024, 5024], "mla_q_norm_g": [4, 256], "mla_w_uq": [4, 256, 768],
    "mla_kv_norm_g": [4, 128], "mla_w_ukv": [4, 128, 1024], "hy_conv_w": [4, 3, 768],
    "hy_conv_b": [4, 768], "hy_w1": [4, 33, 64], "hy_b1": [4, 64], "hy_freq": [4, 64],
    "hy_w2": [4, 64, 64], "hy_b2": [4, 64], "hy_w3": [4, 64, 512], "hy_b3": [4, 512],
    "hy_skip": [4, 256], "df_lq1": [4, 32], "df_lk1": [4, 32], "df_lq2": [4, 32], "df_lk2": [4, 32],
    "df_subln_g": [4, 64], "w_br_a": [4, 512, 1024], "w_br_b": [4, 256, 1024], "w_br_c": [4, 256, 1024],
    "w_out": [4, 1024, 1024], "w_fc1": [4, 1024, 4096], "w_fc2": [4, 4096, 1024], "final_norm_g": [1024],
    "c_ctx": [1024],
}


def lam_init_of(l):
    return 0.8 - 0.6 * math.exp(-0.3 * l)


def host_constants():
    f32 = np.float32
    cs = {}
    L = SEQ
    t = np.arange(L)
    row = (t // 64).astype(f32)
    col = (t % 64).astype(f32)
    inv = (10000.0 ** (-np.arange(8, dtype=f32) / 8)).astype(f32)
    ang = np.concatenate([row[:, None] * inv, col[:, None] * inv], axis=-1).astype(f32)
    cosT = np.cos(ang).astype(f32).T
    sinT = np.sin(ang).astype(f32).T
    rc = np.zeros((128, L), f32)
    rsn = np.zeros((128, L), f32)
    for p in range(128):
        q = p % 32
        i = q % 16
        rc[p] = cosT[i]
        rsn[p] = -sinT[i] if q < 16 else sinT[i]
    cs["rope_c"] = rc
    cs["rope_s"] = rsn
    for tag, L in (("l", SEQ), ("c", CTX)):
        tt_ = np.linspace(0.0, 1.0, L, dtype=f32)[:, None]
        w = ((2.0 * math.pi / L) * np.arange(L, dtype=f32))[:, None].astype(f32)
        bands = np.linspace(1e-4, 15.0, 16, dtype=f32)[None]
        z = np.concatenate([tt_, np.cos(bands * w), -np.sin(bands * w)], axis=-1).astype(f32)
        cs["zT_" + tag] = np.ascontiguousarray(z.T)
        deltas = np.linspace(math.log(1e-2) / 1.5, math.log(1e-2) / 0.3, 256, dtype=f32)
        win = (np.exp(-tt_ * np.abs(deltas)[None]) + 0.05).astype(f32)
        ntt = L // 128
        cs["win_" + tag] = np.ascontiguousarray(win.reshape(ntt, 128, 256).transpose(1, 0, 2))
        N = 2 * L
        ti = np.arange(L, dtype=np.float64)[:, None]
        fi = np.arange(L, dtype=np.float64)[None, :]
        th = math.pi * (2 * fi + 1) / N
        Cm = np.cos(th * ti)
        Sm = np.sin(th * ti)
        nfc = L // 128
        M = np.stack([Cm, Sm], 0)
        FW = M.reshape(2, ntt, 128, nfc, 128).transpose(3, 2, 0, 1, 4)
        cs["FW_" + tag] = np.ascontiguousarray(FW).astype(ml_dtypes.bfloat16).reshape(nfc, 128, 2 * ntt * 128)
        tbw = 512 if L >= 512 else L
        ntb = L // tbw
        GI = M.reshape(2, ntb, tbw, nfc, 128).transpose(1, 4, 3, 0, 2)
        cs["GI_" + tag] = np.ascontiguousarray(GI).astype(ml_dtypes.bfloat16).reshape(ntb, 128, nfc * 2 * tbw)
    return cs


CONST_SPECS = {
    "rope_c": ([128, 2048], F32), "rope_s": ([128, 2048], F32),
    "zT_l": ([33, 2048], F32), "zT_c": ([33, 256], F32),
    "win_l": ([128, 16, 256], F32), "win_c": ([128, 2, 256], F32),
    "FW_l": ([16, 128, 4096], BF16), "FW_c": ([2, 128, 512], BF16),
    "GI_l": ([4, 128, 16384], BF16), "GI_c": ([1, 128, 1024], BF16),
}


class Arena:
    def __init__(self, handle, n):
        self.h = handle
        self.n = n
        self.top = 0

    def alloc(self, n, shape=None):
        n = (n + 15) // 16 * 16
        assert self.top + n <= self.n, ("arena overflow", self.h.name, self.top, n, self.n)
        v = self.h[:, self.top:self.top + n]
        self.top += n
        return v

    def a3(self, a, b):
        v = self.alloc(a * b)
        return v.rearrange("p (a b) -> p a b", a=a)


def build_program(n_layers=DEPTH, dbg=None):
    nc = bass.Bass("TRN2", target_bir_lowering=False)
    S = Sched(nc)
    dbg = dbg or []
    dr = {}
    dr["x"] = nc.dram_tensor("x", [NB, SEQ, D], F32, kind="ExternalInput")
    dr["c"] = nc.dram_tensor("c", [NB, D], F32, kind="ExternalInput")
    dr["ctx"] = nc.dram_tensor("ctx", [NB, CTX, D], F32, kind="ExternalInput")
    for nm in WEIGHT_NAMES:
        dr[nm] = nc.dram_tensor(nm, WEIGHT_SHAPES[nm], F32, kind="ExternalInput")
    for nm, (shp, dt) in CONST_SPECS.items():
        dr[nm] = nc.dram_tensor(nm, shp, dt, kind="ExternalInput")
    out = nc.dram_tensor("out", [NB, SEQ, D], F32, kind="ExternalOutput")

    def scratch(nm, shape, dt):
        kind = "ExternalOutput" if nm in dbg else "Internal"
        dr[nm] = nc.dram_tensor(nm, shape, dt, kind=kind)
        return dr[nm]

    xT = scratch("xT", [D, TT], F32)
    hT = scratch("hT", [D, TT], BF16)
    h2T = scratch("h2T", [D, TT], BF16)
    QnT = scratch("QnT", [512, TT], BF16)
    QrT = scratch("QrT", [256, TT], BF16)
    KnT = scratch("KnT", [512, TT], BF16)
    KrT = scratch("KrT", [32, TT], BF16)
    Vm = scratch("Vm", [TT, 512], BF16)
    phyT = scratch("phyT", [768, TT], BF16)
    DqT = scratch("DqT", [256, TT], BF16)
    DkT = scratch("DkT", [256, TT], BF16)
    Dv = scratch("Dv", [TT, 256], BF16)
    hx0T = scratch("hx0T", [256, TT], BF16)
    huT = scratch("huT", [256, TT], BF16)
    yaT = scratch("yaT", [512, TT], BF16)
    ybT = scratch("ybT", [256, TT], BF16)
    ycT = scratch("ycT", [256, TT], BF16)

    def fm(t):
        return t.rearrange("(c p) t -> p c t", p=128)

    st = contextlib.ExitStack()
    with st:
        AB_N = 58 * 1024
        AF_N = 18 * 1024
        abh = st.enter_context(nc.sbuf_tensor("arena_bf", [128, AB_N], BF16))
        afh = st.enter_context(nc.sbuf_tensor("arena_f", [128, AF_N], F32))
        AB = Arena(abh, AB_N)
        AFa = Arena(afh, AF_N)
        banks = [st.enter_context(nc.psum_tensor("bank%d" % i, [128, 512], F32)) for i in range(7)]
        bankT = st.enter_context(nc.psum_tensor("bankT", [128, 1024], BF16))
        rr = {"i": 0}

        def bank(lo=0, hi=7):
            n = hi - lo
            rr["i"] = (rr["i"] + 1) % n
            return banks[lo + rr["i"]][:, :]

        def mm(o, lhsT, rhs, start=True, stop=True):
            S.add("pe", lambda e: e.matmul(o, lhsT=lhsT, rhs=rhs, start=start, stop=stop),
                  reads=[lhsT, rhs], writes=[o])

        def tr(o, in_, ident_ap):
            S.add("pe", lambda e: e.transpose(o, in_, ident_ap), reads=[in_, ident_ap], writes=[o])

        def act(o, in_, func, scale=1.0, bias=0.0):
            rd = [in_]
            if not isinstance(scale, (int, float)):
                rd.append(scale)
            if not isinstance(bias, (int, float)):
                rd.append(bias)
            S.add("act", lambda e: e.activation(out=o, in_=in_, func=func, bias=bias, scale=scale),
                  reads=rd, writes=[o])

        def cp(eng, o, in_):
            if eng == "act":
                S.add("act", lambda e: e.copy(out=o, in_=in_), reads=[in_], writes=[o])
            else:
                S.add(eng, lambda e: e.tensor_copy(out=o, in_=in_), reads=[in_], writes=[o])

        def tt(eng, o, a, b, op):
            S.add(eng, lambda e: e.tensor_tensor(out=o, in0=a, in1=b, op=op), reads=[a, b], writes=[o])

        def ts(eng, o, a, s1, s2, op0, op1=None):
            rd = [a]
            if not isinstance(s1, (int, float)):
                rd.append(s1)
            if s2 is not None and not isinstance(s2, (int, float)):
                rd.append(s2)
            if op1 is None:
                S.add(eng, lambda e: e.tensor_scalar(out=o, in0=a, scalar1=s1, scalar2=None, op0=op0),
                      reads=rd, writes=[o])
            else:
                S.add(eng, lambda e: e.tensor_scalar(out=o, in0=a, scalar1=s1, scalar2=s2, op0=op0, op1=op1),
                      reads=rd, writes=[o])

        def stt(eng, o, a, s, b, op0, op1):
            rd = [a, b]
            if not isinstance(s, (int, float)):
                rd.append(s)
            S.add(eng, lambda e: e.scalar_tensor_tensor(out=o, in0=a, scalar=s, in1=b, op0=op0, op1=op1),
                  reads=rd, writes=[o])

        def recip(o, in_):
            S.add("dve", lambda e: e.reciprocal(out=o, in_=in_), reads=[in_], writes=[o])

        def memset(eng, o, v):
            S.add(eng, lambda e: e.memset(o, v), writes=[o])

        evq = {"i": 0}

        def evac(o, in_):
            evq["i"] ^= 1
            cp("dve" if evq["i"] else "act", o, in_)

        def rstd_from(ps_ap, o, n):
            act(o, ps_ap, AF.Sqrt, scale=1.0 / n, bias=EPS)
            recip(o, o)

        ident = AFa.alloc(128)
        memset("pool", ident, 0.0)
        S.add("pool", lambda e: e.affine_select(out=ident, in_=ident, pattern=[[-1, 128]],
                                                compare_op=ALU.not_equal, fill=1.0, base=0,
                                                channel_multiplier=1),
              reads=[ident], writes=[ident])
        identb = AB.alloc(128)
        cp("dve", identb, ident)
        onesb = AB.alloc(128)
        memset("pool", onesb, 1.0)
        onesf = AFa.alloc(128)
        memset("pool", onesf, 1.0)
        mask0 = AFa.alloc(16)[:, 0:1]
        memset("pool", mask0, 1.0)
        memset("pool", mask0[0:1, :], 0.0)
        rope_c = AFa.alloc(2048)
        rope_s = AFa.alloc(2048)
        S.dma("sp", rope_c, dr["rope_c"][:, :])
        S.dma("sp", rope_s, dr["rope_s"][:, :])

        VROW = {}
        rows = []

        def vadd(key, src_ap, n):
            VROW[key] = len(rows)
            for j in range(n):
                rows.append((src_ap, j))

        for l in range(DEPTH):
            vadd(("mixg", l), dr["norm_mix_g"][l].rearrange("(j p) -> j p", p=128), 8)
            vadd(("ffng", l), dr["norm_ffn_g"][l].rearrange("(j p) -> j p", p=128), 8)
            vadd(("bmod", l), dr["b_mod"][l].rearrange("(j p) -> j p", p=128), 48)
            vadd(("qg", l), dr["mla_q_norm_g"][l].rearrange("(j p) -> j p", p=128), 2)
            vadd(("kvg", l), dr["mla_kv_norm_g"][l].rearrange("(j p) -> j p", p=128), 1)
            vadd(("cw", l), dr["hy_conv_w"][l].rearrange("t (j p) -> (t j) p", p=128), 18)
            vadd(("cb", l), dr["hy_conv_b"][l].rearrange("(j p) -> j p", p=128), 6)
            vadd(("skip", l), dr["hy_skip"][l].rearrange("(j p) -> j p", p=128), 2)
        vadd(("fing",), dr["final_norm_g"].rearrange("(j p) -> j p", p=128), 8)
        NV = len(rows)
        vecT = AFa.alloc(NV)
        mark_f = AFa.top
        vrows = AFa.alloc(128)
        g0 = 0
        while g0 < NV:
            n = min(128, NV - g0)
            i = g0
            while i < g0 + n:
                src, j = rows[i]
                k = i
                while k + 1 < g0 + n and rows[k + 1][0] is src and rows[k + 1][1] == rows[k][1] + 1:
                    k += 1
                cnt = k - i + 1
                S.dma("sp", vrows[i - g0:i - g0 + cnt, :], src[j:j + cnt, :])
                i = k + 1
            pb = bank()
            tr(pb[:, 0:n], vrows[0:n, :], ident[0:n, 0:n])
            cp("dve", vecT[:, g0:g0 + n], pb[:, 0:n])
            g0 += n
        AFa.top = mark_f

        def vcol(key, j=0):
            c = VROW[key] + j
            return vecT[:, c:c + 1]

        smallv = AFa.alloc(DEPTH * 8)
        SV = {}
        for l in range(DEPTH):
            for i, nm in enumerate(["hy_b1", "hy_freq", "hy_b2", "df_subln_g"]):
                colv = smallv[0:64, l * 8 + i:l * 8 + i + 1]
                S.dma("sp", colv, dr[nm][l].rearrange("(p o) -> p o", o=1))
                SV[(nm, l)] = colv
            for i, (a, b) in enumerate([("hy_freq", "hy_b1"), ("hy_freq", "hy_b2")]):
                colv = smallv[0:64, l * 8 + 4 + i:l * 8 + 5 + i]
                tt("dve", colv, SV[(a, l)], SV[(b, l)], ALU.mult)
                SV[("fb%d" % (i + 1), l)] = colv
            colv = smallv[0:64, l * 8 + 6:l * 8 + 7]
            ts("dve", colv, SV[("df_subln_g", l)], 1.0 - lam_init_of(l), None, ALU.mult)
            SV[("sublng", l)] = colv

        neglamT = AFa.alloc(16)
        mark_f = AFa.top
        lamrow = AFa.alloc(1024)
        for i, nm in enumerate(["df_lq1", "df_lk1", "df_lq2", "df_lk2"]):
            S.dma("sp", lamrow[0:1, i * 128:(i + 1) * 128], dr[nm].rearrange("(o l) d -> o (l d)", o=1))
        tt("dve", lamrow[0:1, 512:640], lamrow[0:1, 0:128], lamrow[0:1, 128:256], ALU.mult)
        tt("dve", lamrow[0:1, 640:768], lamrow[0:1, 256:384], lamrow[0:1, 384:512], ALU.mult)
        for i in range(2):
            src = lamrow[0:1, 512 + i * 128:640 + i * 128].rearrange("p (l d) -> p l d", l=4)
            dst = lamrow[0:1, 768 + i * 4:772 + i * 4]
            S.add("dve", (lambda s_, d_: (lambda e: e.reduce_sum(out=d_, in_=s_, axis=AX.X)))(src, dst),
                  reads=[src], writes=[dst])
        act(lamrow[0:1, 768:776], lamrow[0:1, 768:776], AF.Exp)
        tt("dve", lamrow[0:1, 776:780], lamrow[0:1, 772:776], lamrow[0:1, 768:772], ALU.subtract)
        for l in range(DEPTH):
            ts("dve", lamrow[0:1, 780 + l:781 + l], lamrow[0:1, 776 + l:777 + l], -lam_init_of(l), None, ALU.add)
        pb = bank()
        mm(pb[0:64, 0:4], onesf[0:1, 0:64], lamrow[0:1, 780:784])
        cp("dve", neglamT[0:64, 0:4], pb[0:64, 0:4])
        AFa.top = mark_f

        modT = AFa.alloc(DEPTH * 48 * 5).rearrange("p (l j c) -> p l j c", l=DEPTH, j=48)
        A1 = AFa.alloc(DEPTH * 8 * 5).rearrange("p (l j c) -> p l j c", l=DEPTH, j=8)
        A2 = AFa.alloc(DEPTH * 8 * 5).rearrange("p (l j c) -> p l j c", l=DEPTH, j=8)
        mark_f = AFa.top
        mark_b = AB.top
        cs_rows = AFa.alloc(1024)
        S.dma("sp", cs_rows[0:4, :], dr["c"][:, :])
        S.dma("sp", cs_rows[4:5, :], dr["c_ctx"].rearrange("(o d) -> o d", o=1))
        act(cs_rows[0:5, :], cs_rows[0:5, :], AF.Silu)
        sT = AFa.alloc(48).rearrange("p (k c) -> p k c", k=8)
        for k in range(8):
            pb = bank()
            tr(pb[:, 0:5], cs_rows[0:5, k * 128:(k + 1) * 128], ident[0:5, 0:5])
            cp("dve", sT[:, k, 0:5], pb[:, 0:5])
        wmbuf = [AFa.alloc(4096).rearrange("p (k c) -> p k c", k=8) for _ in range(2)]
        it = 0
        for l in range(n_layers):
            wm_v = dr["w_mod"][l].rearrange("(k p) c -> p k c", p=128)
            for half in range(12):
                wb = wmbuf[it % 2]
                it += 1
                S.dma("sp", wb, wm_v[:, :, half * 512:(half + 1) * 512])
                for jj in range(4):
                    j = half * 4 + jj
                    pb = bank()
                    for k in range(8):
                        mm(pb[:, 0:5], wb[:, k, jj * 128:(jj + 1) * 128], sT[:, k, 0:5], start=(k == 0), stop=(k == 7))
                    ts("dve", modT[:, l, j, :], pb[:, 0:5], vcol(("bmod", l), j), None, ALU.add)
            for j in range(8):
                ts("dve", A1[:, l, j, :], modT[:, l, 8 + j, :], 1.0, vcol(("mixg", l), j), ALU.add, ALU.mult)
                ts("dve", A2[:, l, j, :], modT[:, l, 32 + j, :], 1.0, vcol(("ffng", l), j), ALU.add, ALU.mult)
        AFa.top = mark_f
        AB.top = mark_b

        mark_f = AFa.top
        xin = [AFa.alloc(1024) for _ in range(2)]
        xst = [AFa.a3(8, 512) for _ in range(1)]
        xT_v = fm(xT)
        ti = 0
        for blk in range(NBLK):
            stg = xst[0]
            for t4 in range(4):
                tok0 = blk * 512 + t4 * 128
                if tok0 < NB * CTX:
                    b, r0 = divmod(tok0, CTX)
                    src = dr["ctx"][b, r0:r0 + 128, :]
                else:
                    b, r0 = divmod(tok0 - NB * CTX, SEQ)
                    src = dr["x"][b, r0:r0 + 128, :]
                xi = xin[ti % 2]
                ti += 1
                S.dma("sp", xi, src)
                for half in range(2):
                    pb = bank()
                    for q in range(4):
                        j = half * 4 + q
                        tr(pb[:, q * 128:(q + 1) * 128], xi[:, j * 128:(j + 1) * 128], ident)
                    evac(stg[:, half * 4:half * 4 + 4, t4 * 128:(t4 + 1) * 128],
                         pb[:, :].rearrange("p (q t) -> p q t", q=4))
            S.dma("pool", xT_v[:, :, blk * 512:(blk + 1) * 512], stg)
        AFa.top = mark_f

        def blk_info(blk):
            if blk < 2:
                return 4, False, 0
            b = (blk - 2) // 4
            return b, True, ((blk - 2) % 4) * 512

        def keycols(b):
            return [(b * CTX, CTX), (NB * CTX + b * SEQ, SEQ)]

        for l in range(n_layers):
            with_ctx = l < DEPTH - 1
            mark_f = AFa.top
            mark_b = AB.top
            winA = AB.a3(8, NA)
            win_v = dr["w_in"][l].rearrange("(k p) c -> p k c", p=128)
            for k in range(8):
                S.dma("pool", winA[:, k, 0:1952], win_v[:, k, 0:1952])
            for k in range(8):
                cp("pool", winA[:, k, C_KRP:C_KRP + 16], winA[:, k, 400:416])
                cp("pool", winA[:, k, C_KRP + 16:C_KRP + 32], winA[:, k, 384:400])
                for (src0, dst0) in ((1184, C_DQP), (1440, C_DKP)):
                    sv = winA[:, k, src0:src0 + 256].rearrange("p (s h i) -> p s h i", s=8, h=2)
                    dv_ = winA[:, k, dst0:dst0 + 256].rearrange("p (s h i) -> p s h i", s=8, h=2)
                    cp("pool", dv_[:, :, 0, :], sv[:, :, 1, :])
                    cp("pool", dv_[:, :, 1, :], sv[:, :, 0, :])
            wuq = AB.a3(2, 768)
            S.dma("pool", wuq, dr["mla_w_uq"][l].rearrange("(k p) c -> p k c", p=128))
            wuq_n = AB.a3(2, 512)
            wuq_r = AB.a3(2, 256)
            wuq_p = AB.a3(2, 256)
            for k in range(2):
                sv = wuq[:, k, :].rearrange("p (h d) -> p h d", h=8)
                cp("pool", wuq_n[:, k, :].rearrange("p (h d) -> p h d", h=8), sv[:, :, 0:64])
                cp("pool", wuq_r[:, k, :].rearrange("p (h d) -> p h d", h=8), sv[:, :, 64:96])
                pv = wuq_p[:, k, :].rearrange("p (h d) -> p h d", h=8)
                cp("pool", pv[:, :, 0:16], sv[:, :, 80:96])
                cp("pool", pv[:, :, 16:32], sv[:, :, 64:80])
            wukv = AB.alloc(1024)
            S.dma("pool", wukv, dr["mla_w_ukv"][l])
            wkn = AB.alloc(512)
            wv = AB.alloc(512)
            sv = wukv.rearrange("p (h d) -> p h d", h=8)
            cp("pool", wkn.rearrange("p (h d) -> p h d", h=8), sv[:, :, 0:64])
            cp("pool", wv.rearrange("p (h d) -> p h d", h=8), sv[:, :, 64:128])

            xb_ = [AFa.a3(8, 512) for _ in range(2)]
            rstd_ = [AFa.alloc(512) for _ in range(2)]
            tmpf = [AFa.alloc(512) for _ in range(4)]
            sqb = AB.a3(8, 512)
            hTb = [AB.a3(8, 512) for _ in range(2)]
            qn = AB.a3(2, 512)
            kvn = AB.alloc(512)
            stg_q = AB.a3(4, 512)
            stg_k = AB.a3(4, 512)
            stg_r = AB.a3(2, 512)
            stg_kr = AB.alloc(512)
            stg_v = AB.a3(4, 512)
            stg_hy = AB.a3(6, 512)
            stg_d = AB.a3(4, 512)
            stg_dv = AB.a3(4, 256)
            tq = {"i": 0}

            def tmp():
                tq["i"] = (tq["i"] + 1) % 4
                return tmpf[tq["i"]]

            def rope_out(o, P, Pp, rows, rt0):
                t1 = tmp()
                t2 = tmp()
                tt("dve", t1[0:rows, :], P, rope_c[0:rows, rt0:rt0 + 512], ALU.mult)
                tt("dve", t2[0:rows, :], Pp, rope_s[0:rows, rt0:rt0 + 512], ALU.mult)
                tt("pool", o, t1[0:rows, :], t2[0:rows, :], ALU.add)

            sqq = AB.a3(3, 512)

            def norm_stage(blk):
                mc, islat, rt0 = blk_info(blk)
                c0 = blk * 512
                xb = xb_[blk % 2]
                hb = hTb[blk % 2]
                rs_ = rstd_[blk % 2]
                S.dma("sp", xb, xT_v[:, :, c0:c0 + 512])
                for j in range(8):
                    act(sqb[:, j, :], xb[:, j, :], AF.Square)
                pss = bank()
                for j in range(8):
                    mm(pss, onesb, sqb[:, j, :], start=(j == 0), stop=(j == 7))
                rstd_from(pss, rs_, 1024.0)
                for j in range(8):
                    t1 = tmp()
                    tt("dve", t1, xb[:, j, :], rs_, ALU.mult)
                    ts("pool", hb[:, j, :], t1, A1[:, l, j, mc:mc + 1], modT[:, l, j, mc:mc + 1], ALU.mult, ALU.add)
                S.dma("pool", fm(hT)[:, :, c0:c0 + 512], hb)


            norm_stage(0)
            for blk in range(NBLK):
                mc, islat, rt0 = blk_info(blk)
                c0 = blk * 512
                hb = hTb[blk % 2]
                def proj(col0, M=128, pb=None):
                    pb = pb or bank()
                    for k in range(8):
                        mm(pb[0:M, :], winA[:, k, col0:col0 + M], hb[:, k, :], start=(k == 0), stop=(k == 7))
                    return pb

                pq = [proj(0), proj(128)]
                for i in range(2):
                    act(sqq[:, i, :], pq[i], AF.Square)
                pss = bank()
                for i in range(2):
                    mm(pss, onesb, sqq[:, i, :], start=(i == 0), stop=(i == 1))
                rq = tmp()
                rstd_from(pss, rq, 256.0)
                for i in range(2):
                    t1 = tmp()
                    tt("dve", t1, pq[i], rq, ALU.mult)
                    ts("pool", qn[:, i, :], t1, vcol(("qg", l), i), None, ALU.mult)
                pkv = proj(256)
                act(sqq[:, 2, :], pkv, AF.Square)
                pss = bank()
                mm(pss, onesb, sqq[:, 2, :])
                rk = tmp()
                rstd_from(pss, rk, 128.0)
                t1 = tmp()
                tt("dve", t1, pkv, rk, ALU.mult)
                ts("pool", kvn, t1, vcol(("kvg", l), 0), None, ALU.mult)
                if blk + 1 < NBLK:
                    norm_stage(blk + 1)
                pb = proj(384, M=32)
                if islat:
                    pb2 = proj(C_KRP, M=32)
                    rope_out(stg_kr[0:32, :], pb[0:32, :], pb2[0:32, :], 32, rt0)
                else:
                    evac(stg_kr[0:32, :], pb[0:32, :])
                S.dma("pool", KrT[:, c0:c0 + 512], stg_kr[0:32, :])
                for ch in range(6):
                    pb = proj(416 + ch * 128)
                    evac(stg_hy[:, ch, :], pb)
                S.dma("pool", fm(phyT)[:, :, c0:c0 + 512], stg_hy)
                for qi, (cbase, pbase, dst) in enumerate(((1184, C_DQP, DqT), (1440, C_DKP, DkT))):
                    for ch in range(2):
                        pb = proj(cbase + ch * 128)
                        if islat:
                            pb2 = proj(pbase + ch * 128)
                            rope_out(stg_d[:, qi * 2 + ch, :], pb, pb2, 128, rt0)
                        else:
                            evac(stg_d[:, qi * 2 + ch, :], pb)
                    S.dma("pool", fm(dst)[:, :, c0:c0 + 512], stg_d[:, qi * 2:qi * 2 + 2, :])
                for t4 in range(4):
                    pb = bank()
                    for k in range(8):
                        mm(pb[:, 0:256], hb[:, k, t4 * 128:(t4 + 1) * 128], winA[:, k, 1696:1952],
                           start=(k == 0), stop=(k == 7))
                    evac(stg_dv[:, t4, :], pb[:, 0:256])
                S.dma("pool", Dv.rearrange("(n p) c -> p n c", p=128)[:, blk * 4:(blk + 1) * 4, :], stg_dv)
                for ch in range(4):
                    pb = bank()
                    for k in range(2):
                        mm(pb, wuq_n[:, k, ch * 128:(ch + 1) * 128], qn[:, k, :], start=(k == 0), stop=(k == 1))
                    evac(stg_q[:, ch, :], pb)
                S.dma("pool", fm(QnT)[:, :, c0:c0 + 512], stg_q)
                for ch in range(2):
                    pb = bank()
                    for k in range(2):
                        mm(pb, wuq_r[:, k, ch * 128:(ch + 1) * 128], qn[:, k, :], start=(k == 0), stop=(k == 1))
                    if islat:
                        pb2 = bank()
                        for k in range(2):
                            mm(pb2, wuq_p[:, k, ch * 128:(ch + 1) * 128], qn[:, k, :], start=(k == 0), stop=(k == 1))
                        rope_out(stg_r[:, ch, :], pb, pb2, 128, rt0)
                    else:
                        evac(stg_r[:, ch, :], pb)
                S.dma("pool", fm(QrT)[:, :, c0:c0 + 512], stg_r)
                for ch in range(4):
                    pb = bank()
                    mm(pb, wkn[:, ch * 128:(ch + 1) * 128], kvn)
                    evac(stg_k[:, ch, :], pb)
                S.dma("pool", fm(KnT)[:, :, c0:c0 + 512], stg_k)
                for t4 in range(4):
                    pb = bank()
                    mm(pb, kvn[:, t4 * 128:(t4 + 1) * 128], wv)
                    evac(stg_v[:, t4, :], pb)
                S.dma("pool", Vm.rearrange("(n p) c -> p n c", p=128)[:, blk * 4:(blk + 1) * 4, :], stg_v)
            AFa.top = mark_f
            AB.top = mark_b

            def hyena(tag, L, col_of_b):
                mark_f0 = AFa.top
                mark_b0 = AB.top
                ntt = L // 128
                nfc = L // 128
                N = 2 * L
                tbw = 512 if L >= 512 else L
                ntb = L // tbw
                GB = 2
                Ec = AB.a3(ntt, 256)
                Es = AB.a3(ntt, 256)
                mark_f = AFa.top
                w1 = AFa.alloc(64)
                w2 = AFa.alloc(64)
                w3e = AFa.alloc(512)
                S.dma("sp", w1[0:33, 0:64], dr["hy_w1"][l])
                S.dma("sp", w2[0:64, 0:64], dr["hy_w2"][l])
                S.dma("sp", w3e[0:64, :], dr["hy_w3"][l])
                S.dma("sp", w3e[64:65, :], dr["hy_b3"][l].rearrange("(o d) -> o d", o=1))
                zTb = AFa.alloc(512)
                a1b = AFa.alloc(512)
                a2T = AFa.alloc(L)
                memset("pool", a2T[64:65, :], 1.0)
                tf = [AFa.alloc(512) for _ in range(2)]
                tfk = AFa.alloc(512)
                for tb in range(ntb):
                    cs_ = slice(tb * tbw, (tb + 1) * tbw)
                    S.dma("sp", zTb[0:33, 0:tbw], dr["zT_" + tag][:, cs_])
                    for (wm_, src, dst, fbk) in ((w1[0:33, 0:64], zTb[0:33, 0:tbw], a1b[0:64, 0:tbw], "fb1"),
                                                 (w2[0:64, 0:64], a1b[0:64, 0:tbw], a2T[0:64, cs_], "fb2")):
                        pb = bank()
                        mm(pb[0:64, 0:tbw], wm_, src)
                        t_ = tf[0][0:64, 0:tbw]
                        ts("dve", t_, pb[0:64, 0:tbw], SV[("hy_freq", l)], SV[(fbk, l)], ALU.mult, ALU.add)
                        ts("dve", t_, t_, 1.0 / (2.0 * math.pi), 16.0, ALU.mult, ALU.add)
                        ki = tf[1][0:64, 0:tbw].bitcast(mybir.dt.int32)
                        cp("dve", ki, t_)
                        kf = tfk[0:64, 0:tbw]
                        cp("dve", kf, ki)
                        tt("dve", t_, t_, kf, ALU.subtract)
                        ts("dve", kf, t_, 0.5, None, ALU.is_gt)
                        tt("dve", t_, t_, kf, ALU.subtract)
                        ts("dve", kf, t_, -0.5, None, ALU.is_lt)
                        tt("dve", t_, t_, kf, ALU.add)
                        act(dst, t_, AF.Sin, scale=2.0 * math.pi)
                win = AFa.a3(ntt, 256)
                S.dma("sp", win, dr["win_" + tag][:, :, :])
                for t_i in range(ntt):
                    pb = bank()
                    mm(pb, a2T[0:65, t_i * 128:(t_i + 1) * 128], w3e[0:65, :])
                    hf = tf[0][:, 0:256]
                    hbk = tf[1][:, 0:256]
                    tt("dve", hf, pb[:, 0:256], win[:, t_i, :], ALU.mult)
                    tt("dve", hbk, pb[:, 256:512], win[:, t_i, :], ALU.mult)
                    tt("pool", Es[:, t_i, :], hbk, hf, ALU.subtract)
                    if t_i == 0:
                        stt("dve", Ec[:, t_i, :], hbk, mask0, hf, ALU.mult, ALU.add)
                    else:
                        tt("pool", Ec[:, t_i, :], hbk, hf, ALU.add)
                AFa.top = mark_f
                cw = lambda tap, ch: vcol(("cw", l), tap * 6 + ch)
                Y = AB.alloc(nfc * 2 * GB * 256).rearrange("p (q b c) -> p q b c", q=nfc * 2, b=GB)
                tf = [AFa.alloc(512) for _ in range(2)]
                mark_f1 = AFa.top
                mark_b1 = AB.top
                for g in range(NB // GB):
                    AFa.top = mark_f1
                    AB.top = mark_b1
                    uTok = AB.alloc(ntt * GB * 256).rearrange("p (t b c) -> p t b c", t=ntt, b=GB)
                    pt = [AB.alloc(L) for _ in range(3)]
                    cf = [AFa.alloc(L) for _ in range(2)]
                    uTb = AB.alloc(L)
                    x0b = AB.alloc(L)

                    def conv(dst, src, ch):
                        ts("dve", dst, src, cw(1, ch), vcol(("cb", l), ch), ALU.mult, ALU.add)
                        stt("dve", dst[:, 1:L], src[:, 0:L - 1], cw(0, ch), dst[:, 1:L], ALU.mult, ALU.add)
                        stt("dve", dst[:, 0:L - 1], src[:, 1:L], cw(2, ch), dst[:, 0:L - 1], ALU.mult, ALU.add)

                    for bl in range(GB):
                        b = g * GB + bl
                        cb0 = col_of_b(b)
                        for i in range(2):
                            for q, ch in enumerate((i, 2 + i, 4 + i)):
                                S.dma("sp", pt[q], phyT[ch * 128:(ch + 1) * 128, cb0:cb0 + L])
                            conv(cf[0], pt[0], i)
                            cp("pool", x0b, cf[0])
                            S.dma("pool", hx0T[i * 128:(i + 1) * 128, cb0:cb0 + L], x0b)
                            conv(cf[0], pt[1], 2 + i)
                            conv(cf[1], pt[2], 4 + i)
                            tt("dve", uTb, cf[0], cf[1], ALU.mult)
                            S.dma("pool", huT[i * 128:(i + 1) * 128, cb0:cb0 + L], uTb)
                            for t0 in range(0, ntt, 4):
                                nt_ = min(4, ntt - t0)
                                for q in range(nt_):
                                    tr(bankT[:, q * 128:(q + 1) * 128], uTb[:, (t0 + q) * 128:(t0 + q + 1) * 128], identb)
                                evac(uTok[:, t0:t0 + nt_, bl, i * 128:(i + 1) * 128],
                                     bankT[:, 0:nt_ * 128].rearrange("p (q c) -> p q c", q=nt_))
                    FWb = [AB.alloc(2 * ntt * 128).rearrange("p (r t f) -> p r t f", r=2, t=ntt) for _ in range(2)]
                    Hb = [AFa.a3(2, 256) for _ in range(2)]
                    yt = [AFa.alloc(256) for _ in range(4)]
                    for fc in range(nfc):
                        Fw = FWb[fc % 2]
                        H = Hb[fc % 2]
                        S.dma("sp", Fw, dr["FW_" + tag][fc].rearrange("p (r t f) -> p r t f", r=2, t=ntt))
                        for ri, E in ((0, Ec), (1, Es)):
                            pb = bank()
                            for t_i in range(ntt):
                                mm(pb[:, 0:256], Fw[:, ri, t_i, :], E[:, t_i, :], start=(t_i == 0), stop=(t_i == ntt - 1))
                            evac(H[:, ri, :], pb[:, 0:256])
                        for bl in range(GB):
                            pr = bank()
                            pi = bank()
                            for ri, pb in ((0, pr), (1, pi)):
                                for t_i in range(ntt):
                                    mm(pb[:, 0:256], Fw[:, ri, t_i, :], uTok[:, t_i, bl, :],
                                       start=(t_i == 0), stop=(t_i == ntt - 1))
                            tt("dve", yt[0], pr[:, 0:256], H[:, 0, :], ALU.mult)
                            tt("dve", yt[1], pi[:, 0:256], H[:, 1, :], ALU.mult)
                            tt("pool", Y[:, fc * 2, bl, :], yt[0], yt[1], ALU.add)
                            tt("dve", yt[2], pi[:, 0:256], H[:, 0, :], ALU.mult)
                            tt("dve", yt[3], pr[:, 0:256], H[:, 1, :], ALU.mult)
                            tt("pool", Y[:, fc * 2 + 1, bl, :], yt[2], yt[3], ALU.subtract)
                    AFa.top = mark_f1
                    AB.top = mark_b1
                    GIb = AB.alloc(nfc * 2 * tbw).rearrange("p (q t) -> p q t", q=nfc * 2)
                    x0l = [AB.alloc(512) for _ in range(2)]
                    ul = [AB.alloc(512) for _ in range(2)]
                    ybs = [AB.alloc(512) for _ in range(2)]
                    it_ = 0
                    for tb in range(ntb):
                        S.dma("sp", GIb, dr["GI_" + tag][tb].rearrange("p (q t) -> p q t", q=nfc * 2))
                        for bl in range(GB):
                            b = g * GB + bl
                            cb0 = col_of_b(b) + tb * tbw
                            for cc in range(2):
                                x0_ = x0l[it_ % 2][:, 0:tbw]
                                u_ = ul[it_ % 2][:, 0:tbw]
                                yo = ybs[it_ % 2][:, 0:tbw]
                                it_ += 1
                                S.dma("sp", x0_, hx0T[cc * 128:(cc + 1) * 128, cb0:cb0 + tbw])
                                S.dma("sp", u_, huT[cc * 128:(cc + 1) * 128, cb0:cb0 + tbw])
                                pb = bank()
                                for q in range(nfc * 2):
                                    mm(pb[:, 0:tbw], Y[:, q, bl, cc * 128:(cc + 1) * 128], GIb[:, q, :],
                                       start=(q == 0), stop=(q == nfc * 2 - 1))
                                t1 = tf[0][:, 0:tbw]
                                t2 = tf[1][:, 0:tbw]
                                ts("pool", t1, u_, vcol(("skip", l), cc), None, ALU.mult)
                                stt("dve", t2, pb[:, 0:tbw], 2.0 / N, t1, ALU.mult, ALU.add)
                                tt("dve", yo, t2, x0_, ALU.mult)
                                S.dma("pool", ybT[cc * 128:(cc + 1) * 128, cb0:cb0 + tbw], yo)
                AFa.top = mark_f0
                AB.top = mark_b0

            hyena("l", SEQ, lambda b: NB * CTX + b * SEQ)
            if with_ctx:
                hyena("c", CTX, lambda b: b * CTX)

            def attention(kind):
                mark_f = AFa.top
                mark_b = AB.top
                NK = SEQ + CTX
                nsub = 1 if kind == "mla" else 2
                kbuf = [[AB.alloc(NK) for _ in range(nsub)] for _ in range(2)]
                qbuf = [[AB.alloc(NK) for _ in range(nsub)] for _ in range(2)]
                vaug = [AB.alloc(18 * 128).rearrange("p (t c) -> p t c", t=18) for _ in range(2)]
                for v_ in vaug:
                    memset("pool", v_[:, :, 64:128], 1.0)
                for lst in kbuf + qbuf:
                    for t_ in lst:
                        memset("pool", t_, 0.0)
                pT = [AB.alloc(512) for _ in range(4)]
                ost = [AB.alloc(512) for _ in range(2)]
                rrb = [AFa.alloc(512) for _ in range(2)]
                onf = [AFa.alloc(512) for _ in range(2)]
                of_ = AFa.alloc(512)
                sqd = AB.alloc(512)
                scale = MLA_SCALE if kind == "mla" else DF_SCALE
                nh = 8 if kind == "mla" else 4
                cnt = 0
                pi_ = 0
                sidx = [0]
                pend = deque()
                LAG = 2

                def flush(n):
                    while len(pend) > n:
                        pend.popleft()()

                def make_pv(acc, Va, kt, p_, qw, nkt):
                    return lambda: mm(acc[:, 0:qw], Va[:, kt, :], p_[:, 0:qw], start=(kt == 0), stop=(kt == nkt - 1))

                def make_epi(accs, h, gcol, qw, ei):
                    def f():
                        os_ = ost[ei % 2]
                        if kind == "mla":
                            r_ = rrb[ei % 2]
                            recip(r_[0:64, 0:qw], accs[0][64:128, 0:qw])
                            tt("dve", os_[0:64, 0:qw], accs[0][0:64, 0:qw], r_[0:64, 0:qw], ALU.mult)
                            S.dma("pool", yaT[h * 64:(h + 1) * 64, gcol:gcol + qw], os_[0:64, 0:qw])
                        else:
                            for s_ in range(2):
                                r_ = rrb[s_]
                                recip(r_[0:64, 0:qw], accs[s_][64:128, 0:qw])
                                tt("dve", onf[s_][0:64, 0:qw], accs[s_][0:64, 0:qw], r_[0:64, 0:qw], ALU.mult)
                            o = of_[0:64, 0:qw]
                            stt("dve", o, onf[1][0:64, 0:qw], neglamT[0:64, l:l + 1], onf[0][0:64, 0:qw],
                                ALU.mult, ALU.add)
                            tt("pool", sqd[0:64, 0:qw], o, o, ALU.mult)
                            pn = banks[sidx[0] % 4]
                            sidx[0] += 1
                            mm(pn[0:64, 0:qw], onesb[0:64, 0:64], sqd[0:64, 0:qw])
                            rn = rrb[0][0:64, 0:qw]
                            rstd_from(pn[0:64, 0:qw], rn, 64.0)
                            tt("dve", o, o, rn, ALU.mult)
                            ts("pool", os_[0:64, 0:qw], o, SV[("sublng", l)], None, ALU.mult)
                            S.dma("pool", ycT[h * 64:(h + 1) * 64, gcol:gcol + qw], os_[0:64, 0:qw])
                    return f

                for b in range(NB):
                    kc = keycols(b)
                    for h in range(nh):
                        par = cnt % 2
                        cnt += 1
                        Ks = kbuf[par]
                        Qs = qbuf[par]
                        Va = vaug[par]
                        o_ = 0
                        for (cc0, n_) in kc:
                            if kind == "mla":
                                S.dma("sp", Ks[0][0:64, o_:o_ + n_], KnT[h * 64:(h + 1) * 64, cc0:cc0 + n_])
                                S.dma("sp", Qs[0][0:64, o_:o_ + n_], QnT[h * 64:(h + 1) * 64, cc0:cc0 + n_])
                                S.dma("sp", Qs[0][64:96, o_:o_ + n_], QrT[h * 32:(h + 1) * 32, cc0:cc0 + n_])
                                S.dma("sp", Ks[0][64:96, o_:o_ + n_], KrT[:, cc0:cc0 + n_])
                                vsrc = Vm[cc0:cc0 + n_, h * 64:(h + 1) * 64]
                            else:
                                for s_ in range(2):
                                    r0 = (2 * h + s_) * 32
                                    S.dma("sp", Ks[s_][0:32, o_:o_ + n_], DkT[r0:r0 + 32, cc0:cc0 + n_])
                                    S.dma("sp", Qs[s_][0:32, o_:o_ + n_], DqT[r0:r0 + 32, cc0:cc0 + n_])
                                vsrc = Dv[cc0:cc0 + n_, h * 64:(h + 1) * 64]
                            S.dma("sp", Va[:, o_ // 128:(o_ + n_) // 128, 0:64],
                                  vsrc.rearrange("(t p) c -> p t c", p=128))
                            o_ += n_
                        qblocks = [(CTX + i * 512, 512, 18) for i in range(4)]
                        if with_ctx:
                            qblocks.append((0, CTX, 2))
                        for (q0, qw, nkt) in qblocks:
                            accs = []
                            for s_ in range(nsub):
                                acc = banks[4 + (pi_ + s_) % 3]
                                accs.append(acc)
                                for kt in range(nkt):
                                    ps = banks[sidx[0] % 4]
                                    p_ = pT[sidx[0] % 4]
                                    sidx[0] += 1
                                    mm(ps[:, 0:qw], Ks[s_][:, kt * 128:(kt + 1) * 128], Qs[s_][:, q0:q0 + qw])
                                    act(p_[:, 0:qw], ps[:, 0:qw], AF.Exp, scale=scale)
                                    pend.append(make_pv(acc, Va, kt, p_, qw, nkt))
                                    flush(LAG)
                            pi_ += nsub
                            if q0 >= CTX:
                                gcol = NB * CTX + b * SEQ + (q0 - CTX)
                            else:
                                gcol = b * CTX
                            pend.append(make_epi(accs, h, gcol, qw, pi_))
                flush(0)
                AFa.top = mark_f
                AB.top = mark_b

            attention("mla")
            attention("diff")

            mark_f = AFa.top
            mark_b = AB.top
            blks = list(range(NBLK)) if with_ctx else list(range(2, NBLK))
            wg = AB.a3(8, 3072)
            for k in range(8):
                S.dma("pool", wg[:, k, :], win_v[:, k, 1952:5024])
            wbr = AB.a3(8, 1024)
            S.dma("pool", wbr[:, 0:4, :], dr["w_br_a"][l].rearrange("(k p) c -> p k c", p=128))
            S.dma("pool", wbr[:, 4:6, :], dr["w_br_b"][l].rearrange("(k p) c -> p k c", p=128))
            S.dma("pool", wbr[:, 6:8, :], dr["w_br_c"][l].rearrange("(k p) c -> p k c", p=128))
            wout = AB.a3(8, 1024)
            S.dma("pool", wout, dr["w_out"][l].rearrange("(k p) c -> p k c", p=128))
            hb_ = AB.a3(8, 512)
            yb_ = AB.a3(8, 512)
            mT = AB.a3(8, 512)
            h2b = AB.a3(8, 512)
            sqb = h2b
            xb = AFa.a3(8, 512)
            sg = [AFa.alloc(512) for _ in range(3)]
            tm = [AFa.alloc(512) for _ in range(3)]
            rs_ = AFa.alloc(512)
            brk = ((0, 4), (4, 2), (6, 2))
            for blk in blks:
                mc, islat, rt0 = blk_info(blk)
                c0 = blk * 512
                S.dma("sp", hb_, fm(hT)[:, :, c0:c0 + 512])
                S.dma("sp", yb_[:, 0:4, :], fm(yaT)[:, :, c0:c0 + 512])
                S.dma("sp", yb_[:, 4:6, :], fm(ybT)[:, :, c0:c0 + 512])
                S.dma("sp", yb_[:, 6:8, :], fm(ycT)[:, :, c0:c0 + 512])
                S.dma("sp", xb, xT_v[:, :, c0:c0 + 512])
                for j in range(8):
                    for bi, (k0, nk) in enumerate(brk):
                        pg = bank()
                        for k in range(8):
                            mm(pg, wg[:, k, bi * 1024 + j * 128:bi * 1024 + (j + 1) * 128], hb_[:, k, :],
                               start=(k == 0), stop=(k == 7))
                        act(sg[bi], pg, AF.Sigmoid)
                        pp = bank()
                        for k in range(nk):
                            mm(pp, wbr[:, k0 + k, j * 128:(j + 1) * 128], yb_[:, k0 + k, :],
                               start=(k == 0), stop=(k == nk - 1))
                        tt("dve", tm[bi], pp, sg[bi], ALU.mult)
                    tt("pool", tm[0], tm[0], tm[1], ALU.add)
                    tt("pool", mT[:, j, :], tm[0], tm[2], ALU.add)
                for j in range(8):
                    py = bank()
                    for k in range(8):
                        mm(py, wout[:, k, j * 128:(j + 1) * 128], mT[:, k, :], start=(k == 0), stop=(k == 7))
                    stt("dve", xb[:, j, :], py, modT[:, l, 16 + j, mc:mc + 1], xb[:, j, :], ALU.mult, ALU.add)
                    act(sqb[:, j, :], xb[:, j, :], AF.Square)
                S.dma("pool", xT_v[:, :, c0:c0 + 512], xb)
                pss = bank()
                for j in range(8):
                    mm(pss, onesb, sqb[:, j, :], start=(j == 0), stop=(j == 7))
                rstd_from(pss, rs_, 1024.0)
                for j in range(8):
                    t1 = tm[j % 3]
                    tt("dve", t1, xb[:, j, :], rs_, ALU.mult)
                    ts("pool", h2b[:, j, :], t1, A2[:, l, j, mc:mc + 1], modT[:, l, 24 + j, mc:mc + 1], ALU.mult, ALU.add)
                S.dma("pool", fm(h2T)[:, :, c0:c0 + 512], h2b)
            AFa.top = mark_f
            AB.top = mark_b

            for half in range(2):
                mark_f = AFa.top
                mark_b = AB.top
                w1h = AB.a3(8, 2048)
                w1_v = dr["w_fc1"][l].rearrange("(k p) c -> p k c", p=128)
                for k in range(8):
                    S.dma("pool", w1h[:, k, :], w1_v[:, k, half * 2048:(half + 1) * 2048])
                w2h = AB.a3(16, 1024)
                w2_v = dr["w_fc2"][l].rearrange("(k p) c -> p k c", p=128)
                for k4 in range(4):
                    S.dma("pool", w2h[:, k4 * 4:(k4 + 1) * 4, :], w2_v[:, half * 16 + k4 * 4:half * 16 + (k4 + 1) * 4, :])
                h2l = [AB.a3(8, 512) for _ in range(2)]
                aT = AB.a3(16, 512)
                xl = [AFa.a3(8, 512) for _ in range(2)]
                rl = [AFa.alloc(512) for _ in range(3)]
                for bi_, blk in enumerate(blks):
                    mc, islat, rt0 = blk_info(blk)
                    c0 = blk * 512
                    h2_ = h2l[bi_ % 2]
                    xb = xl[bi_ % 2]
                    S.dma("sp", h2_, fm(h2T)[:, :, c0:c0 + 512])
                    S.dma("sp", xb, xT_v[:, :, c0:c0 + 512])
                    for jf in range(16):
                        pf = bank()
                        for k in range(8):
                            mm(pf, w1h[:, k, jf * 128:(jf + 1) * 128], h2_[:, k, :], start=(k == 0), stop=(k == 7))
                        r_ = rl[jf % 3]
                        if jf % 2 == 0:
                            act(r_, pf, AF.Relu)
                        else:
                            ts("dve", r_, pf, 0.0, None, ALU.max)
                        tt("pool", aT[:, jf, :], r_, r_, ALU.mult)
                    for j in range(8):
                        py = bank()
                        for k in range(16):
                            mm(py, w2h[:, k, j * 128:(j + 1) * 128], aT[:, k, :], start=(k == 0), stop=(k == 15))
                        stt("dve", xb[:, j, :], py, modT[:, l, 40 + j, mc:mc + 1], xb[:, j, :], ALU.mult, ALU.add)
                    S.dma("pool", xT_v[:, :, c0:c0 + 512], xb)
                AFa.top = mark_f
                AB.top = mark_b

        mark_f = AFa.top
        mark_b = AB.top
        xl = [AFa.a3(8, 512) for _ in range(2)]
        sqb = AB.a3(8, 512)
        rs_ = AFa.alloc(512)
        ob = [AFa.alloc(1024) for _ in range(2)]
        oi = 0
        for blk in range(2, NBLK):
            b, islat, rt0 = blk_info(blk)
            c0 = blk * 512
            xb = xl[blk % 2]
            S.dma("sp", xb, xT_v[:, :, c0:c0 + 512])
            for j in range(8):
                act(sqb[:, j, :], xb[:, j, :], AF.Square)
            pss = bank()
            for j in range(8):
                mm(pss, onesb, sqb[:, j, :], start=(j == 0), stop=(j == 7))
            rstd_from(pss, rs_, 1024.0)
            for j in range(8):
                tt("dve", xb[:, j, :], xb[:, j, :], rs_, ALU.mult)
                ts("pool", xb[:, j, :], xb[:, j, :], vcol(("fing",), j), None, ALU.mult)
            for t4 in range(4):
                o_ = ob[oi % 2]
                oi += 1
                for half in range(2):
                    pb = bank()
                    for q in range(4):
                        j = half * 4 + q
                        tr(pb[:, q * 128:(q + 1) * 128], xb[:, j, t4 * 128:(t4 + 1) * 128], ident)
                    evac(o_[:, half * 512:(half + 1) * 512], pb)
                r0 = rt0 + t4 * 128
                S.dma("pool", out[b, r0:r0 + 128, :], o_)
        AFa.top = mark_f
        AB.top = mark_b

        S.emit()
    return nc, S


_CACHE = {}


def kernel(**inputs):
    n_cores = 8
    if "nc" not in _CACHE:
        _CACHE["nc"] = build_program()[0]
        _CACHE["consts"] = host_constants()
    nc = _CACHE["nc"]
    consts = _CACHE["consts"]
    x = np.ascontiguousarray(np.asarray(inputs["x"], dtype=np.float32))
    c = np.ascontiguousarray(np.asarray(inputs["c"], dtype=np.float32))
    ctx = np.ascontiguousarray(np.asarray(inputs["ctx"], dtype=np.float32))
    shared = {nm: np.ascontiguousarray(np.asarray(inputs[nm], dtype=np.float32)) for nm in WEIGHT_NAMES}
    shared.update(consts)
    in_maps = []
    for i in range(n_cores):
        m = dict(shared)
        m["x"] = x[i * NB:(i + 1) * NB]
        m["c"] = c[i * NB:(i + 1) * NB]
        m["ctx"] = ctx[i * NB:(i + 1) * NB]
        in_maps.append(m)
    res = run_bass_kernel_spmd(nc, in_maps, core_ids=list(range(n_cores)))
    return np.concatenate([np.asarray(r["out"], dtype=np.float32) for r in res.results], axis=0)
```
